# Optimizing a Trainium2 kernel written in Bass

```python
import math
import jax, jax.numpy as jnp
from jax import lax
import numpy as np

D_MODEL = 1024
BATCH = 8
SEQ = 2048
DEPTH = 2

CTX_LEN = 256
GRID_W = 64
INNER = D_MODEL
HEAD_DIM = 64
A_WIDTH = INNER // 2
B_WIDTH = INNER - A_WIDTH
B_HEADS = B_WIDTH // HEAD_DIM
B_KV_HEADS = B_HEADS // 4
KV_WIDTH = B_KV_HEADS * HEAD_DIM
CONV_WIDTH = 31
Q_BLOCK = 128
ROPE_THETA = 10000.0
C_WIDTH = INNER // 2
C_GROUPS = 4
C_GROUP_DIM = C_WIDTH // C_GROUPS
D_WIDTH = INNER - C_WIDTH
D_HEADS = D_WIDTH // HEAD_DIM
NA_ROWS = 8
NA_COLS = 16
EPS = 1e-6
NEG_INF = -1e30
N_AB = (DEPTH + 1) // 2
N_CD = DEPTH // 2

AB_Q_OFF = 2 * A_WIDTH
AB_K_OFF = AB_Q_OFF + B_WIDTH
AB_V_OFF = AB_K_OFF + KV_WIDTH
AB_Z_OFF = AB_V_OFF + KV_WIDTH
AB_IN = AB_Z_OFF + INNER
CD_Q_OFF = C_WIDTH
CD_K_OFF = CD_Q_OFF + D_WIDTH
CD_V_OFF = CD_K_OFF + D_WIDTH
CD_Z_OFF = CD_V_OFF + D_WIDTH
CD_IN = CD_Z_OFF + INNER

kernel_name = "hybrid_prefix_dit_conv_gqa_fnet_natten"


def rms_norm(x, g):
    x32 = x.astype(jnp.float32)
    y = x32 * lax.rsqrt(jnp.mean(x32 * x32, axis=-1, keepdims=True) + EPS)
    return (y * g.astype(jnp.float32)).astype(x.dtype)


def layer_norm(x, g, b):
    x32 = x.astype(jnp.float32)
    mu = jnp.mean(x32, axis=-1, keepdims=True)
    xc = x32 - mu
    y = xc * lax.rsqrt(jnp.mean(xc * xc, axis=-1, keepdims=True) + EPS)
    return (y * g.astype(jnp.float32) + b.astype(jnp.float32)).astype(x.dtype)


def adaln(cond, w, b):
    m = jax.nn.silu(cond) @ w + b
    return jnp.split(m, 3, axis=-1)


def axial_rope(x):
    S = x.shape[1]
    t = jnp.arange(S)
    row = (t // GRID_W).astype(jnp.float32)
    col = (t % GRID_W).astype(jnp.float32)
    nf = HEAD_DIM // 4
    inv = ROPE_THETA ** (-jnp.arange(nf, dtype=jnp.float32) / nf)
    ang = jnp.concatenate([row[:, None] * inv, col[:, None] * inv], axis=-1)
    cos = jnp.cos(ang)[None, :, None, :]
    sin = jnp.sin(ang)[None, :, None, :]
    xp = x.astype(jnp.float32).reshape(x.shape[:-1] + (HEAD_DIM // 2, 2))
    a, b = xp[..., 0], xp[..., 1]
    out = jnp.stack([a * cos - b * sin, a * sin + b * cos], axis=-1)
    return out.reshape(x.shape).astype(x.dtype)


def dense_attn(q, k, v):
    Bq, Lq, Hq, dh = q.shape
    Hk = k.shape[2]
    qg = q.reshape(Bq, Lq, Hk, Hq // Hk, dh)
    s = jnp.einsum('bqkgd,bskd->bkgqs', qg, k).astype(jnp.float32) * (dh ** -0.5)
    p = jax.nn.softmax(s, axis=-1).astype(v.dtype)
    o = jnp.einsum('bkgqs,bskd->bqkgd', p, v)
    return o.reshape(Bq, Lq, Hq * dh)


def blocked_attn(q, k, v):
    Bq, S, Hq, dh = q.shape
    nb = S // Q_BLOCK
    qb = q.reshape(Bq, nb, Q_BLOCK, Hq, dh).transpose(1, 0, 2, 3, 4)
    ob = lax.map(lambda qi: dense_attn(qi, k, v), qb)
    return ob.transpose(1, 0, 2, 3).reshape(Bq, S, Hq * dh)


def dwconv(x, w, b):
    K = w.shape[0]
    y = lax.conv_general_dilated(x, w[:, None, :].astype(x.dtype), (1,), [(K // 2, K // 2)],
                                 dimension_numbers=('NWC', 'WIO', 'NWC'),
                                 feature_group_count=x.shape[-1])
    return y + b


def conformer_conv(a_val, a_gate, conv_w, conv_b, ln_g, ln_b):
    u = a_val * jax.nn.sigmoid(a_gate)
    u = dwconv(u, conv_w, conv_b)
    return jax.nn.silu(layer_norm(u, ln_g, ln_b))


def fourier_mix(f, w_c):
    Bf, L, _ = f.shape
    fg = f.reshape(Bf, L, C_GROUPS, C_GROUP_DIM).astype(jnp.float32)
    fr = jnp.fft.fft2(fg, axes=(1, 3), norm='ortho').real.astype(f.dtype)
    return jnp.einsum('blgc,gce->blge', fr, w_c).reshape(Bf, L, C_WIDTH)


def neighborhood_attn(q, k, v, kc, vc, rpb):
    Bq, S, H, dh = q.shape
    rows = S // GRID_W
    kr = min(NA_ROWS, rows)
    qg = q.reshape(Bq, rows, GRID_W, H, dh)
    kg = k.reshape(Bq, rows, GRID_W, H, dh)
    vg = v.reshape(Bq, rows, GRID_W, H, dh)
    r = jnp.arange(rows)
    rs = jnp.clip(r - kr // 2, 0, rows - kr)
    row_idx = rs[:, None] + jnp.arange(kr)[None, :]
    kb = kg[:, row_idx]
    vb = vg[:, row_idx]
    wq = jnp.arange(GRID_W)
    cs = jnp.clip(wq - NA_COLS // 2, 0, GRID_W - NA_COLS)
    col_ok = (wq[None, :] >= cs[:, None]) & (wq[None, :] < cs[:, None] + NA_COLS)
    dr_idx = row_idx - r[:, None] + (NA_ROWS - 1)
    dc_idx = jnp.clip(wq[None, :] - wq[:, None] + (NA_COLS - 1), 0, 2 * NA_COLS - 2)
    bias = rpb[:, dr_idx[:, None, :, None], dc_idx[None, :, None, :]]
    scale = dh ** -0.5
    s_win = jnp.einsum('brqhd,brkwhd->bhrqkw', qg, kb).astype(jnp.float32) * scale \
        + bias.astype(jnp.float32)[None]
    s_win = jnp.where(col_ok[:, None, :], s_win, NEG_INF).reshape(Bq, H, rows, GRID_W, kr * GRID_W)
    s_ctx = jnp.einsum('brqhd,bchd->bhrqc', qg, kc).astype(jnp.float32) * scale
    p = jax.nn.softmax(jnp.concatenate([s_win, s_ctx], axis=-1), axis=-1).astype(v.dtype)
    p_win = p[..., :kr * GRID_W].reshape(Bq, H, rows, GRID_W, kr, GRID_W)
    p_ctx = p[..., kr * GRID_W:]
    o = jnp.einsum('bhrqkw,brkwhd->brqhd', p_win, vb) + jnp.einsum('bhrqc,bchd->brqhd', p_ctx, vc)
    return o.reshape(Bq, S, H * dh)


def layer_ab(x, xc, c, c_ctx, w_ada, b_ada, norm_g, w_in, conv_w, conv_b, ln_g, ln_b,
             qn_g, kn_g, w_out, need_ctx):
    Bx, S, _ = x.shape
    L = xc.shape[1]
    shift, scale, gate = adaln(c[:, None, :], w_ada, b_ada)
    cshift, cscale, cgate = adaln(c_ctx, w_ada, b_ada)
    h = rms_norm(x, norm_g) * (1 + scale) + shift
    hc = rms_norm(xc, norm_g) * (1 + cscale) + cshift
    u = h @ w_in
    a_val, a_gate = u[..., :A_WIDTH], u[..., A_WIDTH:AB_Q_OFF]
    q = u[..., AB_Q_OFF:AB_K_OFF].reshape(Bx, S, B_HEADS, HEAD_DIM)
    k = u[..., AB_K_OFF:AB_V_OFF].reshape(Bx, S, B_KV_HEADS, HEAD_DIM)
    v = u[..., AB_V_OFF:AB_Z_OFF].reshape(Bx, S, B_KV_HEADS, HEAD_DIM)
    z = u[..., AB_Z_OFF:]
    uc = hc @ (w_in if need_ctx else w_in[:, AB_K_OFF:AB_Z_OFF])
    kv_off = AB_K_OFF if need_ctx else 0
    ck = uc[..., kv_off:kv_off + KV_WIDTH].reshape(Bx, L, B_KV_HEADS, HEAD_DIM)
    cv = uc[..., kv_off + KV_WIDTH:kv_off + 2 * KV_WIDTH].reshape(Bx, L, B_KV_HEADS, HEAD_DIM)
    ck = rms_norm(ck, kn_g)
    q = axial_rope(rms_norm(q, qn_g))
    k = axial_rope(rms_norm(k, kn_g))
    y_b = blocked_attn(q, jnp.concatenate([ck, k], axis=1), jnp.concatenate([cv, v], axis=1))
    y_a = conformer_conv(a_val, a_gate, conv_w, conv_b, ln_g, ln_b)
    y = jnp.concatenate([y_a, y_b], axis=-1) * jax.nn.silu(z)
    x = x + gate * (y @ w_out)
    if need_ctx:
        cq = rms_norm(uc[..., AB_Q_OFF:AB_K_OFF].reshape(Bx, L, B_HEADS, HEAD_DIM), qn_g)
        yc_b = dense_attn(cq, ck, cv)
        yc_a = conformer_conv(uc[..., :A_WIDTH], uc[..., A_WIDTH:AB_Q_OFF], conv_w, conv_b, ln_g, ln_b)
        yc = jnp.concatenate([yc_a, yc_b], axis=-1) * jax.nn.silu(uc[..., AB_Z_OFF:])
        xc = xc + cgate * (yc @ w_out)
    return x, xc


def layer_cd(x, xc, c, c_ctx, w_ada, b_ada, norm_g, w_in, w_fourier, rpb, w_out, need_ctx):
    Bx, S, _ = x.shape
    L = xc.shape[1]
    shift, scale, gate = adaln(c[:, None, :], w_ada, b_ada)
    cshift, cscale, cgate = adaln(c_ctx, w_ada, b_ada)
    h = rms_norm(x, norm_g) * (1 + scale) + shift
    hc = rms_norm(xc, norm_g) * (1 + cscale) + cshift
    u = h @ w_in
    f = u[..., :C_WIDTH]
    q = u[..., CD_Q_OFF:CD_K_OFF].reshape(Bx, S, D_HEADS, HEAD_DIM)
    k = u[..., CD_K_OFF:CD_V_OFF].reshape(Bx, S, D_HEADS, HEAD_DIM)
    v = u[..., CD_V_OFF:CD_Z_OFF].reshape(Bx, S, D_HEADS, HEAD_DIM)
    z = u[..., CD_Z_OFF:]
    uc = hc @ (w_in if need_ctx else w_in[:, CD_K_OFF:CD_Z_OFF])
    kv_off = CD_K_OFF if need_ctx else 0
    ck = uc[..., kv_off:kv_off + D_WIDTH].reshape(Bx, L, D_HEADS, HEAD_DIM)
    cv = uc[..., kv_off + D_WIDTH:kv_off + 2 * D_WIDTH].reshape(Bx, L, D_HEADS, HEAD_DIM)
    y_c = fourier_mix(f, w_fourier)
    y_d = neighborhood_attn(q, k, v, ck, cv, rpb)
    y = jnp.concatenate([y_c, y_d], axis=-1) * jax.nn.silu(z)
    x = x + gate * (y @ w_out)
    if need_ctx:
        cq = uc[..., CD_Q_OFF:CD_K_OFF].reshape(Bx, L, D_HEADS, HEAD_DIM)
        yc_d = dense_attn(cq, ck, cv)
        yc_c = fourier_mix(uc[..., :C_WIDTH], w_fourier)
        yc = jnp.concatenate([yc_c, yc_d], axis=-1) * jax.nn.silu(uc[..., CD_Z_OFF:])
        xc = xc + cgate * (yc @ w_out)
    return x, xc


def setup_inputs(seed: int = 0) -> dict:
    key = jax.random.key(seed)
    ks = iter(jax.random.split(key, 32))
    nrm = lambda shape, s: jax.random.normal(next(ks), shape, jnp.float32) * s
    gain = lambda shape: 1.0 + nrm(shape, 0.05)
    return {
        "x": nrm((BATCH, SEQ, D_MODEL), 1.0),
        "c": nrm((BATCH, D_MODEL), 1.0),
        "ctx": nrm((BATCH, CTX_LEN, D_MODEL), 1.0),
        "c_ctx": nrm((D_MODEL,), 1.0),
        "ab_w_ada": nrm((N_AB, D_MODEL, 3 * D_MODEL), 0.5 * D_MODEL ** -0.5),
        "ab_b_ada": nrm((N_AB, 3 * D_MODEL), 0.01),
        "ab_norm_g": gain((N_AB, D_MODEL)),
        "ab_w_in": nrm((N_AB, D_MODEL, AB_IN), D_MODEL ** -0.5),
        "ab_conv_w": nrm((N_AB, CONV_WIDTH, A_WIDTH), CONV_WIDTH ** -0.5),
        "ab_conv_b": nrm((N_AB, A_WIDTH), 0.01),
        "ab_ln_g": gain((N_AB, A_WIDTH)),
        "ab_ln_b": nrm((N_AB, A_WIDTH), 0.01),
        "ab_q_norm_g": gain((N_AB, HEAD_DIM)),
        "ab_k_norm_g": gain((N_AB, HEAD_DIM)),
        "ab_w_out": nrm((N_AB, INNER, D_MODEL), INNER ** -0.5),
        "cd_w_ada": nrm((N_CD, D_MODEL, 3 * D_MODEL), 0.5 * D_MODEL ** -0.5),
        "cd_b_ada": nrm((N_CD, 3 * D_MODEL), 0.01),
        "cd_norm_g": gain((N_CD, D_MODEL)),
        "cd_w_in": nrm((N_CD, D_MODEL, CD_IN), D_MODEL ** -0.5),
        "cd_w_fourier": nrm((N_CD, C_GROUPS, C_GROUP_DIM, C_GROUP_DIM), C_GROUP_DIM ** -0.5),
        "cd_rpb": nrm((N_CD, D_HEADS, 2 * NA_ROWS - 1, 2 * NA_COLS - 1), 0.1),
        "cd_w_out": nrm((N_CD, INNER, D_MODEL), INNER ** -0.5),
        "final_norm_g": gain((D_MODEL,)),
    }


def reference(x, c, ctx, c_ctx, ab_w_ada, ab_b_ada, ab_norm_g, ab_w_in, ab_conv_w, ab_conv_b,
              ab_ln_g, ab_ln_b, ab_q_norm_g, ab_k_norm_g, ab_w_out, cd_w_ada, cd_b_ada, cd_norm_g,
              cd_w_in, cd_w_fourier, cd_rpb, cd_w_out, final_norm_g):
    xc = ctx
    for layer in range(DEPTH):
        need_ctx = layer < DEPTH - 1
        i = layer // 2
        if layer % 2 == 0:
            x, xc = layer_ab(x, xc, c, c_ctx, ab_w_ada[i], ab_b_ada[i], ab_norm_g[i], ab_w_in[i],
                             ab_conv_w[i], ab_conv_b[i], ab_ln_g[i], ab_ln_b[i],
                             ab_q_norm_g[i], ab_k_norm_g[i], ab_w_out[i], need_ctx)
        else:
            x, xc = layer_cd(x, xc, c, c_ctx, cd_w_ada[i], cd_b_ada[i], cd_norm_g[i], cd_w_in[i],
                             cd_w_fourier[i], cd_rpb[i], cd_w_out[i], need_ctx)
    return rms_norm(x, final_norm_g)
```

```python
import numpy as np
from contextlib import ExitStack
import ml_dtypes
import concourse.bass as bass
import concourse.mybir as mybir
from concourse.bass_utils import run_bass_kernel_spmd

F32 = mybir.dt.float32
BF16 = mybir.dt.bfloat16
AF = mybir.ActivationFunctionType
ALU = mybir.AluOpType
AX = mybir.AxisListType

SAME_ENG_SYNC = True

D = 1024
S = 2048
LC = 256
T = S + LC
NT = 18
NTL = 16
KC = 8
EPS = 1e-6
AB_IN = 2816
CD_IN = 3072
NEG = -30000.0


class Prog:
    ENG = ['pe', 'act', 'dve', 'pool', 'sp']

    def __init__(self, nc):
        self.nc = nc
        self.ops = {e: [] for e in self.ENG}
        self.cnt = {}
        self.seen = {e: {} for e in self.ENG}
        self.res = {}
        self.stack = ExitStack()
        self.nsb = 0

    def sb(self, shape, dtype=F32, name=None):
        self.nsb += 1
        name = name or f"sb{self.nsb}"
        return self.stack.enter_context(self.nc.sbuf_tensor(name, list(shape), dtype))

    def ps(self, shape, dtype=F32, name=None):
        self.nsb += 1
        name = name or f"ps{self.nsb}"
        return self.stack.enter_context(self.nc.psum_tensor(name, list(shape), dtype))

    def _collect(self, eng, reads, writes):
        need = {}

        def add(t):
            if t is None:
                return
            k, v = t
            if need.get(k, 0) < v:
                need[k] = v
        for r in reads:
            st = self.res.get(r)
            if st is not None:
                add(st[0])
        for w in writes:
            st = self.res.get(w)
            if st is not None:
                add(st[0])
                for k, v in st[1].items():
                    add((k, v))
        waits = []
        seen = self.seen[eng]
        for k, v in need.items():
            if k == ('e', eng):
                if eng == 'pe' or not SAME_ENG_SYNC:
                    continue
            if k[0] == 'd':
                v = self.cnt[k]
            if seen.get(k, 0) >= v:
                continue
            seen[k] = v
            waits.append((k, v))
        return waits

    def _update(self, reads, writes, tick):
        k, v = tick
        for r in reads:
            st = self.res.setdefault(r, [None, {}])
            st[1][k] = v
        for w in writes:
            self.res[w] = [tick, {}]

    def capture(self, f):
        self._cap = []
        f()
        ops, self._cap = self._cap, None
        return ops

    def emit_rr(self, lists):
        idx = [0] * len(lists)
        left = sum(len(l) for l in lists)
        while left:
            for i, l in enumerate(lists):
                if idx[i] < len(l):
                    kind, args = l[idx[i]]
                    idx[i] += 1
                    left -= 1
                    if kind == 'op':
                        self.op(*args)
                    else:
                        self.dma(*args)

    def op(self, eng, fn, reads=(), writes=()):
        if getattr(self, '_cap', None) is not None:
            self._cap.append(('op', (eng, fn, reads, writes)))
            return
        writes = list(writes) + [r for r in reads if r.startswith('PB')]
        reads = [r for r in reads if not r.startswith('PB')]
        waits = self._collect(eng, reads, writes)
        k = ('e', eng)
        self.cnt[k] = self.cnt.get(k, 0) + 1
        self.ops[eng].append((waits, fn, (k, 1)))
        self._update(reads, writes, (k, self.cnt[k]))

    def dma(self, q, out, in_, reads=(), writes=(), sem=None):
        assert sem is not None
        if getattr(self, '_cap', None) is not None:
            self._cap.append(('dma', (q, out, in_, reads, writes, sem)))
            return
        waits = self._collect(q, reads, writes)
        k = ('d', sem)
        self.cnt[k] = self.cnt.get(k, 0) + 16
        self.ops[q].append((waits, lambda e: e.dma_start(out=out, in_=in_), (k, 16)))
        self._update(reads, writes, (k, self.cnt[k]))

    def barrier(self):
        for e in self.ENG:
            waits = []
            for k, v in self.cnt.items():
                if k == ('e', e):
                    continue
                if self.seen[e].get(k, 0) >= v:
                    continue
                self.seen[e][k] = v
                waits.append((k, v))
            if waits:
                self.ops[e].append((waits, None, None))

    def mm(self, out, lhsT, rhs, start=True, stop=True, r=(), w=()):
        self.op('pe', lambda e: e.matmul(out, lhsT, rhs, start=start, stop=stop), r, w)

    def tr(self, out, in_, ident, r=(), w=()):
        self.op('pe', lambda e: e.transpose(out, in_, ident), r, w)

    def act(self, out, in_, func, r=(), w=(), **kw):
        self.op('act', lambda e: e.activation(out, in_, func, **kw), r, w)

    def tt(self, eng, out, a, b, op, r=(), w=()):
        self.op(eng, lambda e: e.tensor_tensor(out, a, b, op=op), r, w)

    def ts(self, eng, out, a, s1, s2, op0, op1, r=(), w=()):
        if op1 is None:
            self.op(eng, lambda e: e.tensor_scalar(out, a, s1, None, op0=op0), r, w)
        else:
            self.op(eng, lambda e: e.tensor_scalar(out, a, s1, s2, op0=op0, op1=op1), r, w)

    def stt(self, out, a, s, b, op0, op1, r=(), w=()):
        self.op('dve', lambda e: e.scalar_tensor_tensor(out, a, s, b, op0=op0, op1=op1), r, w)

    def cp(self, eng, out, in_, r=(), w=()):
        if eng == 'act':
            self.op('act', lambda e: e.activation(out, in_, AF.Copy), r, w)
        else:
            self.op(eng, lambda e: e.tensor_copy(out, in_), r, w)

    def red(self, out, in_, r=(), w=()):
        self.op('dve', lambda e: e.tensor_reduce(out, in_, axis=AX.X, op=ALU.add), r, w)

    def recip(self, out, in_, r=(), w=()):
        self.op('dve', lambda e: e.reciprocal(out, in_), r, w)

    def ms(self, eng, ap, val, w=()):
        self.op(eng, lambda e: e.memset(ap, val), (), w)

    def finalize(self):
        nc = self.nc
        sems = {}
        for i, k in enumerate(self.cnt):
            sems[k] = self.stack.enter_context(nc.semaphore(f"s{i}"))
        final_waits = [(k, v) for k, v in self.cnt.items()]
        self.ops['sp'].append((final_waits, None, None))
        engmap = {'pe': 'tensor', 'act': 'scalar', 'dve': 'vector', 'pool': 'gpsimd', 'sp': 'sync'}
        with nc.Block() as block:
            for e in self.ENG:
                ops = self.ops[e]

                def body(engine, ops=ops):
                    for waits, fn, inc in ops:
                        for k, v in waits:
                            engine.wait_ge(sems[k], v)
                        if fn is not None:
                            ins = fn(engine)
                            ins.then_inc(sems[inc[0]], inc[1])
                getattr(block, engmap[e])(body)
        self.stack.close()


class Arena:
    def __init__(self, t, nbytes):
        self.t = t
        self.n = nbytes
        self.off = 0

    def reset(self):
        self.off = 0

    def alloc(self, free_shape, dtype, parts=128):
        n = int(np.prod(free_shape))
        sz = 4 if dtype == F32 else 2
        self.off = (self.off + 63) // 64 * 64
        nb = n * sz
        assert self.off + nb <= self.n, (self.off, nb, self.n)
        v = self.t[:, self.off // 2:(self.off + nb) // 2]
        self.off += nb
        if dtype == F32:
            v = v.bitcast(F32)
        if len(free_shape) == 2:
            v = v.rearrange("p (a b) -> p a b", a=free_shape[0])
        elif len(free_shape) == 3:
            v = v.rearrange("p (a b c) -> p a b c", a=free_shape[0], b=free_shape[1])
        if parts != 128:
            v = v[0:parts]
        return v


V_C, V_CC = 0, 8
V_L = [dict(bada=16, g=40), dict(bada=60, g=84)]
V_CB, V_LNG, V_LNB = 48, 52, 56
NV = 92


def na_patterns():
    rows = 32
    pats = {}
    plist = []
    tiles = []
    for i in range(16):
        js = []
        for j in range(16):
            key = []
            anyv = False
            for qr in range(2):
                r = 2 * i + qr
                rs = min(max(r - 4, 0), rows - 8)
                for kr in range(2):
                    R = 2 * j + kr
                    ok = rs <= R < rs + 8
                    anyv |= ok
                    key.append((ok, R - r + 7 if ok else -1))
            if not anyv:
                continue
            key = tuple(key)
            if key not in pats:
                pats[key] = len(plist)
                plist.append(key)
            js.append((j, pats[key]))
        tiles.append(js)
    order = [8, 7, 4, 0, 1, 2, 3, 5, 4, 0, 1, 6]
    tiles2 = []
    for js in tiles:
        seq = [pt for _, pt in js]
        s0 = None
        for st in range(len(order) - len(seq) + 1):
            if order[st:st + len(seq)] == seq:
                s0 = st
                break
        assert s0 is not None, seq
        tiles2.append([(j, s0 + n) for n, (j, _) in enumerate(js)])
    return [plist[o] for o in order], tiles2


def build(debug=False, nphase=10):
    nc = bass.Bass("TRN2", target_bir_lowering=False)

    def din(name, shape, dt=F32):
        return nc.dram_tensor(name, list(shape), dt, kind="ExternalInput").ap()

    plist, na_tiles = na_patterns()
    NP = len(plist)

    x = din("x", [S, D])
    ctx = din("ctx", [LC, D])
    vecs = din("vecs", [NV, 128])
    ident_d = din("ident", [128, 128])
    rope_d = din("rope", [128, NT * 128])
    gqk_d = din("gqk", [2, 64])
    conv_w_d = din("conv_w", [31, 512])
    w_ada_d = [din("w_ada0", [D, 3072]), din("w_ada1", [D, 3072])]
    b_gate_d = [din("b_gate0", [D]), din("b_gate1", [D])]
    w_in_d = [din("w_in0", [D, AB_IN]), din("w_in1", [D, CD_IN])]
    w_out_d = [din("w_out0", [D, D]), din("w_out1", [D, D])]
    w_four_d = din("w_four", [4, 128, 128])
    ccsc_d = din("ccsc", [128, 256])
    dft_d = din("dft", [2, 4, 128, 16 * 512], BF16)
    nyq_d = din("nyq", [128, 2])
    bm_d = din("bm", [128, 4, 2 * NP * 128])
    fg_d = din("final_g", [D])
    xs1 = nc.dram_tensor("xs1", [T, D], F32, kind="ExternalOutput" if debug else "Internal").ap()
    out_d = nc.dram_tensor("out", [S, D], F32, kind="ExternalOutput").ap()

    p = Prog(nc)
    HT = p.sb([128, KC, T], BF16, "HT")
    YT = p.sb([128, KC, T], BF16, "YT")
    W0 = p.sb([128, KC, 1024], BF16, "W0")
    W1 = p.sb([128, KC, 512], BF16, "W1")
    GATE = p.sb([128, 2, D], F32, "GATE")
    XT = [p.sb([128, D], F32, "XT0"), p.sb([128, D], F32, "XT1")]
    VT = p.sb([128, NV], F32, "VT")
    MOD = p.sb([128, 4, KC], F32, "MOD")
    ident32 = p.sb([128, 128], F32, "ident32")
    identb = p.sb([128, 128], BF16, "identb")
    ONESM = p.sb([128, 128], BF16, "ONESM")
    NHB = p.sb([128, 512], BF16, "NHB")
    CW = p.sb([128, 4, 31], F32, "CW")
    GQK = p.sb([128, 2, 64], F32, "GQK")
    SS = p.sb([128, NT], F32, "SS")
    MSQ = p.sb([128, NT], F32, "MSQ")
    RS = p.sb([128, NT], F32, "RS")
    SC32 = p.sb([128, 16], F32, "SC32")
    SCB = p.sb([128, KC, 2], BF16, "SCB")
    CCSC = p.sb([128, 256], BF16, "CCSC")
    WC = p.sb([128, 4, 128], BF16, "WC")
    AR_BYTES = 88 * 1024
    ARt = p.sb([128, AR_BYTES // 2], BF16, "ARENA")
    AR = Arena(ARt, AR_BYTES)
    PSt = [p.ps([128, 1024], F32, f"PS{i}") for i in range(4)]

    def bank(i):
        return PSt[i // 2][:, (i % 2) * 512:(i % 2 + 1) * 512]

    def bankb(i):
        return bank(i).bitcast(BF16)

    PB = [f"PB{i}" for i in range(8)]

    p.dma('sp', ident32[:], ident_d, writes=['ident32'], sem='ident32')
    p.cp('dve', identb[:], ident32[:], r=['ident32'], w=['identb'])
    p.ms('dve', ONESM[:], 1.0 / 512.0, w=['ONESM'])
    p.ms('pool', NHB[:], -0.5, w=['NHB'])
    p.dma('sp', GQK[:, 0, :], gqk_d[0].partition_broadcast(128), writes=['GQK0'], sem='GQK0')
    p.dma('sp', GQK[:, 1, :], gqk_d[1].partition_broadcast(128), writes=['GQK1'], sem='GQK1')
    p.dma('pool', CCSC[:], ccsc_d, writes=['CCSC'], sem='CCSC')
    p.dma('pool', WC[:], w_four_d.rearrange("g m e -> m g e"), writes=['WC'], sem='WC')
    AR.reset()
    VL = AR.alloc([128], F32)
    CWL = AR.alloc([512], F32)
    p.dma('sp', VL[0:NV, :], vecs, writes=['VL'], sem='VL')
    p.dma('sp', CWL[0:31, :], conv_w_d, writes=['CWL'], sem='CWL')
    p.tr(bank(0)[:, 0:NV], VL[0:NV, :], ident32[0:NV, 0:NV], r=['VL', 'ident32'], w=[PB[0]])
    p.cp('dve', VT[:], bank(0)[:, 0:NV], r=[PB[0]], w=['VT'])
    for cc in range(4):
        p.tr(bank(1)[:, cc * 32:cc * 32 + 31], CWL[0:31, cc * 128:(cc + 1) * 128], ident32[0:31, 0:31],
             r=['CWL', 'ident32'], w=[PB[1]])
    p.cp('dve', CW[:], bank(1)[:, 0:128].rearrange("p (c k) -> p c k", c=4)[:, :, 0:31], r=[PB[1]], w=['CW'])
    p.act(SC32[:], VT[:, 0:16], AF.Silu, r=['VT'], w=['SC32'])
    p.cp('dve', SCB[:, :, 0], SC32[:, 0:8], r=['SC32'], w=['SCB'])
    p.cp('dve', SCB[:, :, 1], SC32[:, 8:16], r=['SC32'], w=['SCB'])

    def wload(Wt, wkey, src, c0, n, d0=0):
        p.dma('pool', Wt[:, :, d0:d0 + n], src.rearrange("(k q) n -> q k n", q=128)[:, :, c0:c0 + n],
              writes=[wkey], sem=wkey)

    def adaln(l):
        p.barrier()
        AR.reset()
        BG = AR.alloc([D], F32)
        SCR = AR.alloc([16, 128], BF16)
        p.cp('dve', SCR, SC32[:, 0:16].unsqueeze(2).broadcast_to([128, 16, 128]), r=['SC32'], w=['SCR'])
        p.dma('sp', BG, b_gate_d[l].partition_broadcast(128), writes=['BG'], sem='BG')
        Ws = [(W0, 'W0'), (W1, 'W1')]
        vb = V_L[l]['bada']
        vg = V_L[l]['g']
        for blk in range(4):
            Wt, wk = Ws[blk % 2]
            wload(Wt, wk, w_ada_d[l], blk * 512, 512)
            for jj in range(4):
                j = blk * 4 + jj
                for k in range(KC):
                    p.mm(bank(7)[:, j * 2:j * 2 + 2], Wt[:, k, jj * 128:(jj + 1) * 128], SCB[:, k, :],
                         start=(k == 0), stop=(k == KC - 1), r=[wk, 'SCB'], w=[PB[7]])
        pm = bank(7)[:, 0:32].rearrange("p (j w) -> p j w", w=2)
        for wsel in range(2):
            p.tt('dve', MOD[:, 2 * wsel, :], pm[:, 0:8, wsel], VT[:, vb:vb + 8], ALU.add, r=[PB[7], 'VT'], w=['MOD'])
            p.tt('dve', MOD[:, 2 * wsel + 1, :], pm[:, 8:16, wsel], VT[:, vb + 8:vb + 16], ALU.add, r=[PB[7], 'VT'], w=['MOD'])
            p.stt(MOD[:, 2 * wsel + 1, :], MOD[:, 2 * wsel + 1, :], 1.0, VT[:, vg:vg + 8], ALU.add, ALU.mult,
                  r=['MOD', 'VT'], w=['MOD'])

        def gate_part():
            for blk in range(4, 6):
                Wt, wk = Ws[blk % 2]
                wload(Wt, wk, w_ada_d[l], blk * 512, 512)
                half = blk - 4
                for wsel in range(2):
                    bi = 5 + wsel
                    for k in range(KC):
                        p.mm(bank(bi), SCR[:, wsel * 8 + k, :], Wt[:, k, 0:512], start=(k == 0), stop=(k == KC - 1),
                             r=[wk, 'SCR'], w=[PB[bi]])
                    p.tt('dve', GATE[:, wsel, half * 512:(half + 1) * 512], bank(bi), BG[:, half * 512:(half + 1) * 512],
                         ALU.add, r=[PB[bi], 'BG'], w=['GATE'])

        p.emit_rr([p.capture(gate_part), p.capture(lambda: phase_norm(l, reset=False))])

    def xsrc(l, t):
        if l == 0:
            return x[t * 128:(t + 1) * 128, :] if t < NTL else ctx[(t - NTL) * 128:(t - NTL + 1) * 128, :]
        return xs1[t * 128:(t + 1) * 128, :]

    def phase_norm(l, reset=True):
        if reset:
            AR.reset()
        XH = [AR.alloc([D], F32), AR.alloc([D], F32)]
        JUNK = AR.alloc([D], BF16)
        def stA(t):
            xt = XT[t % 2]
            xk = f"XT{t % 2}"
            xh = XH[t % 2]
            hk = f"XH{t % 2}"
            p.dma('sp', xt[:], xsrc(l, t), reads=['xs1'] if l else [], writes=[xk], sem=xk)
            p.act(JUNK, xt[:], AF.Square, r=[xk], w=['JUNK', f'SS{t}'], accum_out=SS[:, t:t + 1])
            p.ts('dve', MSQ[:, t:t + 1], SS[:, t:t + 1], 1.0 / D, EPS, ALU.mult, ALU.add, r=[f'SS{t}'], w=[f'MSQ{t}'])
            p.tt('pool', RS[:, t:t + 1], MSQ[:, t:t + 1], NHB[:, 0:1], ALU.pow, r=[f'MSQ{t}', 'NHB'], w=[f'RS{t}'])
            p.ts('dve', xh, xt[:], RS[:, t:t + 1], None, ALU.mult, None, r=[xk, f'RS{t}'], w=[hk])

        def stB(t):
            xh = XH[t % 2]
            hk = f"XH{t % 2}"
            wsel = 0 if t < NTL else 1
            b0 = (t % 2) * 2
            for c in range(KC):
                bi = b0 + c // 4
                p.tr(bank(bi)[:, (c % 4) * 128:(c % 4 + 1) * 128], xh[:, c * 128:(c + 1) * 128], ident32[:],
                     r=[hk, 'ident32'], w=[PB[bi]])
            for c in range(KC):
                bi = b0 + c // 4
                src = bank(bi)[:, (c % 4) * 128:(c % 4 + 1) * 128]
                dst = HT[:, c, t * 128:(t + 1) * 128]
                if c < 4:
                    p.act(dst, src, AF.Identity, r=[PB[bi], 'MOD'], w=[f'HT{t}a'],
                          scale=MOD[:, 2 * wsel + 1, c:c + 1], bias=MOD[:, 2 * wsel, c:c + 1])
                else:
                    p.ts('dve', dst, src, MOD[:, 2 * wsel + 1, c:c + 1], MOD[:, 2 * wsel, c:c + 1], ALU.mult, ALU.add,
                         r=[PB[bi], 'MOD'], w=[f'HT{t}b'])

        stA(0)
        for t in range(NT):
            if t + 1 < NT:
                stA(t + 1)
            stB(t)

    BLOCKS = [(0, 512), (512, 512), (1024, 512), (1536, 512), (2048, 256)]

    def ht_keys(b0, bn):
        return [f'HT{t}{s_}' for t in range(b0 // 128, (b0 + bn) // 128) for s_ in 'ab']

    def phase_conv():
        p.barrier()
        AR.reset()
        NPE = 21
        GLUs = [AR.alloc([2349], F32), AR.alloc([2349], F32)]
        GLBs = [AR.alloc([2349], BF16), AR.alloc([2349], BF16)]
        ACCs = [AR.alloc([2319], F32), AR.alloc([2319], F32)]
        DGs = [AR.alloc([NPE, 128], BF16), AR.alloc([NPE, 128], BF16)]
        SIGs = [AR.alloc([512], F32), AR.alloc([512], F32)]
        AVs = [AR.alloc([512], F32), AR.alloc([512], F32)]
        wload(W0, 'W0', w_in_d[0], 0, 1024)
        wload(W1, 'W1', w_in_d[0], 1792, 512)
        for i in range(2):
            p.ms('dve', GLUs[i], 0.0, w=[f'GLU{i}'])
            p.ms('pool', GLBs[i], 0.0, w=[f'GLB{i}'])
        OB5 = [(0, 512), (512, 512), (1024, 512), (1536, 512), (2048, 271)]
        nblk = 0

        def castout(c_):
            p.cp('act', YT[:, c_, 0:S], ACCs[c_ % 2][:, 0:S], r=[f'ACC{c_ % 2}'], w=[f'YT{c_}'])
            p.cp('act', YT[:, c_, S:T], ACCs[c_ % 2][:, 2063:2319], r=[f'ACC{c_ % 2}'], w=[f'YT{c_}'])

        for cc in range(4):
            GLU, GLB, DG, ACC = GLUs[cc % 2], GLBs[cc % 2], DGs[cc % 2], ACCs[cc % 2]
            gk, bk, dk, ak = f'GLU{cc % 2}', f'GLB{cc % 2}', f'DG{cc % 2}', f'ACC{cc % 2}'
            p.tt('pool', DG, identb[:].unsqueeze(1).broadcast_to([128, NPE, 128]),
                 CW[:, cc, 0:NPE].unsqueeze(2).broadcast_to([128, NPE, 128]), ALU.mult, r=['identb', 'CW'], w=[dk])
            for bi_, (b0, bn) in enumerate(BLOCKS):
                pv, pg = 2 * (bi_ % 2), 2 * (bi_ % 2) + 1
                SIG, AV = SIGs[nblk % 2], AVs[nblk % 2]
                sgk, avk = f'SIG{nblk % 2}', f'AV{nblk % 2}'
                nblk += 1
                hk = ht_keys(b0, bn)
                for k in range(KC):
                    p.mm(bank(pv)[:, 0:bn], W0[:, k, cc * 128:(cc + 1) * 128], HT[:, k, b0:b0 + bn],
                         start=(k == 0), stop=(k == KC - 1), r=['W0'] + hk, w=[PB[pv]])
                for k in range(KC):
                    p.mm(bank(pg)[:, 0:bn], W0[:, k, 512 + cc * 128:512 + (cc + 1) * 128], HT[:, k, b0:b0 + bn],
                         start=(k == 0), stop=(k == KC - 1), r=['W0'] + hk, w=[PB[pg]])
                p.cp('act', AV[:, 0:bn], bank(pv)[:, 0:bn], r=[PB[pv]], w=[avk])
                p.act(SIG[:, 0:bn], bank(pg)[:, 0:bn], AF.Sigmoid, r=[PB[pg]], w=[sgk])
                pos = 15 + b0 if b0 < S else 2078
                p.tt('pool', GLU[:, pos:pos + bn], AV[:, 0:bn], SIG[:, 0:bn], ALU.mult, r=[avk, sgk], w=[gk])
                p.cp('act', GLB[:, pos:pos + bn], GLU[:, pos:pos + bn], r=[gk], w=[bk])
            for bi_, (p0, pn) in enumerate(OB5):
                cb = 4 + bi_ % 4
                for k in range(NPE):
                    p.mm(bank(cb)[:, 0:pn], DG[:, k, :], GLB[:, p0 + k:p0 + k + pn], start=(k == 0), stop=(k == NPE - 1),
                         r=[dk, bk], w=[PB[cb]])
                p.act(ACC[:, p0:p0 + pn], bank(cb)[:, 0:pn], AF.Identity, r=[PB[cb], 'VT'], w=[ak],
                      bias=VT[:, V_CB + cc:V_CB + cc + 1])
            if cc > 0:
                castout(cc - 1)
            for k in range(NPE, 31):
                p.stt(ACC, GLU[:, k:k + 2319], CW[:, cc, k:k + 1], ACC, ALU.mult, ALU.add, r=[gk, ak, 'CW'], w=[ak])
        castout(3)

    def phase_attn():
        p.barrier()
        AR.reset()
        KT2 = AR.alloc([2, T], BF16)
        VA3 = AR.alloc([NT, 2, 192], BF16)
        ROPE = AR.alloc([NT, 128], F32)
        QAB = AR.alloc([4, 2, 512], BF16)
        QN = AR.alloc([512], F32)
        TA = AR.alloc([8, 32], F32)
        TB = AR.alloc([8, 32], F32)
        QR = AR.alloc([512], BF16)
        SSQ = AR.alloc([16], F32)
        TMPS = [(QN, TA, TB, SSQ), (AR.alloc([512], F32), AR.alloc([8, 32], F32), AR.alloc([8, 32], F32), AR.alloc([16], F32))]
        QR1 = AR.alloc([512], BF16)
        KRs = [AR.alloc([4, 2, 128], BF16), AR.alloc([4, 2, 128], BF16)]
        arena_mark = AR.off
        p.dma('sp', ROPE, rope_d.rearrange("p (t c) -> p t c", t=NT), writes=['ROPE'], sem='ROPE')
        wload(W0, 'W0', w_in_d[0], 1024, 768)
        p.ms('dve', VA3, 1.0, w=['VA'])
        p.ms('pool', QAB, 0.0, w=['QAB0'])

        def normrope(t0, nt, hpt, src3, pbk, dst4, dkey, gsel, cq, sq_, ts_=0):
            nh = nt * hpt
            QN, TA, TB, SSQ = TMPS[ts_]
            kq, ka, kb_, ks = [f"{n_}{ts_}" for n_ in ("QN", "TA", "TB", "SSQ")]
            n = nh * 64
            p.act(QN[:, 0:n].rearrange("p (t c) -> p t c", t=nt), src3, AF.Square, r=[pbk], w=[kq])
            p.red(SSQ[:, 0:nh], QN[:, 0:n].rearrange("p (h d) -> p h d", h=nh), r=[kq], w=[ks])
            p.ts('dve', SSQ[:, 0:nh], SSQ[:, 0:nh], 1.0 / 64, EPS, ALU.mult, ALU.add, r=[ks], w=[ks])
            p.tt('pool', SSQ[:, 0:nh], SSQ[:, 0:nh], NHB[:, 0:nh], ALU.pow, r=[ks, 'NHB'], w=[ks])
            qn4 = QN[:, 0:n].rearrange("p (t h d) -> p t h d", t=nt, h=hpt)
            p.tt('dve', qn4, src3.rearrange("p t (h d) -> p t h d", h=hpt),
                 SSQ[:, 0:nh].rearrange("p (t h) -> p t h", t=nt).unsqueeze(3).broadcast_to([128, nt, hpt, 64]),
                 ALU.mult, r=[pbk, ks], w=[kq])
            qn3 = QN[:, 0:n].rearrange("p (h d) -> p h d", h=nh)
            p.tt('dve', qn3, qn3, GQK[:, gsel, :].unsqueeze(1).broadcast_to([128, nh, 64]), ALU.mult,
                 r=[kq, f'GQK{gsel}'], w=[kq])
            q5 = QN[:, 0:n].rearrange("p (t h d two) -> p t h d two", t=nt, h=hpt, two=2)
            A, B = q5[:, :, :, :, 0], q5[:, :, :, :, 1]
            cosb = ROPE[:, t0:t0 + nt, cq:cq + 32].unsqueeze(2).broadcast_to([128, nt, hpt, 32])
            sinb = ROPE[:, t0:t0 + nt, sq_:sq_ + 32].unsqueeze(2).broadcast_to([128, nt, hpt, 32])
            ta = TA[:, 0:nh, :].rearrange("p (t h) d -> p t h d", t=nt)
            tb = TB[:, 0:nh, :].rearrange("p (t h) d -> p t h d", t=nt)
            p.tt('dve', ta, A, cosb, ALU.mult, r=[kq, 'ROPE'], w=[ka])
            p.tt('dve', tb, B, sinb, ALU.mult, r=[kq, 'ROPE'], w=[kb_])
            p.tt('dve', dst4[0], ta, tb, ALU.subtract, r=[ka, kb_], w=[dkey])
            p.tt('dve', ta, A, sinb, ALU.mult, r=[kq, 'ROPE'], w=[ka])
            p.tt('dve', tb, B, cosb, ALU.mult, r=[kq, 'ROPE'], w=[kb_])
            p.tt('dve', dst4[1], ta, tb, ALU.add, r=[ka, kb_], w=[dkey])

        QABs = [QAB, None]
        QRs = [QR, QR1]

        def prepA1(t):
            for k in range(KC):
                p.mm(bank(6), HT[:, k, t * 128:(t + 1) * 128], W0[:, k, 0:512],
                     start=(k == 0), stop=(k == KC - 1), r=['W0', f'HT{t}a', f'HT{t}b'], w=[PB[6]])

        def prepA2(t):
            qd = QRs[t % 2].rearrange("p (t h d two) -> p t h d two", t=1, h=8, two=2)
            normrope(t, 1, 8, bank(6).rearrange("p (t c) -> p t c", t=1), PB[6], (qd[:, :, :, :, 0], qd[:, :, :, :, 1]),
                     f'QR{t % 2}', 0, 0, 32, ts_=1)

        def prepA(t):
            prepA1(t)
            prepA2(t)

        def prepB1(t):
            ptb = bankb(7)
            for pr in range(4):
                p.tr(ptb[:, pr * 128:(pr + 1) * 128], QRs[t % 2][:, pr * 128:(pr + 1) * 128], identb[:],
                     r=[f'QR{t % 2}', 'identb'], w=[PB[7]])

        def prepB2(t, qs, bsel):
            qab = QABs[bsel]
            src3 = bankb(7)[:, 0:512].rearrange("p (a n) -> p a n", a=4)
            p.cp('dve', qab[0:64, :, 0, qs * 128:(qs + 1) * 128], src3[0:64], r=[PB[7]], w=[f'QAB{bsel}'])
            p.cp('dve', qab[64:128, :, 1, qs * 128:(qs + 1) * 128], src3[64:128], r=[PB[7]], w=[f'QAB{bsel}'])

        def prepB(t, qs, bsel):
            prepB1(t)
            prepB2(t, qs, bsel)

        KB = [(0, 4), (4, 4), (8, 4), (12, 4), (16, 2)]

        def kvA(bi_):
            t0, nt = KB[bi_]
            pst = PSt[2]
            pk = [PB[4], PB[5]]
            kr = KRs[bi_ % 2]
            kk = f'KR{bi_ % 2}'
            for j in range(nt):
                t = t0 + j
                for k in range(KC):
                    p.mm(pst[:, j * 256:(j + 1) * 256], HT[:, k, t * 128:(t + 1) * 128], W0[:, k, 512:768],
                         start=(k == 0), stop=(k == KC - 1), r=['W0', f'HT{t}a', f'HT{t}b'], w=pk)
            v3 = pst[:, 0:nt * 256].rearrange("p (t c) -> p t c", c=256)
            kd = kr[:, 0:nt, :, 0:64].rearrange("p t h (d two) -> p t h d two", two=2)
            normrope(t0, nt, 2, v3[:, :, 0:128], pk[0], (kd[:, :, :, :, 0], kd[:, :, :, :, 1]), kk, 1, 64, 96)
            for kv in range(2):
                vsrc = v3[:, :, 128 + kv * 64:128 + (kv + 1) * 64]
                p.cp('act', VA3[:, t0:t0 + nt, kv, 0:64], vsrc, r=pk, w=['VA'])
                p.cp('act', VA3[:, t0:t0 + nt, kv, 128:192], vsrc, r=pk, w=['VA'])
            p.cp('act', kr[:, 0:nt, :, 64:128], kr[:, 0:nt, :, 0:64], r=[kk], w=[kk])

        def kvB(bi_):
            t0, nt = KB[bi_]
            kr = KRs[bi_ % 2]
            kk = f'KR{bi_ % 2}'
            tb_ = 7
            ptb = bankb(tb_)
            for h in range(2):
                for j in range(nt):
                    sl = h * nt + j
                    p.tr(ptb[:, sl * 128:(sl + 1) * 128], kr[:, j, h, :], identb[:], r=[kk, 'identb'], w=[PB[tb_]])
            p.cp('act', KT2[:, :, t0 * 128:(t0 + nt) * 128], ptb[:, 0:2 * nt * 128].rearrange("p (h n) -> p h n", h=2),
                 r=[PB[tb_]], w=['KT'])

        SZs = [AR.alloc([4, 512], BF16), AR.alloc([4, 512], BF16)]
        M2 = AR.alloc([512], F32)
        VEs = [AR.alloc([512], F32), AR.alloc([512], F32)]
        MNs = [AR.alloc([512], F32), AR.alloc([512], F32)]
        T1s = [AR.alloc([512], F32), AR.alloc([512], F32)]
        SQs = [AR.alloc([4, 512], BF16), AR.alloc([4, 512], BF16)]
        ytk = [f'YT{cc}' for cc in range(4)]

        def lnS(bi_):
            b0, bn = BLOCKS[bi_]
            SQ, VE, MN = SQs[bi_ % 2], VEs[bi_ % 2], MNs[bi_ % 2]
            sqk, vek, mnk = f'SQ{bi_ % 2}', f'VE{bi_ % 2}', f'MN{bi_ % 2}'
            for cc in range(4):
                p.act(SQ[:, cc, 0:bn], YT[:, cc, b0:b0 + bn], AF.Square, r=ytk, w=[sqk])
            for cc in range(4):
                p.mm(bank(0)[:, 0:bn], ONESM[:], YT[:, cc, b0:b0 + bn], start=(cc == 0), stop=(cc == 3),
                     r=ytk + ['ONESM'], w=[PB[0]])
            for cc in range(4):
                p.mm(bank(1)[:, 0:bn], ONESM[:], SQ[:, cc, 0:bn], start=(cc == 0), stop=(cc == 3), r=[sqk, 'ONESM'], w=[PB[1]])
            p.act(M2[:, 0:bn], bank(0)[:, 0:bn], AF.Square, r=[PB[0]], w=['M2'])
            p.cp('act', MN[:, 0:bn], bank(0)[:, 0:bn], r=[PB[0]], w=[mnk])
            p.stt(VE[:, 0:bn], bank(1)[:, 0:bn], EPS, M2[:, 0:bn], ALU.add, ALU.subtract, r=[PB[1], 'M2'], w=[vek])
            p.act(VE[:, 0:bn], VE[:, 0:bn], AF.Sqrt, r=[vek], w=[vek])
            p.recip(VE[:, 0:bn], VE[:, 0:bn], r=[vek], w=[vek])

        def lnZ(bi_):
            b0, bn = BLOCKS[bi_]
            hk = ht_keys(b0, bn)
            SZ = SZs[bi_ % 2]
            for cc in range(4):
                pz = 2 + (bi_ * 4 + cc) % 2
                for k in range(KC):
                    p.mm(bank(pz)[:, 0:bn], W1[:, k, cc * 128:(cc + 1) * 128], HT[:, k, b0:b0 + bn],
                         start=(k == 0), stop=(k == KC - 1), r=['W1'] + hk, w=[PB[pz]])
                p.act(SZ[:, cc, 0:bn], bank(pz)[:, 0:bn], AF.Silu, r=[PB[pz]], w=[f'SZ{bi_ % 2}_{cc}'])

        def lnP(bi_):
            b0, bn = BLOCKS[bi_]
            VE, MN, SZ = VEs[bi_ % 2], MNs[bi_ % 2], SZs[bi_ % 2]
            vek, mnk = f'VE{bi_ % 2}', f'MN{bi_ % 2}'
            for cc in range(4):
                T1 = T1s[cc % 2]
                tk = f'T1{cc % 2}'
                p.tt('dve', T1[:, 0:bn], YT[:, cc, b0:b0 + bn], MN[:, 0:bn], ALU.subtract, r=ytk + [mnk], w=[tk])
                p.tt('dve', T1[:, 0:bn], T1[:, 0:bn], VE[:, 0:bn], ALU.mult, r=[tk, vek], w=[tk])
                p.act(T1[:, 0:bn], T1[:, 0:bn], AF.Silu, r=[tk, 'VT'], w=[tk],
                      scale=VT[:, V_LNG + cc:V_LNG + cc + 1], bias=VT[:, V_LNB + cc:V_LNB + cc + 1])
                p.tt('dve', YT[:, cc, b0:b0 + bn], T1[:, 0:bn], SZ[:, cc, 0:bn], ALU.mult,
                     r=[tk, f'SZ{bi_ % 2}_{cc}'], w=[f'YTL{cc}_{bi_}'])

        lnS(0)
        lnZ(0)
        kvA(0)
        for bi_ in range(len(KB)):
            chains = []
            if bi_ >= 1:
                chains.append(p.capture(lambda: (kvB(bi_ - 1), prepB(bi_ - 2, bi_ - 2, 0) if bi_ >= 2 else None)))
            if bi_ + 1 < len(KB):
                chains.append(p.capture(lambda: (lnS(bi_ + 1), lnZ(bi_ + 1))))
                chains.append(p.capture(lambda: kvA(bi_ + 1)))
            if bi_ < 4:
                chains.append(p.capture(lambda: prepA(bi_)))
            chains.append(p.capture(lambda: lnP(bi_)))
            p.emit_rr(chains)
        kvB(4)
        prepB(2, 2, 0)
        prepB(3, 3, 0)
        p.barrier()
        AR.off = arena_mark
        PTs = [AR.alloc([2, 512], BF16) for _ in range(3)]
        SZ = AR.alloc([512], F32)
        RR = AR.alloc([512], F32)
        OA = AR.alloc([512], F32)
        OB = AR.alloc([512], F32)
        QABs[1] = AR.alloc([4, 2, 512], BF16)
        p.ms('pool', QABs[1], 0.0, w=['QAB1'])
        wload(W1, 'W1', w_in_d[0], 1792 + 512, 512)
        for bidx, (q0, qn) in enumerate(BLOCKS):
            nsub = qn // 128
            keytiles = list(range(NT)) if q0 < S else [16, 17]
            nk = len(keytiles)
            QAB = QABs[bidx % 2]
            qabk = f'QAB{bidx % 2}'
            sched = {}
            if bidx + 1 < len(BLOCKS):
                nq0, nqn = BLOCKS[bidx + 1]
                for qs in range(nqn // 128):
                    sched.setdefault(2 + 16 * qs, []).append(('A1', nq0 // 128 + qs, qs))
                    sched.setdefault(4 + 16 * qs, []).append(('A2', nq0 // 128 + qs, qs))
                    sched.setdefault(12 + 16 * qs, []).append(('B1', nq0 // 128 + qs, qs))
                    sched.setdefault(14 + 16 * qs, []).append(('B2', nq0 // 128 + qs, qs))
            steps = [(pr, idx, kt) for pr in range(4) for idx, kt in enumerate(keytiles)]

            def qk(sn):
                pr, idx, kt = steps[sn]
                st = PSt[sn % 2]
                sk = [PB[2 * (sn % 2)], PB[2 * (sn % 2) + 1]]
                for ab in range(2):
                    p.mm(st[:, ab * 512:ab * 512 + qn], KT2[:, pr // 2, kt * 128:(kt + 1) * 128], QAB[:, pr, ab, 0:qn],
                         r=['KT', qabk], w=sk)

            hk = ht_keys(q0, qn)
            qk(0)
            for sn, (pr, idx, kt) in enumerate(steps):
                kv = pr // 2
                if sn + 1 < len(steps):
                    qk(sn + 1)
                for (kind, pt_, pqs) in sched.get(sn, []):
                    if kind == 'A1':
                        prepA1(pt_)
                    elif kind == 'A2':
                        prepA2(pt_)
                    elif kind == 'B1':
                        prepB1(pt_)
                    else:
                        prepB2(pt_, pqs, (bidx + 1) % 2)
                st = PSt[sn % 2]
                sk = [PB[2 * (sn % 2)], PB[2 * (sn % 2) + 1]]
                pt = PTs[sn % 3]
                ptk = f'PT{sn % 3}'
                p.act(pt[:, :, 0:qn], st[:, :].rearrange("p (a n) -> p a n", a=2)[:, :, 0:qn], AF.Exp, r=sk, w=[ptk])
                p.mm(bank(4)[:, 0:qn], VA3[:, kt, kv, 0:128], pt[:, 0, 0:qn], start=(idx == 0), stop=(idx == nk - 1),
                     r=[ptk, 'VA'], w=[PB[4]])
                p.mm(bank(5)[:, 0:qn], VA3[:, kt, kv, 64:192], pt[:, 1, 0:qn], start=(idx == 0), stop=(idx == nk - 1),
                     r=[ptk, 'VA'], w=[PB[5]])
                if idx == nk - 1:
                    p.cp('act', OA[:, 0:qn], bank(4)[:, 0:qn], r=[PB[4]], w=['OA'])
                    p.cp('act', OB[:, 0:qn], bank(5)[:, 0:qn], r=[PB[5]], w=['OB'])
                    p.recip(RR[0:64, 0:qn], OA[64:128, 0:qn], r=['OA'], w=['RR'])
                    p.recip(RR[64:128, 0:qn], OB[0:64, 0:qn], r=['OB'], w=['RR'])
                    p.tt('dve', YT[0:64, 4 + pr, q0:q0 + qn], OA[0:64, 0:qn], RR[0:64, 0:qn], ALU.mult,
                         r=['OA', 'RR'], w=[f'YT{4 + pr}'])
                    p.tt('dve', YT[64:128, 4 + pr, q0:q0 + qn], OB[64:128, 0:qn], RR[64:128, 0:qn], ALU.mult,
                         r=['OB', 'RR'], w=[f'YT{4 + pr}'])

        SZ2 = [SZ, RR]
        n_ = 0
        for j in range(4):
            for (b0, bn) in BLOCKS:
                hk = ht_keys(b0, bn)
                bi = 6 + n_ % 2
                sz = SZ2[n_ % 2]
                szk = ['SZ', 'RR'][n_ % 2]
                n_ += 1
                for k in range(KC):
                    p.mm(bank(bi)[:, 0:bn], W1[:, k, j * 128:(j + 1) * 128], HT[:, k, b0:b0 + bn],
                         start=(k == 0), stop=(k == KC - 1), r=['W1'] + hk, w=[PB[bi]])
                p.act(sz[:, 0:bn], bank(bi)[:, 0:bn], AF.Silu, r=[PB[bi]], w=[szk])
                p.tt('dve', YT[:, 4 + j, b0:b0 + bn], YT[:, 4 + j, b0:b0 + bn], sz[:, 0:bn], ALU.mult,
                     r=[f'YT{4 + j}', szk], w=[f'YT{4 + j}'])

    def phase_out(l):
        p.barrier()
        AR.reset()
        XN = [AR.alloc([D], F32) for _ in range(4)]
        TMP = AR.alloc([D], F32)
        last = (l == 1)
        if last:
            FG = AR.alloc([D], F32)
            JUNK = AR.alloc([D], BF16)
            p.dma('sp', FG, fg_d.partition_broadcast(128), writes=['FG'], sem='FG')
        wload(W0, 'W0', w_out_d[l], 0, 1024)
        ntile = NTL if last else NT
        ytk = [f'YT{c}' for c in range(KC)]
        def tail(t):
            xn = XN[t % 4]
            nk = f"XN{t % 4}"
            p.ts('dve', MSQ[:, t:t + 1], SS[:, t:t + 1], 1.0 / D, EPS, ALU.mult, ALU.add, r=[f'SS{t}'], w=[f'MSQ{t}'])
            p.tt('pool', RS[:, t:t + 1], MSQ[:, t:t + 1], NHB[:, 0:1], ALU.pow, r=[f'MSQ{t}', 'NHB'], w=[f'RS{t}'])
            p.stt(xn, xn, RS[:, t:t + 1], FG, ALU.mult, ALU.mult, r=[nk, f'RS{t}', 'FG'], w=[nk])
            p.dma('sp', out_d[t * 128:(t + 1) * 128, :], xn, reads=[nk], writes=['out'], sem=f'st{t % 4}')

        XT3 = [XT[0], XT[1], AR.alloc([D], F32)]

        def xload(t):
            p.dma('sp', XT3[t % 3][:] if t % 3 < 2 else XT3[2], xsrc(l, t), reads=['xs1'] if l else [],
                  writes=[f"XT{t % 3}"], sem=f"XT{t % 3}")

        for t in range(ntile):
            xt = XT3[t % 3]
            xk = f"XT{t % 3}"
            xn = XN[t % 4]
            nk = f"XN{t % 4}"
            wsel = 0 if t < NTL else 1
            if t == 0:
                xload(0)
            if t + 1 < ntile:
                xload(t + 1)
            for half in range(2):
                bi = 2 * (t % 2) + half
                for k in range(KC):
                    p.mm(bank(bi), YT[:, k, t * 128:(t + 1) * 128], W0[:, k, half * 512:(half + 1) * 512],
                         start=(k == 0), stop=(k == KC - 1), r=['W0'] + ytk, w=[PB[bi]])
                sl = slice(half * 512, (half + 1) * 512)
                p.tt('dve', TMP[:, sl], bank(bi), GATE[:, wsel, sl], ALU.mult, r=[PB[bi], 'GATE'], w=['TMP'])
                p.tt('dve', xn[:, sl], TMP[:, sl], xt[:, sl], ALU.add, r=['TMP', xk], w=[nk])
            if not last:
                p.dma('sp', xs1[t * 128:(t + 1) * 128, :], xn, reads=[nk], writes=['xs1'], sem=f'st{t % 4}')
            else:
                p.act(JUNK, xn, AF.Square, r=[nk], w=['JUNK', f'SS{t}'], accum_out=SS[:, t:t + 1])
                if t > 1:
                    tail(t - 2)
        if last:
            tail(ntile - 2)
            tail(ntile - 1)

    def phase_fourier():
        p.barrier()
        AR.reset()
        Fm = AR.alloc([NTL, 512], BF16)
        DB = [AR.alloc([8, 512], BF16), AR.alloc([8, 512], BF16)]
        PQ = AR.alloc([2, 4, 512], BF16)
        UB = AR.alloc([512], F32)
        FRa = AR.alloc([512], BF16)
        FRb = AR.alloc([512], BF16)
        SZa = AR.alloc([512], F32)
        SZb = AR.alloc([512], F32)
        NYQ = AR.alloc([2], BF16)
        PN = AR.alloc([4, 2], BF16)
        FRN = AR.alloc([2], BF16)
        SZN = AR.alloc([2], F32)
        wload(W0, 'W0', w_in_d[1], 0, 512)
        wload(W1, 'W1', w_in_d[1], 2048, 512)
        p.dma('pool', NYQ, nyq_d, writes=['NYQ'], sem='NYQ')
        for t in range(NTL):
            bi = t % 2
            for k in range(KC):
                p.mm(bank(bi), HT[:, k, t * 128:(t + 1) * 128], W0[:, k, 0:512],
                     start=(k == 0), stop=(k == KC - 1), r=['W0', f'HT{t}a', f'HT{t}b'], w=[PB[bi]])
            p.cp('act', Fm[:, t, :], bank(bi), r=[PB[bi]], w=['Fm'])

        def zproj(g, t0, n, bi, sz, szk):
            hk = [f'HT{t}{s_}' for t in range(t0 // 128, (t0 + n - 1) // 128 + 1) for s_ in 'ab']
            for k in range(KC):
                p.mm(bank(bi)[:, 0:n], W1[:, k, g * 128:(g + 1) * 128], HT[:, k, t0:t0 + n],
                     start=(k == 0), stop=(k == KC - 1), r=['W1'] + hk, w=[PB[bi]])
            p.act(sz[:, 0:n], bank(bi)[:, 0:n], AF.Silu, r=[PB[bi]], w=[szk])

        nload = 0
        for kb in range(2):
            for trig in range(2):
                for half in range(2):
                    db = DB[nload % 2]
                    dk = f'DB{nload % 2}'
                    nload += 1
                    p.dma('sp', db, dft_d[trig, kb][:, half * 4096:(half + 1) * 4096].rearrange("p (l n) -> p l n", l=8),
                          writes=[dk], sem=dk)
                    for l8 in range(8):
                        lc = half * 8 + l8
                        for g in range(4):
                            p.mm(bank(g), Fm[:, lc, g * 128:(g + 1) * 128], db[:, l8, :],
                                 start=(lc == 0), stop=(lc == 15), r=['Fm', dk], w=[PB[g]])
                for g in range(4):
                    p.cp('act' if g % 2 == 0 else 'dve', PQ[:, trig, g, :], bank(g), r=[PB[g]], w=[f'PQ{g}'])
            b0 = kb * 512
            if kb == 0:
                mt0, mn = 1537, 511
            else:
                mt0, mn = 1025, 512
            for g in range(4):
                p.mm(bank(4), CCSC[:, 0:128], PQ[:, 0, g, :], r=['CCSC', f'PQ{g}'], w=[PB[4]])
                p.mm(bank(5), CCSC[:, 128:256], PQ[:, 1, g, :], r=['CCSC', f'PQ{g}'], w=[PB[5]])
                p.cp('act', UB, bank(4), r=[PB[4]], w=['UB'])
                p.tt('dve', FRa, UB, bank(5), ALU.add, r=['UB', PB[5]], w=['FRa'])
                p.tt('dve', FRb, UB, bank(5), ALU.subtract, r=['UB', PB[5]], w=['FRb'])
                p.mm(bank(6), WC[:, g, :], FRa, r=['WC', 'FRa'], w=[PB[6]])
                p.mm(bank(7), WC[:, g, :], FRb, r=['WC', 'FRb'], w=[PB[7]])
                zproj(g, b0, 512, 4, SZa, 'SZa')
                zproj(g, mt0, mn, 5, SZb, 'SZb')
                p.tt('dve', YT[:, g, b0:b0 + 512], bank(6), SZa, ALU.mult, r=[PB[6], 'SZa'], w=[f'YT{g}'])
                if kb == 0:
                    p.tt('dve', YT[:, g, 2047:1536:-1], bank(7)[:, 1:512], SZb[:, 510::-1], ALU.mult,
                         r=[PB[7], 'SZb'], w=[f'YT{g}'])
                else:
                    p.tt('dve', YT[:, g, 1536:1024:-1], bank(7)[:, 0:512], SZb[:, 511::-1], ALU.mult,
                         r=[PB[7], 'SZb'], w=[f'YT{g}'])
        for g in range(4):
            for lc in range(NTL):
                p.mm(bank(0)[:, g * 2:g * 2 + 2], Fm[:, lc, g * 128:(g + 1) * 128], NYQ, start=(lc == 0), stop=(lc == NTL - 1),
                     r=['Fm', 'NYQ'], w=[PB[0]])
        p.cp('act', PN, bank(0)[:, 0:8].rearrange("p (g n) -> p g n", g=4), r=[PB[0]], w=['PN'])
        for g in range(4):
            p.mm(bank(1)[:, 0:2], CCSC[:, 0:128], PN[:, g, :], r=['CCSC', 'PN'], w=[PB[1]])
            p.cp('act', FRN, bank(1)[:, 0:2], r=[PB[1]], w=['FRN'])
            p.mm(bank(2)[:, 0:2], WC[:, g, :], FRN, r=['WC', 'FRN'], w=[PB[2]])
            zproj(g, 1024, 2, 3, SZN, 'SZN')
            p.tt('dve', YT[:, g, 1024:1025], bank(2)[:, 0:1], SZN[:, 0:1], ALU.mult, r=[PB[2], 'SZN'], w=[f'YT{g}'])

    def phase_na():
        p.barrier()
        AR.reset()
        VA = AR.alloc([NT, 8, 65], BF16)
        KT2 = AR.alloc([T], BF16)
        QAB = AR.alloc([NTL, 2, 128], BF16)
        EBG = AR.alloc([2, NP, 128], BF16)
        PT = [AR.alloc([7, 2, 128], BF16), AR.alloc([7, 2, 128], BF16)]
        YB = AR.alloc([128], BF16)
        RINV = AR.alloc([2], F32)
        SZ = AR.alloc([512], F32)
        wload(W0, 'W0', w_in_d[1], 1536, 512, 0)
        p.ms('dve', VA, 1.0, w=['VA'])
        p.ms('pool', QAB, 0.0, w=['QAB'])
        for t in range(NT):
            bi = 6 + t % 2
            for k in range(KC):
                p.mm(bank(bi), HT[:, k, t * 128:(t + 1) * 128], W0[:, k, 0:512],
                     start=(k == 0), stop=(k == KC - 1), r=['W0', f'HT{t}a', f'HT{t}b'], w=[PB[bi]])
            p.cp('act' if t % 2 == 0 else 'dve', VA[:, t, :, 0:64], bank(bi).rearrange("p (h d) -> p h d", h=8),
                 r=[PB[bi]], w=['VA'])
        EBGs = [EBG, AR.alloc([2, NP, 128], BF16)]
        for hg in range(4):
            Wn, wnk = (W1, 'W1') if hg % 2 == 0 else (W0, 'W0')
            EBG, ebk = EBGs[hg % 2], f'EBG{hg % 2}'
            wload(Wn, wnk, w_in_d[1], 512 + hg * 128, 128, 0)
            wload(Wn, wnk, w_in_d[1], 1024 + hg * 128, 128, 128)
            wload(Wn, wnk, w_in_d[1], 2048 + 512 + hg * 128, 128, 256)
            p.dma('pool', EBG, bm_d[:, hg, :].rearrange("p (h a q) -> p h a q", h=2, q=128), writes=[ebk], sem=ebk)
            p.act(EBG, EBG, AF.Exp, r=[ebk], w=[ebk])
            for bi_, (b0, bn) in enumerate(BLOCKS):
                hk = ht_keys(b0, bn)
                bi = 6 + bi_ % 2
                for k in range(KC):
                    p.mm(bank(bi)[:, 0:bn], Wn[:, k, 128:256], HT[:, k, b0:b0 + bn],
                         start=(k == 0), stop=(k == KC - 1), r=[wnk] + hk, w=[PB[bi]])
                p.cp('act', KT2[:, b0:b0 + bn], bank(bi)[:, 0:bn], r=[PB[bi]], w=['KT2'])
            for bi_, (b0, bn) in enumerate(BLOCKS[:4]):
                hk = ht_keys(b0, bn)
                bi = 6 + bi_ % 2
                for k in range(KC):
                    p.mm(bank(bi), Wn[:, k, 0:128], HT[:, k, b0:b0 + bn],
                         start=(k == 0), stop=(k == KC - 1), r=[wnk] + hk, w=[PB[bi]])
                src3 = bank(bi).rearrange("p (t n) -> p t n", t=4)
                p.act(QAB[0:64, b0 // 128:b0 // 128 + 4, 0, :], src3[0:64], AF.Copy, r=[PB[bi]], w=['QAB'], scale=0.125)
                p.act(QAB[64:128, b0 // 128:b0 // 128 + 4, 1, :], src3[64:128], AF.Copy, r=[PB[bi]], w=['QAB'], scale=0.125)

            def Jof(i):
                return na_tiles[i] + [(16, None), (17, None)]

            def qkg(i):
                for idx, (j, pat) in enumerate(Jof(i)):
                    bi = idx // 2
                    p.mm(bank(bi)[:, (idx % 2) * 256:(idx % 2 + 1) * 256], KT2[:, j * 128:(j + 1) * 128],
                         QAB[:, i, :, :], r=['KT2', 'QAB'], w=[PB[bi]])

            EBv = EBG.rearrange("p h a q -> p a h q")

            def postA(i):
                pob = 4 + i % 2
                po3 = bank(pob)[:, 0:130].rearrange("p (h d) -> p h d", h=2)
                p.recip(RINV, po3[:, :, 64], r=[PB[pob]], w=['RINV'])
                p.tt('dve', YB.rearrange("p (h d) -> p h d", h=2), po3[:, :, 0:64],
                     RINV.unsqueeze(2).broadcast_to([128, 2, 64]), ALU.mult, r=[PB[pob], 'RINV'], w=['YB'])
                p.tr(bankb(6)[:, 0:128], YB, identb[:], r=['YB', 'identb'], w=[PB[6]])

            def postC(i):
                p.cp('dve', YT[:, 4 + hg, i * 128:(i + 1) * 128], bankb(6)[:, 0:128], r=[PB[6]], w=[f'YT{4 + hg}'])

            qkg(0)
            for i in range(NTL):
                J = Jof(i)
                nj = len(J)
                nl = nj - 2
                pt = PT[i % 2]
                ptk = f'PTn{i % 2}'
                pob = 4 + i % 2
                n0 = min(nj, 4)
                p.act(pt[:, 0:n0], PSt[0][:, 0:n0 * 256].rearrange("p (j h q) -> p j h q", h=2, q=128), AF.Exp,
                      r=[PB[0], PB[1]], w=[ptk + 'a'])
                if nj > 4:
                    p.act(pt[:, 4:nj], PSt[1][:, 0:(nj - 4) * 256].rearrange("p (j h q) -> p j h q", h=2, q=128), AF.Exp,
                          r=[PB[2], PB[3]], w=[ptk + 'b'])
                if i + 1 < NTL:
                    qkg(i + 1)
                if i > 0:
                    postA(i - 1)
                s0 = J[0][1]
                p.tt('dve', pt[:, 0:nl], pt[:, 0:nl], EBv[:, s0:s0 + nl], ALU.mult,
                     r=[ebk, ptk + 'a', ptk + 'b'], w=[ptk + 'a', ptk + 'b'])
                if i > 0:
                    postC(i - 1)
                for h in range(2):
                    for idx, (j, pat) in enumerate(J):
                        p.mm(bank(pob)[:, h * 65:(h + 1) * 65], pt[:, idx, h, :], VA[:, j, 2 * hg + h, :],
                             start=(idx == 0), stop=(idx == nj - 1), r=[ptk + 'a', ptk + 'b', 'VA'], w=[PB[pob]])
            postA(NTL - 1)
            postC(NTL - 1)
            for kb in range(4):
                b0 = kb * 512
                hk = ht_keys(b0, 512)
                bi = 6 + kb % 2
                for k in range(KC):
                    p.mm(bank(bi), Wn[:, k, 256:384], HT[:, k, b0:b0 + 512],
                         start=(k == 0), stop=(k == KC - 1), r=[wnk] + hk, w=[PB[bi]])
                p.act(SZ, bank(bi), AF.Silu, r=[PB[bi]], w=['SZ'])
                p.tt('dve', YT[:, 4 + hg, b0:b0 + 512], YT[:, 4 + hg, b0:b0 + 512], SZ, ALU.mult,
                     r=[f'YT{4 + hg}', 'SZ'], w=[f'YT{4 + hg}'])

    phases = [lambda: adaln(0), lambda: None, phase_conv, phase_attn, lambda: phase_out(0),
              lambda: adaln(1), lambda: None, phase_fourier, phase_na, lambda: phase_out(1)]
    for ph in phases[:nphase]:
        ph()
    p.finalize()
    return nc, plist


def _consts():
    ident = np.eye(128, dtype=np.float32)
    tok = np.arange(S)
    row = (tok // 64).astype(np.float32)
    col = (tok % 64).astype(np.float32)
    nf = 16
    inv = (np.float32(10000.0) ** (-np.arange(nf, dtype=np.float32) / nf)).astype(np.float32)
    ang = np.concatenate([row[:, None] * inv, col[:, None] * inv], axis=-1).astype(np.float32)
    cos = np.cos(ang).astype(np.float32)
    sin = np.sin(ang).astype(np.float32)
    cos = np.concatenate([cos, np.ones((LC, 32), np.float32)], 0)
    sin = np.concatenate([sin, np.zeros((LC, 32), np.float32)], 0)
    tab = np.concatenate([cos * 0.125, sin * 0.125, cos, sin], axis=-1).astype(np.float32)
    rope = np.ascontiguousarray(tab.reshape(NT, 128, 128).transpose(1, 0, 2)).reshape(128, NT * 128)
    l = np.arange(S)[:, None]
    k = np.arange(S)[None, :]
    ph = ((l * k) % S).astype(np.float64) * (2 * np.pi / S)
    dft = np.empty((2, 4, 128, 16 * 512), dtype=ml_dtypes.bfloat16)
    for trig, m in enumerate((np.cos(ph), np.sin(ph))):
        m4 = m.reshape(16, 128, 4, 512).transpose(2, 1, 0, 3)
        dft[trig] = m4.reshape(4, 128, 16 * 512).astype(ml_dtypes.bfloat16)
    c = np.arange(128)[:, None]
    mm_ = np.arange(128)[None, :]
    phc = ((c * mm_) % 128).astype(np.float64) * (2 * np.pi / 128)
    ccsc = np.concatenate([np.cos(phc) / 512.0, -np.sin(phc) / 512.0], axis=1).astype(np.float32)
    nyq = np.stack([(-1.0) ** np.arange(128), (-1.0) ** np.arange(128)], axis=1).astype(np.float32)
    return ident, rope, dft, ccsc, nyq


def _bm_tables(rpb, plist):
    NP = len(plist)
    wq = np.arange(64)
    wk = np.arange(64)
    cs = np.clip(wq - 8, 0, 48)
    col_ok = (wk[None, :] >= cs[:, None]) & (wk[None, :] < cs[:, None] + 16)
    dc = np.clip(wk[None, :] - wq[:, None] + 15, 0, 30)
    bm = np.full((128, 8, NP, 128), NEG, dtype=np.float32)
    for pi, key in enumerate(plist):
        n = 0
        for qr in range(2):
            for kr in range(2):
                ok, dr = key[n]
                n += 1
                if not ok:
                    continue
                for h in range(8):
                    blk = rpb[h, dr][dc]
                    blk = np.where(col_ok, blk, np.float32(NEG))
                    bm[kr * 64:(kr + 1) * 64, h, pi, qr * 64:(qr + 1) * 64] = blk.T
    return np.ascontiguousarray(bm.reshape(128, 4, 2 * NP * 128))


_CACHE = {}


def kernel(x, c, ctx, c_ctx, ab_w_ada, ab_b_ada, ab_norm_g, ab_w_in, ab_conv_w, ab_conv_b,
           ab_ln_g, ab_ln_b, ab_q_norm_g, ab_k_norm_g, ab_w_out, cd_w_ada, cd_b_ada, cd_norm_g,
           cd_w_in, cd_w_fourier, cd_rpb, cd_w_out, final_norm_g):
    f = lambda a: np.ascontiguousarray(np.asarray(a, dtype=np.float32))
    if 'nc' not in _CACHE:
        _CACHE['nc'] = build()
        _CACHE['consts'] = _consts()
    nc, plist = _CACHE['nc']
    ident, rope, dft, ccsc, nyq = _CACHE['consts']
    bm = _bm_tables(f(cd_rpb)[0], plist)
    shared = {
        "ident": ident, "rope": rope, "dft": dft, "ccsc": ccsc, "bm": bm, "nyq": nyq,
        "gqk": np.stack([f(ab_q_norm_g)[0], f(ab_k_norm_g)[0]], 0),
        "conv_w": f(ab_conv_w)[0],
        "w_ada0": f(ab_w_ada)[0], "w_ada1": f(cd_w_ada)[0],
        "b_gate0": f(ab_b_ada)[0, 2048:3072], "b_gate1": f(cd_b_ada)[0, 2048:3072],
        "w_in0": f(ab_w_in)[0], "w_in1": f(cd_w_in)[0],
        "w_out0": f(ab_w_out)[0], "w_out1": f(cd_w_out)[0],
        "w_four": f(cd_w_fourier)[0],
        "final_g": f(final_norm_g),
    }
    in_maps = []
    for b in range(8):
        vecs = np.concatenate([
            f(c)[b].reshape(8, 128), f(c_ctx).reshape(8, 128),
            f(ab_b_ada)[0].reshape(24, 128), f(ab_norm_g)[0].reshape(8, 128),
            f(ab_conv_b)[0].reshape(4, 128), f(ab_ln_g)[0].reshape(4, 128), f(ab_ln_b)[0].reshape(4, 128),
            f(cd_b_ada)[0].reshape(24, 128), f(cd_norm_g)[0].reshape(8, 128)], axis=0)
        m = dict(shared)
        m["x"] = f(x)[b]
        m["ctx"] = f(ctx)[b]
        m["vecs"] = np.ascontiguousarray(vecs)
        in_maps.append(m)
    if _CACHE.get('debug_hook') is not None:
        return _CACHE['debug_hook'](nc, in_maps)
    res = run_bass_kernel_spmd(nc, in_maps, core_ids=list(range(8)))
    return np.stack([np.asarray(r["out"], dtype=np.float32) for r in res.results], axis=0)
```

```python
import numpy as np
from contextlib import ExitStack
import ml_dtypes
import concourse.bass as bass
import concourse.mybir as mybir
from concourse.bass_utils import run_bass_kernel_spmd

F32 = mybir.dt.float32
BF16 = mybir.dt.bfloat16
AF = mybir.ActivationFunctionType
ALU = mybir.AluOpType
AX = mybir.AxisListType

SAME_ENG_SYNC = True

D = 1024
S = 2048
LC = 256
T = S + LC
NT = 18
NTL = 16
KC = 8
EPS = 1e-6
AB_IN = 2816
CD_IN = 3072
NEG = -30000.0


class Prog:
    ENG = ['pe', 'act', 'dve', 'pool', 'sp']

    def __init__(self, nc):
        self.nc = nc
        self.ops = {e: [] for e in self.ENG}
        self.cnt = {}
        self.seen = {e: {} for e in self.ENG}
        self.res = {}
        self.stack = ExitStack()
        self.nsb = 0

    def sb(self, shape, dtype=F32, name=None):
        self.nsb += 1
        name = name or f"sb{self.nsb}"
        return self.stack.enter_context(self.nc.sbuf_tensor(name, list(shape), dtype))

    def ps(self, shape, dtype=F32, name=None):
        self.nsb += 1
        name = name or f"ps{self.nsb}"
        return self.stack.enter_context(self.nc.psum_tensor(name, list(shape), dtype))

    def _collect(self, eng, reads, writes):
        need = {}

        def add(t):
            if t is None:
                return
            k, v = t
            if need.get(k, 0) < v:
                need[k] = v
        for r in reads:
            st = self.res.get(r)
            if st is not None:
                add(st[0])
        for w in writes:
            st = self.res.get(w)
            if st is not None:
                add(st[0])
                for k, v in st[1].items():
                    add((k, v))
        waits = []
        seen = self.seen[eng]
        for k, v in need.items():
            if k == ('e', eng):
                if eng == 'pe' or not SAME_ENG_SYNC:
                    continue
            if k[0] == 'd':
                v = self.cnt[k]
            if seen.get(k, 0) >= v:
                continue
            seen[k] = v
            waits.append((k, v))
        return waits

    def _update(self, reads, writes, tick):
        k, v = tick
        for r in reads:
            st = self.res.setdefault(r, [None, {}])
            st[1][k] = v
        for w in writes:
            self.res[w] = [tick, {}]

    def capture(self, f):
        self._cap = []
        f()
        ops, self._cap = self._cap, None
        return ops

    def emit_rr(self, lists):
        idx = [0] * len(lists)
        left = sum(len(l) for l in lists)
        while left:
            for i, l in enumerate(lists):
                if idx[i] < len(l):
                    kind, args = l[idx[i]]
                    idx[i] += 1
                    left -= 1
                    if kind == 'op':
                        self.op(*args)
                    else:
                        self.dma(*args)

    def op(self, eng, fn, reads=(), writes=()):
        if getattr(self, '_cap', None) is not None:
            self._cap.append(('op', (eng, fn, reads, writes)))
            return
        writes = list(writes) + [r for r in reads if r.startswith('PB')]
        reads = [r for r in reads if not r.startswith('PB')]
        waits = self._collect(eng, reads, writes)
        k = ('e', eng)
        self.cnt[k] = self.cnt.get(k, 0) + 1
        self.ops[eng].append((waits, fn, (k, 1)))
        self._update(reads, writes, (k, self.cnt[k]))

    def dma(self, q, out, in_, reads=(), writes=(), sem=None):
        assert sem is not None
        if getattr(self, '_cap', None) is not None:
            self._cap.append(('dma', (q, out, in_, reads, writes, sem)))
            return
        waits = self._collect(q, reads, writes)
        k = ('d', sem)
        self.cnt[k] = self.cnt.get(k, 0) + 16
        self.ops[q].append((waits, lambda e: e.dma_start(out=out, in_=in_), (k, 16)))
        self._update(reads, writes, (k, self.cnt[k]))

    def barrier(self):
        for e in self.ENG:
            waits = []
            for k, v in self.cnt.items():
                if k == ('e', e):
                    continue
                if self.seen[e].get(k, 0) >= v:
                    continue
                self.seen[e][k] = v
                waits.append((k, v))
            if waits:
                self.ops[e].append((waits, None, None))

    def mm(self, out, lhsT, rhs, start=True, stop=True, r=(), w=()):
        self.op('pe', lambda e: e.matmul(out, lhsT, rhs, start=start, stop=stop), r, w)

    def tr(self, out, in_, ident, r=(), w=()):
        self.op('pe', lambda e: e.transpose(out, in_, ident), r, w)

    def act(self, out, in_, func, r=(), w=(), **kw):
        self.op('act', lambda e: e.activation(out, in_, func, **kw), r, w)

    def tt(self, eng, out, a, b, op, r=(), w=()):
        self.op(eng, lambda e: e.tensor_tensor(out, a, b, op=op), r, w)

    def ts(self, eng, out, a, s1, s2, op0, op1, r=(), w=()):
        if op1 is None:
            self.op(eng, lambda e: e.tensor_scalar(out, a, s1, None, op0=op0), r, w)
        else:
            self.op(eng, lambda e: e.tensor_scalar(out, a, s1, s2, op0=op0, op1=op1), r, w)

    def stt(self, out, a, s, b, op0, op1, r=(), w=()):
        self.op('dve', lambda e: e.scalar_tensor_tensor(out, a, s, b, op0=op0, op1=op1), r, w)

    def cp(self, eng, out, in_, r=(), w=()):
        if eng == 'act':
            self.op('act', lambda e: e.activation(out, in_, AF.Copy), r, w)
        else:
            self.op(eng, lambda e: e.tensor_copy(out, in_), r, w)

    def red(self, out, in_, r=(), w=()):
        self.op('dve', lambda e: e.tensor_reduce(out, in_, axis=AX.X, op=ALU.add), r, w)

    def recip(self, out, in_, r=(), w=()):
        self.op('dve', lambda e: e.reciprocal(out, in_), r, w)

    def ms(self, eng, ap, val, w=()):
        self.op(eng, lambda e: e.memset(ap, val), (), w)

    def finalize(self):
        nc = self.nc
        sems = {}
        for i, k in enumerate(self.cnt):
            sems[k] = self.stack.enter_context(nc.semaphore(f"s{i}"))
        final_waits = [(k, v) for k, v in self.cnt.items()]
        self.ops['sp'].append((final_waits, None, None))
        engmap = {'pe': 'tensor', 'act': 'scalar', 'dve': 'vector', 'pool': 'gpsimd', 'sp': 'sync'}
        with nc.Block() as block:
            for e in self.ENG:
                ops = self.ops[e]

                def body(engine, ops=ops):
                    for waits, fn, inc in ops:
                        for k, v in waits:
                            engine.wait_ge(sems[k], v)
                        if fn is not None:
                            ins = fn(engine)
                            ins.then_inc(sems[inc[0]], inc[1])
                getattr(block, engmap[e])(body)
        self.stack.close()


class Arena:
    def __init__(self, t, nbytes):
        self.t = t
        self.n = nbytes
        self.off = 0

    def reset(self):
        self.off = 0

    def alloc(self, free_shape, dtype, parts=128):
        n = int(np.prod(free_shape))
        sz = 4 if dtype == F32 else 2
        self.off = (self.off + 63) // 64 * 64
        nb = n * sz
        assert self.off + nb <= self.n, (self.off, nb, self.n)
        v = self.t[:, self.off // 2:(self.off + nb) // 2]
        self.off += nb
        if dtype == F32:
            v = v.bitcast(F32)
        if len(free_shape) == 2:
            v = v.rearrange("p (a b) -> p a b", a=free_shape[0])
        elif len(free_shape) == 3:
            v = v.rearrange("p (a b c) -> p a b c", a=free_shape[0], b=free_shape[1])
        if parts != 128:
            v = v[0:parts]
        return v


V_C, V_CC = 0, 8
V_L = [dict(bada=16, g=40), dict(bada=60, g=84)]
V_CB, V_LNG, V_LNB = 48, 52, 56
NV = 92


def na_patterns():
    rows = 32
    pats = {}
    plist = []
    tiles = []
    for i in range(16):
        js = []
        for j in range(16):
            key = []
            anyv = False
            for qr in range(2):
                r = 2 * i + qr
                rs = min(max(r - 4, 0), rows - 8)
                for kr in range(2):
                    R = 2 * j + kr
                    ok = rs <= R < rs + 8
                    anyv |= ok
                    key.append((ok, R - r + 7 if ok else -1))
            if not anyv:
                continue
            key = tuple(key)
            if key not in pats:
                pats[key] = len(plist)
                plist.append(key)
            js.append((j, pats[key]))
        tiles.append(js)
    order = [8, 7, 4, 0, 1, 2, 3, 5, 4, 0, 1, 6]
    tiles2 = []
    for js in tiles:
        seq = [pt for _, pt in js]
        s0 = None
        for st in range(len(order) - len(seq) + 1):
            if order[st:st + len(seq)] == seq:
                s0 = st
                break
        assert s0 is not None, seq
        tiles2.append([(j, s0 + n) for n, (j, _) in enumerate(js)])
    return [plist[o] for o in order], tiles2


def build(debug=False, nphase=10):
    nc = bass.Bass("TRN2", target_bir_lowering=False)

    def din(name, shape, dt=F32):
        return nc.dram_tensor(name, list(shape), dt, kind="ExternalInput").ap()

    plist, na_tiles = na_patterns()
    NP = len(plist)

    x = din("x", [S, D])
    ctx = din("ctx", [LC, D])
    vecs = din("vecs", [NV, 128])
    ident_d = din("ident", [128, 128])
    rope_d = din("rope", [128, NT * 128])
    gqk_d = din("gqk", [2, 64])
    conv_w_d = din("conv_w", [31, 512])
    w_ada_d = [din("w_ada0", [D, 3072]), din("w_ada1", [D, 3072])]
    b_gate_d = [din("b_gate0", [D]), din("b_gate1", [D])]
    w_in_d = [din("w_in0", [D, AB_IN]), din("w_in1", [D, CD_IN])]
    w_out_d = [din("w_out0", [D, D]), din("w_out1", [D, D])]
    w_four_d = din("w_four", [4, 128, 128])
    ccsc_d = din("ccsc", [128, 256])
    dft_d = din("dft", [2, 4, 128, 16 * 512], BF16)
    nyq_d = din("nyq", [128, 2])
    bm_d = din("bm", [128, 4, 2 * NP * 128])
    fg_d = din("final_g", [D])
    xs1 = nc.dram_tensor("xs1", [T, D], F32, kind="ExternalOutput" if debug else "Internal").ap()
    out_d = nc.dram_tensor("out", [S, D], F32, kind="ExternalOutput").ap()

    p = Prog(nc)
    HT = p.sb([128, KC, T], BF16, "HT")
    YT = p.sb([128, KC, T], BF16, "YT")
    W0 = p.sb([128, KC, 1024], BF16, "W0")
    W1 = p.sb([128, KC, 512], BF16, "W1")
    GATE = p.sb([128, 2, D], F32, "GATE")
    XT = [p.sb([128, D], F32, "XT0"), p.sb([128, D], F32, "XT1")]
    VT = p.sb([128, NV], F32, "VT")
    MOD = p.sb([128, 4, KC], F32, "MOD")
    ident32 = p.sb([128, 128], F32, "ident32")
    identb = p.sb([128, 128], BF16, "identb")
    ONESM = p.sb([128, 128], BF16, "ONESM")
    NHB = p.sb([128, 512], BF16, "NHB")
    CW = p.sb([128, 4, 31], F32, "CW")
    GQK = p.sb([128, 2, 64], F32, "GQK")
    SS = p.sb([128, NT], F32, "SS")
    MSQ = p.sb([128, NT], F32, "MSQ")
    RS = p.sb([128, NT], F32, "RS")
    SC32 = p.sb([128, 16], F32, "SC32")
    SCB = p.sb([128, KC, 2], BF16, "SCB")
    CCSC = p.sb([128, 256], BF16, "CCSC")
    WC = p.sb([128, 4, 128], BF16, "WC")
    AR_BYTES = 88 * 1024
    ARt = p.sb([128, AR_BYTES // 2], BF16, "ARENA")
    AR = Arena(ARt, AR_BYTES)
    PSt = [p.ps([128, 1024], F32, f"PS{i}") for i in range(4)]

    def bank(i):
        return PSt[i // 2][:, (i % 2) * 512:(i % 2 + 1) * 512]

    def bankb(i):
        return bank(i).bitcast(BF16)

    PB = [f"PB{i}" for i in range(8)]

    p.dma('sp', ident32[:], ident_d, writes=['ident32'], sem='ident32')
    p.cp('dve', identb[:], ident32[:], r=['ident32'], w=['identb'])
    p.ms('dve', ONESM[:], 1.0 / 512.0, w=['ONESM'])
    p.ms('pool', NHB[:], -0.5, w=['NHB'])
    p.dma('sp', GQK[:, 0, :], gqk_d[0].partition_broadcast(128), writes=['GQK0'], sem='GQK0')
    p.dma('sp', GQK[:, 1, :], gqk_d[1].partition_broadcast(128), writes=['GQK1'], sem='GQK1')
    p.dma('pool', CCSC[:], ccsc_d, writes=['CCSC'], sem='CCSC')
    p.dma('pool', WC[:], w_four_d.rearrange("g m e -> m g e"), writes=['WC'], sem='WC')
    AR.reset()
    VL = AR.alloc([128], F32)
    CWL = AR.alloc([512], F32)
    p.dma('sp', VL[0:NV, :], vecs, writes=['VL'], sem='VL')
    p.dma('sp', CWL[0:31, :], conv_w_d, writes=['CWL'], sem='CWL')
    p.tr(bank(0)[:, 0:NV], VL[0:NV, :], ident32[0:NV, 0:NV], r=['VL', 'ident32'], w=[PB[0]])
    p.cp('dve', VT[:], bank(0)[:, 0:NV], r=[PB[0]], w=['VT'])
    for cc in range(4):
        p.tr(bank(1)[:, cc * 32:cc * 32 + 31], CWL[0:31, cc * 128:(cc + 1) * 128], ident32[0:31, 0:31],
             r=['CWL', 'ident32'], w=[PB[1]])
    p.cp('dve', CW[:], bank(1)[:, 0:128].rearrange("p (c k) -> p c k", c=4)[:, :, 0:31], r=[PB[1]], w=['CW'])
    p.act(SC32[:], VT[:, 0:16], AF.Silu, r=['VT'], w=['SC32'])
    p.cp('dve', SCB[:, :, 0], SC32[:, 0:8], r=['SC32'], w=['SCB'])
    p.cp('dve', SCB[:, :, 1], SC32[:, 8:16], r=['SC32'], w=['SCB'])

    def wload(Wt, wkey, src, c0, n, d0=0):
        p.dma('pool', Wt[:, :, d0:d0 + n], src.rearrange("(k q) n -> q k n", q=128)[:, :, c0:c0 + n],
              writes=[wkey], sem=wkey)

    def adaln(l):
        p.barrier()
        AR.reset()
        BG = AR.alloc([D], F32)
        SCR = AR.alloc([16, 128], BF16)
        p.cp('dve', SCR, SC32[:, 0:16].unsqueeze(2).broadcast_to([128, 16, 128]), r=['SC32'], w=['SCR'])
        p.dma('sp', BG, b_gate_d[l].partition_broadcast(128), writes=['BG'], sem='BG')
        Ws = [(W0, 'W0'), (W1, 'W1')]
        vb = V_L[l]['bada']
        vg = V_L[l]['g']
        for blk in range(4):
            Wt, wk = Ws[blk % 2]
            wload(Wt, wk, w_ada_d[l], blk * 512, 512)
            for jj in range(4):
                j = blk * 4 + jj
                for k in range(KC):
                    p.mm(bank(7)[:, j * 2:j * 2 + 2], Wt[:, k, jj * 128:(jj + 1) * 128], SCB[:, k, :],
                         start=(k == 0), stop=(k == KC - 1), r=[wk, 'SCB'], w=[PB[7]])
        pm = bank(7)[:, 0:32].rearrange("p (j w) -> p j w", w=2)
        for wsel in range(2):
            p.tt('dve', MOD[:, 2 * wsel, :], pm[:, 0:8, wsel], VT[:, vb:vb + 8], ALU.add, r=[PB[7], 'VT'], w=['MOD'])
            p.tt('dve', MOD[:, 2 * wsel + 1, :], pm[:, 8:16, wsel], VT[:, vb + 8:vb + 16], ALU.add, r=[PB[7], 'VT'], w=['MOD'])
            p.stt(MOD[:, 2 * wsel + 1, :], MOD[:, 2 * wsel + 1, :], 1.0, VT[:, vg:vg + 8], ALU.add, ALU.mult,
                  r=['MOD', 'VT'], w=['MOD'])

        def gate_part():
            for blk in range(4, 6):
                Wt, wk = Ws[blk % 2]
                wload(Wt, wk, w_ada_d[l], blk * 512, 512)
                half = blk - 4
                for wsel in range(2):
                    bi = 5 + wsel
                    for k in range(KC):
                        p.mm(bank(bi), SCR[:, wsel * 8 + k, :], Wt[:, k, 0:512], start=(k == 0), stop=(k == KC - 1),
                             r=[wk, 'SCR'], w=[PB[bi]])
                    p.tt('dve', GATE[:, wsel, half * 512:(half + 1) * 512], bank(bi), BG[:, half * 512:(half + 1) * 512],
                         ALU.add, r=[PB[bi], 'BG'], w=['GATE'])

        p.emit_rr([p.capture(gate_part), p.capture(lambda: phase_norm(l, reset=False))])

    def xsrc(l, t):
        if l == 0:
            return x[t * 128:(t + 1) * 128, :] if t < NTL else ctx[(t - NTL) * 128:(t - NTL + 1) * 128, :]
        return xs1[t * 128:(t + 1) * 128, :]

    def phase_norm(l, reset=True):
        if reset:
            AR.reset()
        XTs = [XT[0], XT[1], AR.alloc([D], F32)]
        XH = [AR.alloc([D], F32), AR.alloc([D], F32), AR.alloc([D], F32)]
        JUNK = AR.alloc([D], BF16)

        def xta(t):
            return XTs[t % 3][:] if t % 3 < 2 else XTs[2]

        def stA1(t):
            xk = f"XT{t % 3}"
            p.dma('sp', xta(t), xsrc(l, t), reads=['xs1'] if l else [], writes=[xk], sem=xk)
            p.act(JUNK, xta(t), AF.Square, r=[xk], w=['JUNK', f'SS{t}'], accum_out=SS[:, t:t + 1])

        def stA2(t):
            p.ts('dve', MSQ[:, t:t + 1], SS[:, t:t + 1], 1.0 / D, EPS, ALU.mult, ALU.add, r=[f'SS{t}'], w=[f'MSQ{t}'])
            p.tt('pool', RS[:, t:t + 1], MSQ[:, t:t + 1], NHB[:, 0:1], ALU.pow, r=[f'MSQ{t}', 'NHB'], w=[f'RS{t}'])

        def stA3(t):
            p.ts('dve', XH[t % 3], xta(t), RS[:, t:t + 1], None, ALU.mult, None, r=[f"XT{t % 3}", f'RS{t}'], w=[f"XH{t % 3}"])

        def stB(t):
            xh = XH[t % 3]
            hk = f"XH{t % 3}"
            wsel = 0 if t < NTL else 1
            b0 = (t % 2) * 2
            for c in range(KC):
                bi = b0 + c // 4
                p.tr(bank(bi)[:, (c % 4) * 128:(c % 4 + 1) * 128], xh[:, c * 128:(c + 1) * 128], ident32[:],
                     r=[hk, 'ident32'], w=[PB[bi]])
            for c in range(KC):
                bi = b0 + c // 4
                src = bank(bi)[:, (c % 4) * 128:(c % 4 + 1) * 128]
                dst = HT[:, c, t * 128:(t + 1) * 128]
                if c < 4:
                    p.act(dst, src, AF.Identity, r=[PB[bi], 'MOD'], w=[f'HT{t}a'],
                          scale=MOD[:, 2 * wsel + 1, c:c + 1], bias=MOD[:, 2 * wsel, c:c + 1])
                else:
                    p.ts('dve', dst, src, MOD[:, 2 * wsel + 1, c:c + 1], MOD[:, 2 * wsel, c:c + 1], ALU.mult, ALU.add,
                         r=[PB[bi], 'MOD'], w=[f'HT{t}b'])

        for t0 in range(3):
            stA1(t0)
        stA2(0)
        stA3(0)
        stA2(1)
        stA3(1)
        for t in range(NT):
            if t + 3 < NT:
                stA1(t + 3)
            if t + 2 < NT:
                stA2(t + 2)
            stB(t)
            if t + 2 < NT:
                stA3(t + 2)

    BLOCKS = [(0, 512), (512, 512), (1024, 512), (1536, 512), (2048, 256)]

    def ht_keys(b0, bn):
        return [f'HT{t}{s_}' for t in range(b0 // 128, (b0 + bn) // 128) for s_ in 'ab']

    def phase_conv():
        p.barrier()
        AR.reset()
        NPE = 21
        GLUs = [AR.alloc([2349], F32), AR.alloc([2349], F32)]
        GLBs = [AR.alloc([2349], BF16), AR.alloc([2349], BF16)]
        ACCs = [AR.alloc([2319], F32), AR.alloc([2319], F32)]
        DGs = [AR.alloc([NPE, 128], BF16), AR.alloc([NPE, 128], BF16)]
        SIGs = [AR.alloc([512], F32), AR.alloc([512], F32)]
        AVs = [AR.alloc([512], F32), AR.alloc([512], F32)]
        wload(W0, 'W0', w_in_d[0], 0, 1024)
        wload(W1, 'W1', w_in_d[0], 1792, 512)
        for i in range(2):
            p.ms('dve', GLUs[i], 0.0, w=[f'GLU{i}'])
            p.ms('pool', GLBs[i], 0.0, w=[f'GLB{i}'])
        OB5 = [(0, 512), (512, 512), (1024, 512), (1536, 512), (2048, 271)]
        nblk = 0

        def castout(c_):
            p.cp('act', YT[:, c_, 0:S], ACCs[c_ % 2][:, 0:S], r=[f'ACC{c_ % 2}'], w=[f'YT{c_}'])
            p.cp('act', YT[:, c_, S:T], ACCs[c_ % 2][:, 2063:2319], r=[f'ACC{c_ % 2}'], w=[f'YT{c_}'])

        for cc in range(4):
            GLU, GLB, DG, ACC = GLUs[cc % 2], GLBs[cc % 2], DGs[cc % 2], ACCs[cc % 2]
            gk, bk, dk, ak = f'GLU{cc % 2}', f'GLB{cc % 2}', f'DG{cc % 2}', f'ACC{cc % 2}'
            p.tt('pool', DG, identb[:].unsqueeze(1).broadcast_to([128, NPE, 128]),
                 CW[:, cc, 0:NPE].unsqueeze(2).broadcast_to([128, NPE, 128]), ALU.mult, r=['identb', 'CW'], w=[dk])
            for bi_, (b0, bn) in enumerate(BLOCKS):
                pv, pg = 2 * (bi_ % 2), 2 * (bi_ % 2) + 1
                SIG, AV = SIGs[nblk % 2], AVs[nblk % 2]
                sgk, avk = f'SIG{nblk % 2}', f'AV{nblk % 2}'
                nblk += 1
                hk = ht_keys(b0, bn)
                for k in range(KC):
                    p.mm(bank(pv)[:, 0:bn], W0[:, k, cc * 128:(cc + 1) * 128], HT[:, k, b0:b0 + bn],
                         start=(k == 0), stop=(k == KC - 1), r=['W0'] + hk, w=[PB[pv]])
                for k in range(KC):
                    p.mm(bank(pg)[:, 0:bn], W0[:, k, 512 + cc * 128:512 + (cc + 1) * 128], HT[:, k, b0:b0 + bn],
                         start=(k == 0), stop=(k == KC - 1), r=['W0'] + hk, w=[PB[pg]])
                p.cp('act', AV[:, 0:bn], bank(pv)[:, 0:bn], r=[PB[pv]], w=[avk])
                p.act(SIG[:, 0:bn], bank(pg)[:, 0:bn], AF.Sigmoid, r=[PB[pg]], w=[sgk])
                pos = 15 + b0 if b0 < S else 2078
                p.tt('pool', GLU[:, pos:pos + bn], AV[:, 0:bn], SIG[:, 0:bn], ALU.mult, r=[avk, sgk], w=[gk])
                p.cp('act', GLB[:, pos:pos + bn], GLU[:, pos:pos + bn], r=[gk], w=[bk])
            for bi_, (p0, pn) in enumerate(OB5):
                cb = 4 + bi_ % 4
                for k in range(NPE):
                    p.mm(bank(cb)[:, 0:pn], DG[:, k, :], GLB[:, p0 + k:p0 + k + pn], start=(k == 0), stop=(k == NPE - 1),
                         r=[dk, bk], w=[PB[cb]])
                p.act(ACC[:, p0:p0 + pn], bank(cb)[:, 0:pn], AF.Identity, r=[PB[cb], 'VT'], w=[ak],
                      bias=VT[:, V_CB + cc:V_CB + cc + 1])
            if cc > 0:
                castout(cc - 1)
            for k in range(NPE, 31):
                p.stt(ACC, GLU[:, k:k + 2319], CW[:, cc, k:k + 1], ACC, ALU.mult, ALU.add, r=[gk, ak, 'CW'], w=[ak])
        castout(3)

    def phase_attn():
        p.barrier()
        AR.reset()
        KT2 = AR.alloc([2, T], BF16)
        VA3 = AR.alloc([NT, 2, 192], BF16)
        ROPE = AR.alloc([NT, 128], F32)
        QAB = AR.alloc([4, 2, 512], BF16)
        QN = AR.alloc([512], F32)
        TA = AR.alloc([8, 32], F32)
        TB = AR.alloc([8, 32], F32)
        QR = AR.alloc([512], BF16)
        SSQ = AR.alloc([16], F32)
        TMPS = [(QN, TA, TB, SSQ), (AR.alloc([512], F32), AR.alloc([8, 32], F32), AR.alloc([8, 32], F32), AR.alloc([16], F32))]
        QR1 = AR.alloc([512], BF16)
        KRs = [AR.alloc([4, 2, 128], BF16), AR.alloc([4, 2, 128], BF16)]
        arena_mark = AR.off
        p.dma('sp', ROPE, rope_d.rearrange("p (t c) -> p t c", t=NT), writes=['ROPE'], sem='ROPE')
        wload(W0, 'W0', w_in_d[0], 1024, 768)
        p.ms('dve', VA3, 1.0, w=['VA'])
        p.ms('pool', QAB, 0.0, w=['QAB0'])

        def normrope(t0, nt, hpt, src3, pbk, dst4, dkey, gsel, cq, sq_, ts_=0):
            nh = nt * hpt
            QN, TA, TB, SSQ = TMPS[ts_]
            kq, ka, kb_, ks = [f"{n_}{ts_}" for n_ in ("QN", "TA", "TB", "SSQ")]
            n = nh * 64
            p.act(QN[:, 0:n].rearrange("p (t c) -> p t c", t=nt), src3, AF.Square, r=[pbk], w=[kq])
            p.red(SSQ[:, 0:nh], QN[:, 0:n].rearrange("p (h d) -> p h d", h=nh), r=[kq], w=[ks])
            p.ts('dve', SSQ[:, 0:nh], SSQ[:, 0:nh], 1.0 / 64, EPS, ALU.mult, ALU.add, r=[ks], w=[ks])
            p.tt('pool', SSQ[:, 0:nh], SSQ[:, 0:nh], NHB[:, 0:nh], ALU.pow, r=[ks, 'NHB'], w=[ks])
            qn4 = QN[:, 0:n].rearrange("p (t h d) -> p t h d", t=nt, h=hpt)
            p.tt('dve', qn4, src3.rearrange("p t (h d) -> p t h d", h=hpt),
                 SSQ[:, 0:nh].rearrange("p (t h) -> p t h", t=nt).unsqueeze(3).broadcast_to([128, nt, hpt, 64]),
                 ALU.mult, r=[pbk, ks], w=[kq])
            qn3 = QN[:, 0:n].rearrange("p (h d) -> p h d", h=nh)
            p.tt('dve', qn3, qn3, GQK[:, gsel, :].unsqueeze(1).broadcast_to([128, nh, 64]), ALU.mult,
                 r=[kq, f'GQK{gsel}'], w=[kq])
            q5 = QN[:, 0:n].rearrange("p (t h d two) -> p t h d two", t=nt, h=hpt, two=2)
            A, B = q5[:, :, :, :, 0], q5[:, :, :, :, 1]
            cosb = ROPE[:, t0:t0 + nt, cq:cq + 32].unsqueeze(2).broadcast_to([128, nt, hpt, 32])
            sinb = ROPE[:, t0:t0 + nt, sq_:sq_ + 32].unsqueeze(2).broadcast_to([128, nt, hpt, 32])
            ta = TA[:, 0:nh, :].rearrange("p (t h) d -> p t h d", t=nt)
            tb = TB[:, 0:nh, :].rearrange("p (t h) d -> p t h d", t=nt)
            p.tt('dve', ta, A, cosb, ALU.mult, r=[kq, 'ROPE'], w=[ka])
            p.tt('dve', tb, B, sinb, ALU.mult, r=[kq, 'ROPE'], w=[kb_])
            p.tt('dve', dst4[0], ta, tb, ALU.subtract, r=[ka, kb_], w=[dkey])
            p.tt('dve', ta, A, sinb, ALU.mult, r=[kq, 'ROPE'], w=[ka])
            p.tt('dve', tb, B, cosb, ALU.mult, r=[kq, 'ROPE'], w=[kb_])
            p.tt('dve', dst4[1], ta, tb, ALU.add, r=[ka, kb_], w=[dkey])

        QABs = [QAB, None]
        QRs = [QR, QR1]

        def prepA1(t):
            for k in range(KC):
                p.mm(bank(6), HT[:, k, t * 128:(t + 1) * 128], W0[:, k, 0:512],
                     start=(k == 0), stop=(k == KC - 1), r=['W0', f'HT{t}a', f'HT{t}b'], w=[PB[6]])

        def prepA2(t):
            qd = QRs[t % 2].rearrange("p (t h d two) -> p t h d two", t=1, h=8, two=2)
            normrope(t, 1, 8, bank(6).rearrange("p (t c) -> p t c", t=1), PB[6], (qd[:, :, :, :, 0], qd[:, :, :, :, 1]),
                     f'QR{t % 2}', 0, 0, 32, ts_=1)

        def prepA(t):
            prepA1(t)
            prepA2(t)

        def prepB1(t):
            ptb = bankb(7)
            for pr in range(4):
                p.tr(ptb[:, pr * 128:(pr + 1) * 128], QRs[t % 2][:, pr * 128:(pr + 1) * 128], identb[:],
                     r=[f'QR{t % 2}', 'identb'], w=[PB[7]])

        def prepB2(t, qs, bsel):
            qab = QABs[bsel]
            src3 = bankb(7)[:, 0:512].rearrange("p (a n) -> p a n", a=4)
            p.cp('dve', qab[0:64, :, 0, qs * 128:(qs + 1) * 128], src3[0:64], r=[PB[7]], w=[f'QAB{bsel}'])
            p.cp('dve', qab[64:128, :, 1, qs * 128:(qs + 1) * 128], src3[64:128], r=[PB[7]], w=[f'QAB{bsel}'])

        def prepB(t, qs, bsel):
            prepB1(t)
            prepB2(t, qs, bsel)

        KB = [(0, 4), (4, 4), (8, 4), (12, 4), (16, 2)]

        def kvA(bi_):
            t0, nt = KB[bi_]
            pst = PSt[2]
            pk = [PB[4], PB[5]]
            kr = KRs[bi_ % 2]
            kk = f'KR{bi_ % 2}'
            for j in range(nt):
                t = t0 + j
                for k in range(KC):
                    p.mm(pst[:, j * 256:(j + 1) * 256], HT[:, k, t * 128:(t + 1) * 128], W0[:, k, 512:768],
                         start=(k == 0), stop=(k == KC - 1), r=['W0', f'HT{t}a', f'HT{t}b'], w=pk)
            v3 = pst[:, 0:nt * 256].rearrange("p (t c) -> p t c", c=256)
            kd = kr[:, 0:nt, :, 0:64].rearrange("p t h (d two) -> p t h d two", two=2)
            normrope(t0, nt, 2, v3[:, :, 0:128], pk[0], (kd[:, :, :, :, 0], kd[:, :, :, :, 1]), kk, 1, 64, 96)
            for kv in range(2):
                vsrc = v3[:, :, 128 + kv * 64:128 + (kv + 1) * 64]
                p.cp('act', VA3[:, t0:t0 + nt, kv, 0:64], vsrc, r=pk, w=['VA'])
                p.cp('act', VA3[:, t0:t0 + nt, kv, 128:192], vsrc, r=pk, w=['VA'])
            p.cp('act', kr[:, 0:nt, :, 64:128], kr[:, 0:nt, :, 0:64], r=[kk], w=[kk])

        def kvB(bi_):
            t0, nt = KB[bi_]
            kr = KRs[bi_ % 2]
            kk = f'KR{bi_ % 2}'
            tb_ = 7
            ptb = bankb(tb_)
            for h in range(2):
                for j in range(nt):
                    sl = h * nt + j
                    p.tr(ptb[:, sl * 128:(sl + 1) * 128], kr[:, j, h, :], identb[:], r=[kk, 'identb'], w=[PB[tb_]])
            p.cp('act', KT2[:, :, t0 * 128:(t0 + nt) * 128], ptb[:, 0:2 * nt * 128].rearrange("p (h n) -> p h n", h=2),
                 r=[PB[tb_]], w=['KT'])

        SZs = [AR.alloc([4, 512], BF16), AR.alloc([4, 512], BF16)]
        M2 = AR.alloc([512], F32)
        VEs = [AR.alloc([512], F32), AR.alloc([512], F32)]
        MNs = [AR.alloc([512], F32), AR.alloc([512], F32)]
        T1s = [AR.alloc([512], F32), AR.alloc([512], F32)]
        SQs = [AR.alloc([4, 512], BF16), AR.alloc([4, 512], BF16)]
        ytk = [f'YT{cc}' for cc in range(4)]

        def lnS(bi_):
            b0, bn = BLOCKS[bi_]
            SQ, VE, MN = SQs[bi_ % 2], VEs[bi_ % 2], MNs[bi_ % 2]
            sqk, vek, mnk = f'SQ{bi_ % 2}', f'VE{bi_ % 2}', f'MN{bi_ % 2}'
            for cc in range(4):
                p.act(SQ[:, cc, 0:bn], YT[:, cc, b0:b0 + bn], AF.Square, r=ytk, w=[sqk])
            for cc in range(4):
                p.mm(bank(0)[:, 0:bn], ONESM[:], YT[:, cc, b0:b0 + bn], start=(cc == 0), stop=(cc == 3),
                     r=ytk + ['ONESM'], w=[PB[0]])
            for cc in range(4):
                p.mm(bank(1)[:, 0:bn], ONESM[:], SQ[:, cc, 0:bn], start=(cc == 0), stop=(cc == 3), r=[sqk, 'ONESM'], w=[PB[1]])
            p.act(M2[:, 0:bn], bank(0)[:, 0:bn], AF.Square, r=[PB[0]], w=['M2'])
            p.cp('act', MN[:, 0:bn], bank(0)[:, 0:bn], r=[PB[0]], w=[mnk])
            p.stt(VE[:, 0:bn], bank(1)[:, 0:bn], EPS, M2[:, 0:bn], ALU.add, ALU.subtract, r=[PB[1], 'M2'], w=[vek])
            p.act(VE[:, 0:bn], VE[:, 0:bn], AF.Sqrt, r=[vek], w=[vek])
            p.recip(VE[:, 0:bn], VE[:, 0:bn], r=[vek], w=[vek])

        def lnZ(bi_):
            b0, bn = BLOCKS[bi_]
            hk = ht_keys(b0, bn)
            SZ = SZs[bi_ % 2]
            for cc in range(4):
                pz = 2 + (bi_ * 4 + cc) % 2
                for k in range(KC):
                    p.mm(bank(pz)[:, 0:bn], W1[:, k, cc * 128:(cc + 1) * 128], HT[:, k, b0:b0 + bn],
                         start=(k == 0), stop=(k == KC - 1), r=['W1'] + hk, w=[PB[pz]])
                p.act(SZ[:, cc, 0:bn], bank(pz)[:, 0:bn], AF.Silu, r=[PB[pz]], w=[f'SZ{bi_ % 2}_{cc}'])

        def lnP(bi_):
            b0, bn = BLOCKS[bi_]
            VE, MN, SZ = VEs[bi_ % 2], MNs[bi_ % 2], SZs[bi_ % 2]
            vek, mnk = f'VE{bi_ % 2}', f'MN{bi_ % 2}'
            for cc in range(4):
                T1 = T1s[cc % 2]
                tk = f'T1{cc % 2}'
                p.tt('dve', T1[:, 0:bn], YT[:, cc, b0:b0 + bn], MN[:, 0:bn], ALU.subtract, r=ytk + [mnk], w=[tk])
                p.tt('dve', T1[:, 0:bn], T1[:, 0:bn], VE[:, 0:bn], ALU.mult, r=[tk, vek], w=[tk])
                p.act(T1[:, 0:bn], T1[:, 0:bn], AF.Silu, r=[tk, 'VT'], w=[tk],
                      scale=VT[:, V_LNG + cc:V_LNG + cc + 1], bias=VT[:, V_LNB + cc:V_LNB + cc + 1])
                p.tt('dve', YT[:, cc, b0:b0 + bn], T1[:, 0:bn], SZ[:, cc, 0:bn], ALU.mult,
                     r=[tk, f'SZ{bi_ % 2}_{cc}'], w=[f'YTL{cc}_{bi_}'])

        lnS(0)
        lnZ(0)
        kvA(0)
        for bi_ in range(len(KB)):
            chains = []
            if bi_ >= 1:
                chains.append(p.capture(lambda: (kvB(bi_ - 1), prepB(bi_ - 2, bi_ - 2, 0) if bi_ >= 2 else None)))
            if bi_ + 1 < len(KB):
                chains.append(p.capture(lambda: (lnS(bi_ + 1), lnZ(bi_ + 1))))
                chains.append(p.capture(lambda: kvA(bi_ + 1)))
            if bi_ < 4:
                chains.append(p.capture(lambda: prepA(bi_)))
            chains.append(p.capture(lambda: lnP(bi_)))
            p.emit_rr(chains)
        kvB(4)
        prepB(2, 2, 0)
        prepB(3, 3, 0)
        p.barrier()
        AR.off = arena_mark
        PTs = [AR.alloc([2, 512], BF16) for _ in range(3)]
        SZ = AR.alloc([512], F32)
        RR = AR.alloc([512], F32)
        OA = AR.alloc([512], F32)
        OB = AR.alloc([512], F32)
        QABs[1] = AR.alloc([4, 2, 512], BF16)
        p.ms('pool', QABs[1], 0.0, w=['QAB1'])
        wload(W1, 'W1', w_in_d[0], 1792 + 512, 512)
        for bidx, (q0, qn) in enumerate(BLOCKS):
            nsub = qn // 128
            keytiles = list(range(NT)) if q0 < S else [16, 17]
            nk = len(keytiles)
            QAB = QABs[bidx % 2]
            qabk = f'QAB{bidx % 2}'
            sched = {}
            if bidx + 1 < len(BLOCKS):
                nq0, nqn = BLOCKS[bidx + 1]
                for qs in range(nqn // 128):
                    sched.setdefault(2 + 16 * qs, []).append(('A1', nq0 // 128 + qs, qs))
                    sched.setdefault(4 + 16 * qs, []).append(('A2', nq0 // 128 + qs, qs))
                    sched.setdefault(12 + 16 * qs, []).append(('B1', nq0 // 128 + qs, qs))
                    sched.setdefault(14 + 16 * qs, []).append(('B2', nq0 // 128 + qs, qs))
            steps = [(pr, idx, kt) for pr in range(4) for idx, kt in enumerate(keytiles)]

            def qk(sn):
                pr, idx, kt = steps[sn]
                st = PSt[sn % 2]
                sk = [PB[2 * (sn % 2)], PB[2 * (sn % 2) + 1]]
                for ab in range(2):
                    p.mm(st[:, ab * 512:ab * 512 + qn], KT2[:, pr // 2, kt * 128:(kt + 1) * 128], QAB[:, pr, ab, 0:qn],
                         r=['KT', qabk], w=sk)

            hk = ht_keys(q0, qn)
            qk(0)
            for sn, (pr, idx, kt) in enumerate(steps):
                kv = pr // 2
                if sn + 1 < len(steps):
                    qk(sn + 1)
                for (kind, pt_, pqs) in sched.get(sn, []):
                    if kind == 'A1':
                        prepA1(pt_)
                    elif kind == 'A2':
                        prepA2(pt_)
                    elif kind == 'B1':
                        prepB1(pt_)
                    else:
                        prepB2(pt_, pqs, (bidx + 1) % 2)
                st = PSt[sn % 2]
                sk = [PB[2 * (sn % 2)], PB[2 * (sn % 2) + 1]]
                pt = PTs[sn % 3]
                ptk = f'PT{sn % 3}'
                p.act(pt[:, :, 0:qn], st[:, :].rearrange("p (a n) -> p a n", a=2)[:, :, 0:qn], AF.Exp, r=sk, w=[ptk])
                p.mm(bank(4)[:, 0:qn], VA3[:, kt, kv, 0:128], pt[:, 0, 0:qn], start=(idx == 0), stop=(idx == nk - 1),
                     r=[ptk, 'VA'], w=[PB[4]])
                p.mm(bank(5)[:, 0:qn], VA3[:, kt, kv, 64:192], pt[:, 1, 0:qn], start=(idx == 0), stop=(idx == nk - 1),
                     r=[ptk, 'VA'], w=[PB[5]])
                if idx == nk - 1:
                    p.cp('act', OA[:, 0:qn], bank(4)[:, 0:qn], r=[PB[4]], w=['OA'])
                    p.cp('act', OB[:, 0:qn], bank(5)[:, 0:qn], r=[PB[5]], w=['OB'])
                    p.recip(RR[0:64, 0:qn], OA[64:128, 0:qn], r=['OA'], w=['RR'])
                    p.recip(RR[64:128, 0:qn], OB[0:64, 0:qn], r=['OB'], w=['RR'])
                    p.tt('dve', YT[0:64, 4 + pr, q0:q0 + qn], OA[0:64, 0:qn], RR[0:64, 0:qn], ALU.mult,
                         r=['OA', 'RR'], w=[f'YT{4 + pr}'])
                    p.tt('dve', YT[64:128, 4 + pr, q0:q0 + qn], OB[64:128, 0:qn], RR[64:128, 0:qn], ALU.mult,
                         r=['OB', 'RR'], w=[f'YT{4 + pr}'])

        SZ2 = [SZ, RR]
        n_ = 0
        for j in range(4):
            for (b0, bn) in BLOCKS:
                hk = ht_keys(b0, bn)
                bi = 6 + n_ % 2
                sz = SZ2[n_ % 2]
                szk = ['SZ', 'RR'][n_ % 2]
                n_ += 1
                for k in range(KC):
                    p.mm(bank(bi)[:, 0:bn], W1[:, k, j * 128:(j + 1) * 128], HT[:, k, b0:b0 + bn],
                         start=(k == 0), stop=(k == KC - 1), r=['W1'] + hk, w=[PB[bi]])
                p.act(sz[:, 0:bn], bank(bi)[:, 0:bn], AF.Silu, r=[PB[bi]], w=[szk])
                p.tt('dve', YT[:, 4 + j, b0:b0 + bn], YT[:, 4 + j, b0:b0 + bn], sz[:, 0:bn], ALU.mult,
                     r=[f'YT{4 + j}', szk], w=[f'YT{4 + j}'])

    def phase_out(l):
        p.barrier()
        AR.reset()
        XN = [AR.alloc([D], F32) for _ in range(4)]
        TMP = AR.alloc([D], F32)
        last = (l == 1)
        if last:
            FG = AR.alloc([D], F32)
            JUNK = AR.alloc([D], BF16)
            p.dma('sp', FG, fg_d.partition_broadcast(128), writes=['FG'], sem='FG')
        wload(W0, 'W0', w_out_d[l], 0, 1024)
        ntile = NTL if last else NT
        ytk = [f'YT{c}' for c in range(KC)]
        def tail(t):
            xn = XN[t % 4]
            nk = f"XN{t % 4}"
            p.ts('dve', MSQ[:, t:t + 1], SS[:, t:t + 1], 1.0 / D, EPS, ALU.mult, ALU.add, r=[f'SS{t}'], w=[f'MSQ{t}'])
            p.tt('pool', RS[:, t:t + 1], MSQ[:, t:t + 1], NHB[:, 0:1], ALU.pow, r=[f'MSQ{t}', 'NHB'], w=[f'RS{t}'])
            p.stt(xn, xn, RS[:, t:t + 1], FG, ALU.mult, ALU.mult, r=[nk, f'RS{t}', 'FG'], w=[nk])
            p.dma('sp', out_d[t * 128:(t + 1) * 128, :], xn, reads=[nk], writes=['out'], sem=f'st{t % 4}')

        XT3 = [XT[0], XT[1], AR.alloc([D], F32)]

        def xload(t):
            p.dma('sp', XT3[t % 3][:] if t % 3 < 2 else XT3[2], xsrc(l, t), reads=['xs1'] if l else [],
                  writes=[f"XT{t % 3}"], sem=f"XT{t % 3}")

        for t in range(ntile):
            xt = XT3[t % 3]
            xk = f"XT{t % 3}"
            xn = XN[t % 4]
            nk = f"XN{t % 4}"
            wsel = 0 if t < NTL else 1
            if t == 0:
                xload(0)
            if t + 1 < ntile:
                xload(t + 1)
            for half in range(2):
                bi = 2 * (t % 2) + half
                for k in range(KC):
                    p.mm(bank(bi), YT[:, k, t * 128:(t + 1) * 128], W0[:, k, half * 512:(half + 1) * 512],
                         start=(k == 0), stop=(k == KC - 1), r=['W0'] + ytk, w=[PB[bi]])
                sl = slice(half * 512, (half + 1) * 512)
                p.tt('dve', TMP[:, sl], bank(bi), GATE[:, wsel, sl], ALU.mult, r=[PB[bi], 'GATE'], w=['TMP'])
                p.tt('dve', xn[:, sl], TMP[:, sl], xt[:, sl], ALU.add, r=['TMP', xk], w=[nk])
            if not last:
                p.dma('sp', xs1[t * 128:(t + 1) * 128, :], xn, reads=[nk], writes=['xs1'], sem=f'st{t % 4}')
            else:
                p.act(JUNK, xn, AF.Square, r=[nk], w=['JUNK', f'SS{t}'], accum_out=SS[:, t:t + 1])
                if t > 1:
                    tail(t - 2)
        if last:
            tail(ntile - 2)
            tail(ntile - 1)

    def phase_fourier():
        p.barrier()
        AR.reset()
        Fm = AR.alloc([NTL, 512], BF16)
        DB = [AR.alloc([8, 512], BF16), AR.alloc([8, 512], BF16)]
        PQ = AR.alloc([2, 4, 512], BF16)
        UB = AR.alloc([512], F32)
        FRa = AR.alloc([512], BF16)
        FRb = AR.alloc([512], BF16)
        SZa = AR.alloc([512], F32)
        SZb = AR.alloc([512], F32)
        NYQ = AR.alloc([2], BF16)
        PN = AR.alloc([4, 2], BF16)
        FRN = AR.alloc([2], BF16)
        SZN = AR.alloc([2], F32)
        wload(W0, 'W0', w_in_d[1], 0, 512)
        wload(W1, 'W1', w_in_d[1], 2048, 512)
        p.dma('pool', NYQ, nyq_d, writes=['NYQ'], sem='NYQ')
        for t in range(NTL):
            bi = t % 2
            for k in range(KC):
                p.mm(bank(bi), HT[:, k, t * 128:(t + 1) * 128], W0[:, k, 0:512],
                     start=(k == 0), stop=(k == KC - 1), r=['W0', f'HT{t}a', f'HT{t}b'], w=[PB[bi]])
            p.cp('act', Fm[:, t, :], bank(bi), r=[PB[bi]], w=['Fm'])

        def zproj(g, t0, n, bi, sz, szk):
            hk = [f'HT{t}{s_}' for t in range(t0 // 128, (t0 + n - 1) // 128 + 1) for s_ in 'ab']
            for k in range(KC):
                p.mm(bank(bi)[:, 0:n], W1[:, k, g * 128:(g + 1) * 128], HT[:, k, t0:t0 + n],
                     start=(k == 0), stop=(k == KC - 1), r=['W1'] + hk, w=[PB[bi]])
            p.act(sz[:, 0:n], bank(bi)[:, 0:n], AF.Silu, r=[PB[bi]], w=[szk])

        nload = 0
        for kb in range(2):
            for trig in range(2):
                for half in range(2):
                    db = DB[nload % 2]
                    dk = f'DB{nload % 2}'
                    nload += 1
                    p.dma('sp', db, dft_d[trig, kb][:, half * 4096:(half + 1) * 4096].rearrange("p (l n) -> p l n", l=8),
                          writes=[dk], sem=dk)
                    for l8 in range(8):
                        lc = half * 8 + l8
                        for g in range(4):
                            p.mm(bank(g), Fm[:, lc, g * 128:(g + 1) * 128], db[:, l8, :],
                                 start=(lc == 0), stop=(lc == 15), r=['Fm', dk], w=[PB[g]])
                for g in range(4):
                    p.cp('act' if g % 2 == 0 else 'dve', PQ[:, trig, g, :], bank(g), r=[PB[g]], w=[f'PQ{g}'])
            b0 = kb * 512
            if kb == 0:
                mt0, mn = 1537, 511
            else:
                mt0, mn = 1025, 512
            for g in range(4):
                p.mm(bank(4), CCSC[:, 0:128], PQ[:, 0, g, :], r=['CCSC', f'PQ{g}'], w=[PB[4]])
                p.mm(bank(5), CCSC[:, 128:256], PQ[:, 1, g, :], r=['CCSC', f'PQ{g}'], w=[PB[5]])
                p.cp('act', UB, bank(4), r=[PB[4]], w=['UB'])
                p.tt('dve', FRa, UB, bank(5), ALU.add, r=['UB', PB[5]], w=['FRa'])
                p.tt('dve', FRb, UB, bank(5), ALU.subtract, r=['UB', PB[5]], w=['FRb'])
                p.mm(bank(6), WC[:, g, :], FRa, r=['WC', 'FRa'], w=[PB[6]])
                p.mm(bank(7), WC[:, g, :], FRb, r=['WC', 'FRb'], w=[PB[7]])
                zproj(g, b0, 512, 4, SZa, 'SZa')
                zproj(g, mt0, mn, 5, SZb, 'SZb')
                p.tt('dve', YT[:, g, b0:b0 + 512], bank(6), SZa, ALU.mult, r=[PB[6], 'SZa'], w=[f'YT{g}'])
                if kb == 0:
                    p.tt('dve', YT[:, g, 2047:1536:-1], bank(7)[:, 1:512], SZb[:, 510::-1], ALU.mult,
                         r=[PB[7], 'SZb'], w=[f'YT{g}'])
                else:
                    p.tt('dve', YT[:, g, 1536:1024:-1], bank(7)[:, 0:512], SZb[:, 511::-1], ALU.mult,
                         r=[PB[7], 'SZb'], w=[f'YT{g}'])
        for g in range(4):
            for lc in range(NTL):
                p.mm(bank(0)[:, g * 2:g * 2 + 2], Fm[:, lc, g * 128:(g + 1) * 128], NYQ, start=(lc == 0), stop=(lc == NTL - 1),
                     r=['Fm', 'NYQ'], w=[PB[0]])
        p.cp('act', PN, bank(0)[:, 0:8].rearrange("p (g n) -> p g n", g=4), r=[PB[0]], w=['PN'])
        for g in range(4):
            p.mm(bank(1)[:, 0:2], CCSC[:, 0:128], PN[:, g, :], r=['CCSC', 'PN'], w=[PB[1]])
            p.cp('act', FRN, bank(1)[:, 0:2], r=[PB[1]], w=['FRN'])
            p.mm(bank(2)[:, 0:2], WC[:, g, :], FRN, r=['WC', 'FRN'], w=[PB[2]])
            zproj(g, 1024, 2, 3, SZN, 'SZN')
            p.tt('dve', YT[:, g, 1024:1025], bank(2)[:, 0:1], SZN[:, 0:1], ALU.mult, r=[PB[2], 'SZN'], w=[f'YT{g}'])

    def phase_na():
        p.barrier()
        AR.reset()
        VA = AR.alloc([NT, 8, 65], BF16)
        KT2 = AR.alloc([T], BF16)
        QAB = AR.alloc([NTL, 2, 128], BF16)
        EBG = AR.alloc([2, NP, 128], BF16)
        PT = [AR.alloc([7, 2, 128], BF16), AR.alloc([7, 2, 128], BF16)]
        YB = AR.alloc([128], BF16)
        RINV = AR.alloc([2], F32)
        SZ = AR.alloc([512], F32)
        wload(W0, 'W0', w_in_d[1], 1536, 512, 0)
        p.ms('dve', VA, 1.0, w=['VA'])
        p.ms('pool', QAB, 0.0, w=['QAB'])
        for t in range(NT):
            bi = 6 + t % 2
            for k in range(KC):
                p.mm(bank(bi), HT[:, k, t * 128:(t + 1) * 128], W0[:, k, 0:512],
                     start=(k == 0), stop=(k == KC - 1), r=['W0', f'HT{t}a', f'HT{t}b'], w=[PB[bi]])
            p.cp('act' if t % 2 == 0 else 'dve', VA[:, t, :, 0:64], bank(bi).rearrange("p (h d) -> p h d", h=8),
                 r=[PB[bi]], w=['VA'])
        EBGs = [EBG, AR.alloc([2, NP, 128], BF16)]
        for hg in range(4):
            Wn, wnk = (W1, 'W1') if hg % 2 == 0 else (W0, 'W0')
            EBG, ebk = EBGs[hg % 2], f'EBG{hg % 2}'
            wload(Wn, wnk, w_in_d[1], 512 + hg * 128, 128, 0)
            wload(Wn, wnk, w_in_d[1], 1024 + hg * 128, 128, 128)
            wload(Wn, wnk, w_in_d[1], 2048 + 512 + hg * 128, 128, 256)
            p.dma('pool', EBG, bm_d[:, hg, :].rearrange("p (h a q) -> p h a q", h=2, q=128), writes=[ebk], sem=ebk)
            p.act(EBG, EBG, AF.Exp, r=[ebk], w=[ebk])
            for bi_, (b0, bn) in enumerate(BLOCKS):
                hk = ht_keys(b0, bn)
                bi = 6 + bi_ % 2
                for k in range(KC):
                    p.mm(bank(bi)[:, 0:bn], Wn[:, k, 128:256], HT[:, k, b0:b0 + bn],
                         start=(k == 0), stop=(k == KC - 1), r=[wnk] + hk, w=[PB[bi]])
                p.cp('act', KT2[:, b0:b0 + bn], bank(bi)[:, 0:bn], r=[PB[bi]], w=['KT2'])
            for bi_, (b0, bn) in enumerate(BLOCKS[:4]):
                hk = ht_keys(b0, bn)
                bi = 6 + bi_ % 2
                for k in range(KC):
                    p.mm(bank(bi), Wn[:, k, 0:128], HT[:, k, b0:b0 + bn],
                         start=(k == 0), stop=(k == KC - 1), r=[wnk] + hk, w=[PB[bi]])
                src3 = bank(bi).rearrange("p (t n) -> p t n", t=4)
                p.act(QAB[0:64, b0 // 128:b0 // 128 + 4, 0, :], src3[0:64], AF.Copy, r=[PB[bi]], w=['QAB'], scale=0.125)
                p.act(QAB[64:128, b0 // 128:b0 // 128 + 4, 1, :], src3[64:128], AF.Copy, r=[PB[bi]], w=['QAB'], scale=0.125)

            def Jof(i):
                return na_tiles[i] + [(16, None), (17, None)]

            def qkg(i):
                for idx, (j, pat) in enumerate(Jof(i)):
                    bi = idx // 2
                    p.mm(bank(bi)[:, (idx % 2) * 256:(idx % 2 + 1) * 256], KT2[:, j * 128:(j + 1) * 128],
                         QAB[:, i, :, :], r=['KT2', 'QAB'], w=[PB[bi]])

            EBv = EBG.rearrange("p h a q -> p a h q")

            def postA(i):
                pob = 4 + i % 2
                po3 = bank(pob)[:, 0:130].rearrange("p (h d) -> p h d", h=2)
                p.recip(RINV, po3[:, :, 64], r=[PB[pob]], w=['RINV'])
                p.tt('dve', YB.rearrange("p (h d) -> p h d", h=2), po3[:, :, 0:64],
                     RINV.unsqueeze(2).broadcast_to([128, 2, 64]), ALU.mult, r=[PB[pob], 'RINV'], w=['YB'])
                p.tr(bankb(6)[:, 0:128], YB, identb[:], r=['YB', 'identb'], w=[PB[6]])

            def postC(i):
                p.cp('dve', YT[:, 4 + hg, i * 128:(i + 1) * 128], bankb(6)[:, 0:128], r=[PB[6]], w=[f'YT{4 + hg}'])

            qkg(0)
            for i in range(NTL):
                J = Jof(i)
                nj = len(J)
                nl = nj - 2
                pt = PT[i % 2]
                ptk = f'PTn{i % 2}'
                pob = 4 + i % 2
                n0 = min(nj, 4)
                p.act(pt[:, 0:n0], PSt[0][:, 0:n0 * 256].rearrange("p (j h q) -> p j h q", h=2, q=128), AF.Exp,
                      r=[PB[0], PB[1]], w=[ptk + 'a'])
                if nj > 4:
                    p.act(pt[:, 4:nj], PSt[1][:, 0:(nj - 4) * 256].rearrange("p (j h q) -> p j h q", h=2, q=128), AF.Exp,
                          r=[PB[2], PB[3]], w=[ptk + 'b'])
                if i + 1 < NTL:
                    qkg(i + 1)
                if i > 0:
                    postA(i - 1)
                s0 = J[0][1]
                p.tt('dve', pt[:, 0:nl], pt[:, 0:nl], EBv[:, s0:s0 + nl], ALU.mult,
                     r=[ebk, ptk + 'a', ptk + 'b'], w=[ptk + 'a', ptk + 'b'])
                if i > 0:
                    postC(i - 1)
                for h in range(2):
                    for idx, (j, pat) in enumerate(J):
                        p.mm(bank(pob)[:, h * 65:(h + 1) * 65], pt[:, idx, h, :], VA[:, j, 2 * hg + h, :],
                             start=(idx == 0), stop=(idx == nj - 1), r=[ptk + 'a', ptk + 'b', 'VA'], w=[PB[pob]])
            postA(NTL - 1)
            postC(NTL - 1)
            for kb in range(4):
                b0 = kb * 512
                hk = ht_keys(b0, 512)
                bi = 6 + kb % 2
                for k in range(KC):
                    p.mm(bank(bi), Wn[:, k, 256:384], HT[:, k, b0:b0 + 512],
                         start=(k == 0), stop=(k == KC - 1), r=[wnk] + hk, w=[PB[bi]])
                p.act(SZ, bank(bi), AF.Silu, r=[PB[bi]], w=['SZ'])
                p.tt('dve', YT[:, 4 + hg, b0:b0 + 512], YT[:, 4 + hg, b0:b0 + 512], SZ, ALU.mult,
                     r=[f'YT{4 + hg}', 'SZ'], w=[f'YT{4 + hg}'])

    phases = [lambda: adaln(0), lambda: None, phase_conv, phase_attn, lambda: phase_out(0),
              lambda: adaln(1), lambda: None, phase_fourier, phase_na, lambda: phase_out(1)]
    for ph in phases[:nphase]:
        ph()
    p.finalize()
    return nc, plist


def _consts():
    ident = np.eye(128, dtype=np.float32)
    tok = np.arange(S)
    row = (tok // 64).astype(np.float32)
    col = (tok % 64).astype(np.float32)
    nf = 16
    inv = (np.float32(10000.0) ** (-np.arange(nf, dtype=np.float32) / nf)).astype(np.float32)
    ang = np.concatenate([row[:, None] * inv, col[:, None] * inv], axis=-1).astype(np.float32)
    cos = np.cos(ang).astype(np.float32)
    sin = np.sin(ang).astype(np.float32)
    cos = np.concatenate([cos, np.ones((LC, 32), np.float32)], 0)
    sin = np.concatenate([sin, np.zeros((LC, 32), np.float32)], 0)
    tab = np.concatenate([cos * 0.125, sin * 0.125, cos, sin], axis=-1).astype(np.float32)
    rope = np.ascontiguousarray(tab.reshape(NT, 128, 128).transpose(1, 0, 2)).reshape(128, NT * 128)
    l = np.arange(S)[:, None]
    k = np.arange(S)[None, :]
    ph = ((l * k) % S).astype(np.float64) * (2 * np.pi / S)
    dft = np.empty((2, 4, 128, 16 * 512), dtype=ml_dtypes.bfloat16)
    for trig, m in enumerate((np.cos(ph), np.sin(ph))):
        m4 = m.reshape(16, 128, 4, 512).transpose(2, 1, 0, 3)
        dft[trig] = m4.reshape(4, 128, 16 * 512).astype(ml_dtypes.bfloat16)
    c = np.arange(128)[:, None]
    mm_ = np.arange(128)[None, :]
    phc = ((c * mm_) % 128).astype(np.float64) * (2 * np.pi / 128)
    ccsc = np.concatenate([np.cos(phc) / 512.0, -np.sin(phc) / 512.0], axis=1).astype(np.float32)
    nyq = np.stack([(-1.0) ** np.arange(128), (-1.0) ** np.arange(128)], axis=1).astype(np.float32)
    return ident, rope, dft, ccsc, nyq


def _bm_tables(rpb, plist):
    NP = len(plist)
    wq = np.arange(64)
    wk = np.arange(64)
    cs = np.clip(wq - 8, 0, 48)
    col_ok = (wk[None, :] >= cs[:, None]) & (wk[None, :] < cs[:, None] + 16)
    dc = np.clip(wk[None, :] - wq[:, None] + 15, 0, 30)
    bm = np.full((128, 8, NP, 128), NEG, dtype=np.float32)
    for pi, key in enumerate(plist):
        n = 0
        for qr in range(2):
            for kr in range(2):
                ok, dr = key[n]
                n += 1
                if not ok:
                    continue
                for h in range(8):
                    blk = rpb[h, dr][dc]
                    blk = np.where(col_ok, blk, np.float32(NEG))
                    bm[kr * 64:(kr + 1) * 64, h, pi, qr * 64:(qr + 1) * 64] = blk.T
    return np.ascontiguousarray(bm.reshape(128, 4, 2 * NP * 128))


_CACHE = {}


def kernel(x, c, ctx, c_ctx, ab_w_ada, ab_b_ada, ab_norm_g, ab_w_in, ab_conv_w, ab_conv_b,
           ab_ln_g, ab_ln_b, ab_q_norm_g, ab_k_norm_g, ab_w_out, cd_w_ada, cd_b_ada, cd_norm_g,
           cd_w_in, cd_w_fourier, cd_rpb, cd_w_out, final_norm_g):
    f = lambda a: np.ascontiguousarray(np.asarray(a, dtype=np.float32))
    if 'nc' not in _CACHE:
        _CACHE['nc'] = build()
        _CACHE['consts'] = _consts()
    nc, plist = _CACHE['nc']
    ident, rope, dft, ccsc, nyq = _CACHE['consts']
    bm = _bm_tables(f(cd_rpb)[0], plist)
    shared = {
        "ident": ident, "rope": rope, "dft": dft, "ccsc": ccsc, "bm": bm, "nyq": nyq,
        "gqk": np.stack([f(ab_q_norm_g)[0], f(ab_k_norm_g)[0]], 0),
        "conv_w": f(ab_conv_w)[0],
        "w_ada0": f(ab_w_ada)[0], "w_ada1": f(cd_w_ada)[0],
        "b_gate0": f(ab_b_ada)[0, 2048:3072], "b_gate1": f(cd_b_ada)[0, 2048:3072],
        "w_in0": f(ab_w_in)[0], "w_in1": f(cd_w_in)[0],
        "w_out0": f(ab_w_out)[0], "w_out1": f(cd_w_out)[0],
        "w_four": f(cd_w_fourier)[0],
        "final_g": f(final_norm_g),
    }
    in_maps = []
    for b in range(8):
        vecs = np.concatenate([
            f(c)[b].reshape(8, 128), f(c_ctx).reshape(8, 128),
            f(ab_b_ada)[0].reshape(24, 128), f(ab_norm_g)[0].reshape(8, 128),
            f(ab_conv_b)[0].reshape(4, 128), f(ab_ln_g)[0].reshape(4, 128), f(ab_ln_b)[0].reshape(4, 128),
            f(cd_b_ada)[0].reshape(24, 128), f(cd_norm_g)[0].reshape(8, 128)], axis=0)
        m = dict(shared)
        m["x"] = f(x)[b]
        m["ctx"] = f(ctx)[b]
        m["vecs"] = np.ascontiguousarray(vecs)
        in_maps.append(m)
    if _CACHE.get('debug_hook') is not None:
        return _CACHE['debug_hook'](nc, in_maps)
    res = run_bass_kernel_spmd(nc, in_maps, core_ids=list(range(8)))
    return np.stack([np.asarray(r["out"], dtype=np.float32) for r in res.results], axis=0)
```

```python
import numpy as np
from contextlib import ExitStack
import ml_dtypes
import concourse.bass as bass
import concourse.mybir as mybir
from concourse.bass_utils import run_bass_kernel_spmd

F32 = mybir.dt.float32
BF16 = mybir.dt.bfloat16
AF = mybir.ActivationFunctionType
ALU = mybir.AluOpType
AX = mybir.AxisListType

SAME_ENG_SYNC = True

D = 1024
S = 2048
LC = 256
T = S + LC
NT = 18
NTL = 16
KC = 8
EPS = 1e-6
AB_IN = 2816
CD_IN = 3072
NEG = -30000.0


class Prog:
    ENG = ['pe', 'act', 'dve', 'pool', 'sp']

    def __init__(self, nc):
        self.nc = nc
        self.ops = {e: [] for e in self.ENG}
        self.cnt = {}
        self.seen = {e: {} for e in self.ENG}
        self.res = {}
        self.stack = ExitStack()
        self.nsb = 0

    def sb(self, shape, dtype=F32, name=None):
        self.nsb += 1
        name = name or f"sb{self.nsb}"
        return self.stack.enter_context(self.nc.sbuf_tensor(name, list(shape), dtype))

    def ps(self, shape, dtype=F32, name=None):
        self.nsb += 1
        name = name or f"ps{self.nsb}"
        return self.stack.enter_context(self.nc.psum_tensor(name, list(shape), dtype))

    def _collect(self, eng, reads, writes):
        need = {}

        def add(t):
            if t is None:
                return
            k, v = t
            if need.get(k, 0) < v:
                need[k] = v
        for r in reads:
            st = self.res.get(r)
            if st is not None:
                add(st[0])
        for w in writes:
            st = self.res.get(w)
            if st is not None:
                add(st[0])
                for k, v in st[1].items():
                    add((k, v))
        waits = []
        seen = self.seen[eng]
        for k, v in need.items():
            if k == ('e', eng):
                if eng == 'pe' or not SAME_ENG_SYNC:
                    continue
            if k[0] == 'd':
                v = self.cnt[k]
            if seen.get(k, 0) >= v:
                continue
            seen[k] = v
            waits.append((k, v))
        return waits

    def _update(self, reads, writes, tick):
        k, v = tick
        for r in reads:
            st = self.res.setdefault(r, [None, {}])
            st[1][k] = v
        for w in writes:
            self.res[w] = [tick, {}]

    def capture(self, f):
        self._cap = []
        f()
        ops, self._cap = self._cap, None
        return ops

    def emit_rr(self, lists):
        idx = [0] * len(lists)
        left = sum(len(l) for l in lists)
        while left:
            for i, l in enumerate(lists):
                if idx[i] < len(l):
                    kind, args = l[idx[i]]
                    idx[i] += 1
                    left -= 1
                    if kind == 'op':
                        self.op(*args)
                    else:
                        self.dma(*args)

    def op(self, eng, fn, reads=(), writes=()):
        if getattr(self, '_cap', None) is not None:
            self._cap.append(('op', (eng, fn, reads, writes)))
            return
        writes = list(writes) + [r for r in reads if r.startswith('PB')]
        reads = [r for r in reads if not r.startswith('PB')]
        waits = self._collect(eng, reads, writes)
        k = ('e', eng)
        self.cnt[k] = self.cnt.get(k, 0) + 1
        self.ops[eng].append((waits, fn, (k, 1)))
        self._update(reads, writes, (k, self.cnt[k]))

    def dma(self, q, out, in_, reads=(), writes=(), sem=None):
        assert sem is not None
        if getattr(self, '_cap', None) is not None:
            self._cap.append(('dma', (q, out, in_, reads, writes, sem)))
            return
        waits = self._collect(q, reads, writes)
        k = ('d', sem)
        self.cnt[k] = self.cnt.get(k, 0) + 16
        self.ops[q].append((waits, lambda e: e.dma_start(out=out, in_=in_), (k, 16)))
        self._update(reads, writes, (k, self.cnt[k]))

    def barrier(self):
        for e in self.ENG:
            waits = []
            for k, v in self.cnt.items():
                if k == ('e', e):
                    continue
                if self.seen[e].get(k, 0) >= v:
                    continue
                self.seen[e][k] = v
                waits.append((k, v))
            if waits:
                self.ops[e].append((waits, None, None))

    def mm(self, out, lhsT, rhs, start=True, stop=True, r=(), w=()):
        self.op('pe', lambda e: e.matmul(out, lhsT, rhs, start=start, stop=stop), r, w)

    def tr(self, out, in_, ident, r=(), w=()):
        self.op('pe', lambda e: e.transpose(out, in_, ident), r, w)

    def act(self, out, in_, func, r=(), w=(), **kw):
        self.op('act', lambda e: e.activation(out, in_, func, **kw), r, w)

    def tt(self, eng, out, a, b, op, r=(), w=()):
        self.op(eng, lambda e: e.tensor_tensor(out, a, b, op=op), r, w)

    def ts(self, eng, out, a, s1, s2, op0, op1, r=(), w=()):
        if op1 is None:
            self.op(eng, lambda e: e.tensor_scalar(out, a, s1, None, op0=op0), r, w)
        else:
            self.op(eng, lambda e: e.tensor_scalar(out, a, s1, s2, op0=op0, op1=op1), r, w)

    def stt(self, out, a, s, b, op0, op1, r=(), w=()):
        self.op('dve', lambda e: e.scalar_tensor_tensor(out, a, s, b, op0=op0, op1=op1), r, w)

    def cp(self, eng, out, in_, r=(), w=()):
        if eng == 'act':
            self.op('act', lambda e: e.activation(out, in_, AF.Copy), r, w)
        else:
            self.op(eng, lambda e: e.tensor_copy(out, in_), r, w)

    def red(self, out, in_, r=(), w=()):
        self.op('dve', lambda e: e.tensor_reduce(out, in_, axis=AX.X, op=ALU.add), r, w)

    def recip(self, out, in_, r=(), w=()):
        self.op('dve', lambda e: e.reciprocal(out, in_), r, w)

    def ms(self, eng, ap, val, w=()):
        self.op(eng, lambda e: e.memset(ap, val), (), w)

    def finalize(self):
        nc = self.nc
        sems = {}
        for i, k in enumerate(self.cnt):
            sems[k] = self.stack.enter_context(nc.semaphore(f"s{i}"))
        final_waits = [(k, v) for k, v in self.cnt.items()]
        self.ops['sp'].append((final_waits, None, None))
        engmap = {'pe': 'tensor', 'act': 'scalar', 'dve': 'vector', 'pool': 'gpsimd', 'sp': 'sync'}
        with nc.Block() as block:
            for e in self.ENG:
                ops = self.ops[e]

                def body(engine, ops=ops):
                    for waits, fn, inc in ops:
                        for k, v in waits:
                            engine.wait_ge(sems[k], v)
                        if fn is not None:
                            ins = fn(engine)
                            ins.then_inc(sems[inc[0]], inc[1])
                getattr(block, engmap[e])(body)
        self.stack.close()


class Arena:
    def __init__(self, t, nbytes):
        self.t = t
        self.n = nbytes
        self.off = 0

    def reset(self):
        self.off = 0

    def alloc(self, free_shape, dtype, parts=128):
        n = int(np.prod(free_shape))
        sz = 4 if dtype == F32 else 2
        self.off = (self.off + 63) // 64 * 64
        nb = n * sz
        assert self.off + nb <= self.n, (self.off, nb, self.n)
        v = self.t[:, self.off // 2:(self.off + nb) // 2]
        self.off += nb
        if dtype == F32:
            v = v.bitcast(F32)
        if len(free_shape) == 2:
            v = v.rearrange("p (a b) -> p a b", a=free_shape[0])
        elif len(free_shape) == 3:
            v = v.rearrange("p (a b c) -> p a b c", a=free_shape[0], b=free_shape[1])
        if parts != 128:
            v = v[0:parts]
        return v


V_C, V_CC = 0, 8
V_L = [dict(bada=16, g=40), dict(bada=60, g=84)]
V_CB, V_LNG, V_LNB = 48, 52, 56
NV = 92


def na_patterns():
    rows = 32
    pats = {}
    plist = []
    tiles = []
    for i in range(16):
        js = []
        for j in range(16):
            key = []
            anyv = False
            for qr in range(2):
                r = 2 * i + qr
                rs = min(max(r - 4, 0), rows - 8)
                for kr in range(2):
                    R = 2 * j + kr
                    ok = rs <= R < rs + 8
                    anyv |= ok
                    key.append((ok, R - r + 7 if ok else -1))
            if not anyv:
                continue
            key = tuple(key)
            if key not in pats:
                pats[key] = len(plist)
                plist.append(key)
            js.append((j, pats[key]))
        tiles.append(js)
    order = [8, 7, 4, 0, 1, 2, 3, 5, 4, 0, 1, 6]
    tiles2 = []
    for js in tiles:
        seq = [pt for _, pt in js]
        s0 = None
        for st in range(len(order) - len(seq) + 1):
            if order[st:st + len(seq)] == seq:
                s0 = st
                break
        assert s0 is not None, seq
        tiles2.append([(j, s0 + n) for n, (j, _) in enumerate(js)])
    return [plist[o] for o in order], tiles2


def build(debug=False, nphase=10):
    nc = bass.Bass("TRN2", target_bir_lowering=False)

    def din(name, shape, dt=F32):
        return nc.dram_tensor(name, list(shape), dt, kind="ExternalInput").ap()

    plist, na_tiles = na_patterns()
    NP = len(plist)

    x = din("x", [S, D])
    ctx = din("ctx", [LC, D])
    vecs = din("vecs", [NV, 128])
    ident_d = din("ident", [128, 128])
    rope_d = din("rope", [128, NT * 128])
    gqk_d = din("gqk", [2, 64])
    conv_w_d = din("conv_w", [31, 512])
    w_ada_d = [din("w_ada0", [D, 3072]), din("w_ada1", [D, 3072])]
    b_gate_d = [din("b_gate0", [D]), din("b_gate1", [D])]
    w_in_d = [din("w_in0", [D, AB_IN]), din("w_in1", [D, CD_IN])]
    w_out_d = [din("w_out0", [D, D]), din("w_out1", [D, D])]
    w_four_d = din("w_four", [4, 128, 128])
    ccsc_d = din("ccsc", [128, 256])
    dft_d = din("dft", [2, 4, 128, 16 * 512], BF16)
    nyq_d = din("nyq", [128, 2])
    bm_d = din("bm", [128, 4, 2 * NP * 128])
    fg_d = din("final_g", [D])
    xs1 = nc.dram_tensor("xs1", [T, D], F32, kind="ExternalOutput" if debug else "Internal").ap()
    out_d = nc.dram_tensor("out", [S, D], F32, kind="ExternalOutput").ap()

    p = Prog(nc)
    HT = p.sb([128, KC, T], BF16, "HT")
    YT = p.sb([128, KC, T], BF16, "YT")
    W0 = p.sb([128, KC, 1024], BF16, "W0")
    W1 = p.sb([128, KC, 512], BF16, "W1")
    GATE = p.sb([128, 2, D], F32, "GATE")
    XT = [p.sb([128, D], F32, "XT0"), p.sb([128, D], F32, "XT1")]
    VT = p.sb([128, NV], F32, "VT")
    MOD = p.sb([128, 4, KC], F32, "MOD")
    ident32 = p.sb([128, 128], F32, "ident32")
    identb = p.sb([128, 128], BF16, "identb")
    ONESM = p.sb([128, 128], BF16, "ONESM")
    NHB = p.sb([128, 512], BF16, "NHB")
    CW = p.sb([128, 4, 31], F32, "CW")
    GQK = p.sb([128, 2, 64], F32, "GQK")
    SS = p.sb([128, NT], F32, "SS")
    MSQ = p.sb([128, NT], F32, "MSQ")
    RS = p.sb([128, NT], F32, "RS")
    SC32 = p.sb([128, 16], F32, "SC32")
    SCB = p.sb([128, KC, 2], BF16, "SCB")
    CCSC = p.sb([128, 256], BF16, "CCSC")
    WC = p.sb([128, 4, 128], BF16, "WC")
    AR_BYTES = 88 * 1024
    ARt = p.sb([128, AR_BYTES // 2], BF16, "ARENA")
    AR = Arena(ARt, AR_BYTES)
    PSt = [p.ps([128, 1024], F32, f"PS{i}") for i in range(4)]

    def bank(i):
        return PSt[i // 2][:, (i % 2) * 512:(i % 2 + 1) * 512]

    def bankb(i):
        return bank(i).bitcast(BF16)

    PB = [f"PB{i}" for i in range(8)]

    p.dma('sp', ident32[:], ident_d, writes=['ident32'], sem='ident32')
    p.cp('dve', identb[:], ident32[:], r=['ident32'], w=['identb'])
    p.ms('dve', ONESM[:], 1.0 / 512.0, w=['ONESM'])
    p.ms('pool', NHB[:], -0.5, w=['NHB'])
    p.dma('sp', GQK[:, 0, :], gqk_d[0].partition_broadcast(128), writes=['GQK0'], sem='GQK0')
    p.dma('sp', GQK[:, 1, :], gqk_d[1].partition_broadcast(128), writes=['GQK1'], sem='GQK1')
    p.dma('pool', CCSC[:], ccsc_d, writes=['CCSC'], sem='CCSC')
    p.dma('pool', WC[:], w_four_d.rearrange("g m e -> m g e"), writes=['WC'], sem='WC')
    AR.reset()
    VL = AR.alloc([128], F32)
    CWL = AR.alloc([512], F32)
    p.dma('sp', VL[0:NV, :], vecs, writes=['VL'], sem='VL')
    p.dma('sp', CWL[0:31, :], conv_w_d, writes=['CWL'], sem='CWL')
    p.tr(bank(0)[:, 0:NV], VL[0:NV, :], ident32[0:NV, 0:NV], r=['VL', 'ident32'], w=[PB[0]])
    p.cp('dve', VT[:], bank(0)[:, 0:NV], r=[PB[0]], w=['VT'])
    for cc in range(4):
        p.tr(bank(1)[:, cc * 32:cc * 32 + 31], CWL[0:31, cc * 128:(cc + 1) * 128], ident32[0:31, 0:31],
             r=['CWL', 'ident32'], w=[PB[1]])
    p.cp('dve', CW[:], bank(1)[:, 0:128].rearrange("p (c k) -> p c k", c=4)[:, :, 0:31], r=[PB[1]], w=['CW'])
    p.act(SC32[:], VT[:, 0:16], AF.Silu, r=['VT'], w=['SC32'])
    p.cp('dve', SCB[:, :, 0], SC32[:, 0:8], r=['SC32'], w=['SCB'])
    p.cp('dve', SCB[:, :, 1], SC32[:, 8:16], r=['SC32'], w=['SCB'])

    def wload(Wt, wkey, src, c0, n, d0=0):
        p.dma('pool', Wt[:, :, d0:d0 + n], src.rearrange("(k q) n -> q k n", q=128)[:, :, c0:c0 + n],
              writes=[wkey], sem=wkey)

    def adaln(l):
        p.barrier()
        AR.reset()
        BG = AR.alloc([D], F32)
        SCR = AR.alloc([16, 128], BF16)
        p.cp('dve', SCR, SC32[:, 0:16].unsqueeze(2).broadcast_to([128, 16, 128]), r=['SC32'], w=['SCR'])
        p.dma('sp', BG, b_gate_d[l].partition_broadcast(128), writes=['BG'], sem='BG')
        Ws = [(W0, 'W0'), (W1, 'W1')]
        vb = V_L[l]['bada']
        vg = V_L[l]['g']
        for blk in range(4):
            Wt, wk = Ws[blk % 2]
            wload(Wt, wk, w_ada_d[l], blk * 512, 512)
            for jj in range(4):
                j = blk * 4 + jj
                for k in range(KC):
                    p.mm(bank(7)[:, j * 2:j * 2 + 2], Wt[:, k, jj * 128:(jj + 1) * 128], SCB[:, k, :],
                         start=(k == 0), stop=(k == KC - 1), r=[wk, 'SCB'], w=[PB[7]])
        pm = bank(7)[:, 0:32].rearrange("p (j w) -> p j w", w=2)
        for wsel in range(2):
            p.tt('dve', MOD[:, 2 * wsel, :], pm[:, 0:8, wsel], VT[:, vb:vb + 8], ALU.add, r=[PB[7], 'VT'], w=['MOD'])
            p.tt('dve', MOD[:, 2 * wsel + 1, :], pm[:, 8:16, wsel], VT[:, vb + 8:vb + 16], ALU.add, r=[PB[7], 'VT'], w=['MOD'])
            p.stt(MOD[:, 2 * wsel + 1, :], MOD[:, 2 * wsel + 1, :], 1.0, VT[:, vg:vg + 8], ALU.add, ALU.mult,
                  r=['MOD', 'VT'], w=['MOD'])

        def gate_part():
            for blk in range(4, 6):
                Wt, wk = Ws[blk % 2]
                wload(Wt, wk, w_ada_d[l], blk * 512, 512)
                half = blk - 4
                for wsel in range(2):
                    bi = 5 + wsel
                    for k in range(KC):
                        p.mm(bank(bi), SCR[:, wsel * 8 + k, :], Wt[:, k, 0:512], start=(k == 0), stop=(k == KC - 1),
                             r=[wk, 'SCR'], w=[PB[bi]])
                    p.tt('dve', GATE[:, wsel, half * 512:(half + 1) * 512], bank(bi), BG[:, half * 512:(half + 1) * 512],
                         ALU.add, r=[PB[bi], 'BG'], w=['GATE'])

        p.emit_rr([p.capture(gate_part), p.capture(lambda: phase_norm(l, reset=False))])

    def xsrc(l, t):
        if l == 0:
            return x[t * 128:(t + 1) * 128, :] if t < NTL else ctx[(t - NTL) * 128:(t - NTL + 1) * 128, :]
        return xs1[t * 128:(t + 1) * 128, :]

    def phase_norm(l, reset=True):
        if reset:
            AR.reset()
        XH = [AR.alloc([D], F32), AR.alloc([D], F32)]
        JUNK = AR.alloc([D], BF16)
        def stA(t):
            xt = XT[t % 2]
            xk = f"XT{t % 2}"
            xh = XH[t % 2]
            hk = f"XH{t % 2}"
            p.dma('sp', xt[:], xsrc(l, t), reads=['xs1'] if l else [], writes=[xk], sem=xk)
            p.act(JUNK, xt[:], AF.Square, r=[xk], w=['JUNK', f'SS{t}'], accum_out=SS[:, t:t + 1])
            p.ts('dve', MSQ[:, t:t + 1], SS[:, t:t + 1], 1.0 / D, EPS, ALU.mult, ALU.add, r=[f'SS{t}'], w=[f'MSQ{t}'])
            p.tt('pool', RS[:, t:t + 1], MSQ[:, t:t + 1], NHB[:, 0:1], ALU.pow, r=[f'MSQ{t}', 'NHB'], w=[f'RS{t}'])
            p.ts('dve', xh, xt[:], RS[:, t:t + 1], None, ALU.mult, None, r=[xk, f'RS{t}'], w=[hk])

        def stB(t):
            xh = XH[t % 2]
            hk = f"XH{t % 2}"
            wsel = 0 if t < NTL else 1
            b0 = (t % 2) * 2
            for c in range(KC):
                bi = b0 + c // 4
                p.tr(bank(bi)[:, (c % 4) * 128:(c % 4 + 1) * 128], xh[:, c * 128:(c + 1) * 128], ident32[:],
                     r=[hk, 'ident32'], w=[PB[bi]])
            for c in range(KC):
                bi = b0 + c // 4
                src = bank(bi)[:, (c % 4) * 128:(c % 4 + 1) * 128]
                dst = HT[:, c, t * 128:(t + 1) * 128]
                if c < 4:
                    p.act(dst, src, AF.Identity, r=[PB[bi], 'MOD'], w=[f'HT{t}a'],
                          scale=MOD[:, 2 * wsel + 1, c:c + 1], bias=MOD[:, 2 * wsel, c:c + 1])
                else:
                    p.ts('dve', dst, src, MOD[:, 2 * wsel + 1, c:c + 1], MOD[:, 2 * wsel, c:c + 1], ALU.mult, ALU.add,
                         r=[PB[bi], 'MOD'], w=[f'HT{t}b'])

        stA(0)
        for t in range(NT):
            if t + 1 < NT:
                stA(t + 1)
            stB(t)

    BLOCKS = [(0, 512), (512, 512), (1024, 512), (1536, 512), (2048, 256)]

    def ht_keys(b0, bn):
        return [f'HT{t}{s_}' for t in range(b0 // 128, (b0 + bn) // 128) for s_ in 'ab']

    def phase_conv():
        p.barrier()
        AR.reset()
        NPE = 21
        GLUs = [AR.alloc([2349], F32), AR.alloc([2349], F32)]
        GLBs = [AR.alloc([2349], BF16), AR.alloc([2349], BF16)]
        ACCs = [AR.alloc([2319], F32), AR.alloc([2319], F32)]
        DGs = [AR.alloc([NPE, 128], BF16), AR.alloc([NPE, 128], BF16)]
        SIGs = [AR.alloc([512], F32), AR.alloc([512], F32)]
        AVs = [AR.alloc([512], F32), AR.alloc([512], F32)]
        wload(W0, 'W0', w_in_d[0], 0, 1024)
        wload(W1, 'W1', w_in_d[0], 1792, 512)
        for i in range(2):
            p.ms('dve', GLUs[i], 0.0, w=[f'GLU{i}'])
            p.ms('pool', GLBs[i], 0.0, w=[f'GLB{i}'])
        OB5 = [(0, 512), (512, 512), (1024, 512), (1536, 512), (2048, 271)]
        nblk = 0

        def castout(c_):
            p.cp('act', YT[:, c_, 0:S], ACCs[c_ % 2][:, 0:S], r=[f'ACC{c_ % 2}'], w=[f'YT{c_}'])
            p.cp('act', YT[:, c_, S:T], ACCs[c_ % 2][:, 2063:2319], r=[f'ACC{c_ % 2}'], w=[f'YT{c_}'])

        for cc in range(4):
            GLU, GLB, DG, ACC = GLUs[cc % 2], GLBs[cc % 2], DGs[cc % 2], ACCs[cc % 2]
            gk, bk, dk, ak = f'GLU{cc % 2}', f'GLB{cc % 2}', f'DG{cc % 2}', f'ACC{cc % 2}'
            p.tt('pool', DG, identb[:].unsqueeze(1).broadcast_to([128, NPE, 128]),
                 CW[:, cc, 0:NPE].unsqueeze(2).broadcast_to([128, NPE, 128]), ALU.mult, r=['identb', 'CW'], w=[dk])
            for bi_, (b0, bn) in enumerate(BLOCKS):
                pv, pg = 2 * (bi_ % 2), 2 * (bi_ % 2) + 1
                SIG, AV = SIGs[nblk % 2], AVs[nblk % 2]
                sgk, avk = f'SIG{nblk % 2}', f'AV{nblk % 2}'
                nblk += 1
                hk = ht_keys(b0, bn)
                for k in range(KC):
                    p.mm(bank(pv)[:, 0:bn], W0[:, k, cc * 128:(cc + 1) * 128], HT[:, k, b0:b0 + bn],
                         start=(k == 0), stop=(k == KC - 1), r=['W0'] + hk, w=[PB[pv]])
                for k in range(KC):
                    p.mm(bank(pg)[:, 0:bn], W0[:, k, 512 + cc * 128:512 + (cc + 1) * 128], HT[:, k, b0:b0 + bn],
                         start=(k == 0), stop=(k == KC - 1), r=['W0'] + hk, w=[PB[pg]])
                p.cp('act', AV[:, 0:bn], bank(pv)[:, 0:bn], r=[PB[pv]], w=[avk])
                p.act(SIG[:, 0:bn], bank(pg)[:, 0:bn], AF.Sigmoid, r=[PB[pg]], w=[sgk])
                pos = 15 + b0 if b0 < S else 2078
                p.tt('pool', GLU[:, pos:pos + bn], AV[:, 0:bn], SIG[:, 0:bn], ALU.mult, r=[avk, sgk], w=[gk])
                p.cp('act', GLB[:, pos:pos + bn], GLU[:, pos:pos + bn], r=[gk], w=[bk])
            for bi_, (p0, pn) in enumerate(OB5):
                cb = 4 + bi_ % 4
                for k in range(NPE):
                    p.mm(bank(cb)[:, 0:pn], DG[:, k, :], GLB[:, p0 + k:p0 + k + pn], start=(k == 0), stop=(k == NPE - 1),
                         r=[dk, bk], w=[PB[cb]])
                p.act(ACC[:, p0:p0 + pn], bank(cb)[:, 0:pn], AF.Identity, r=[PB[cb], 'VT'], w=[ak],
                      bias=VT[:, V_CB + cc:V_CB + cc + 1])
            if cc > 0:
                castout(cc - 1)
            for k in range(NPE, 31):
                p.stt(ACC, GLU[:, k:k + 2319], CW[:, cc, k:k + 1], ACC, ALU.mult, ALU.add, r=[gk, ak, 'CW'], w=[ak])
        castout(3)

    def phase_attn():
        p.barrier()
        AR.reset()
        KT2 = AR.alloc([2, T], BF16)
        VA3 = AR.alloc([NT, 2, 192], BF16)
        ROPE = AR.alloc([NT, 128], F32)
        QAB = AR.alloc([4, 2, 512], BF16)
        QN = AR.alloc([512], F32)
        TA = AR.alloc([8, 32], F32)
        TB = AR.alloc([8, 32], F32)
        QR = AR.alloc([512], BF16)
        SSQ = AR.alloc([16], F32)
        TMPS = [(QN, TA, TB, SSQ), (AR.alloc([512], F32), AR.alloc([8, 32], F32), AR.alloc([8, 32], F32), AR.alloc([16], F32))]
        QR1 = AR.alloc([512], BF16)
        KRs = [AR.alloc([4, 2, 128], BF16), AR.alloc([4, 2, 128], BF16)]
        arena_mark = AR.off
        p.dma('sp', ROPE, rope_d.rearrange("p (t c) -> p t c", t=NT), writes=['ROPE'], sem='ROPE')
        wload(W0, 'W0', w_in_d[0], 1024, 768)
        p.ms('dve', VA3, 1.0, w=['VA'])
        p.ms('pool', QAB, 0.0, w=['QAB0'])

        def normrope(t0, nt, hpt, src3, pbk, dst4, dkey, gsel, cq, sq_, ts_=0):
            nh = nt * hpt
            QN, TA, TB, SSQ = TMPS[ts_]
            kq, ka, kb_, ks = [f"{n_}{ts_}" for n_ in ("QN", "TA", "TB", "SSQ")]
            n = nh * 64
            p.act(QN[:, 0:n].rearrange("p (t c) -> p t c", t=nt), src3, AF.Square, r=[pbk], w=[kq])
            p.red(SSQ[:, 0:nh], QN[:, 0:n].rearrange("p (h d) -> p h d", h=nh), r=[kq], w=[ks])
            p.ts('dve', SSQ[:, 0:nh], SSQ[:, 0:nh], 1.0 / 64, EPS, ALU.mult, ALU.add, r=[ks], w=[ks])
            p.tt('pool', SSQ[:, 0:nh], SSQ[:, 0:nh], NHB[:, 0:nh], ALU.pow, r=[ks, 'NHB'], w=[ks])
            qn4 = QN[:, 0:n].rearrange("p (t h d) -> p t h d", t=nt, h=hpt)
            p.tt('dve', qn4, src3.rearrange("p t (h d) -> p t h d", h=hpt),
                 SSQ[:, 0:nh].rearrange("p (t h) -> p t h", t=nt).unsqueeze(3).broadcast_to([128, nt, hpt, 64]),
                 ALU.mult, r=[pbk, ks], w=[kq])
            qn3 = QN[:, 0:n].rearrange("p (h d) -> p h d", h=nh)
            p.tt('dve', qn3, qn3, GQK[:, gsel, :].unsqueeze(1).broadcast_to([128, nh, 64]), ALU.mult,
                 r=[kq, f'GQK{gsel}'], w=[kq])
            q5 = QN[:, 0:n].rearrange("p (t h d two) -> p t h d two", t=nt, h=hpt, two=2)
            A, B = q5[:, :, :, :, 0], q5[:, :, :, :, 1]
            cosb = ROPE[:, t0:t0 + nt, cq:cq + 32].unsqueeze(2).broadcast_to([128, nt, hpt, 32])
            sinb = ROPE[:, t0:t0 + nt, sq_:sq_ + 32].unsqueeze(2).broadcast_to([128, nt, hpt, 32])
            ta = TA[:, 0:nh, :].rearrange("p (t h) d -> p t h d", t=nt)
            tb = TB[:, 0:nh, :].rearrange("p (t h) d -> p t h d", t=nt)
            p.tt('dve', ta, A, cosb, ALU.mult, r=[kq, 'ROPE'], w=[ka])
            p.tt('dve', tb, B, sinb, ALU.mult, r=[kq, 'ROPE'], w=[kb_])
            p.tt('dve', dst4[0], ta, tb, ALU.subtract, r=[ka, kb_], w=[dkey])
            p.tt('dve', ta, A, sinb, ALU.mult, r=[kq, 'ROPE'], w=[ka])
            p.tt('dve', tb, B, cosb, ALU.mult, r=[kq, 'ROPE'], w=[kb_])
            p.tt('dve', dst4[1], ta, tb, ALU.add, r=[ka, kb_], w=[dkey])

        QABs = [QAB, None]
        QRs = [QR, QR1]

        def prepA1(t):
            for k in range(KC):
                p.mm(bank(6), HT[:, k, t * 128:(t + 1) * 128], W0[:, k, 0:512],
                     start=(k == 0), stop=(k == KC - 1), r=['W0', f'HT{t}a', f'HT{t}b'], w=[PB[6]])

        def prepA2(t):
            qd = QRs[t % 2].rearrange("p (t h d two) -> p t h d two", t=1, h=8, two=2)
            normrope(t, 1, 8, bank(6).rearrange("p (t c) -> p t c", t=1), PB[6], (qd[:, :, :, :, 0], qd[:, :, :, :, 1]),
                     f'QR{t % 2}', 0, 0, 32, ts_=1)

        def prepA(t):
            prepA1(t)
            prepA2(t)

        def prepB1(t):
            ptb = bankb(7)
            for pr in range(4):
                p.tr(ptb[:, pr * 128:(pr + 1) * 128], QRs[t % 2][:, pr * 128:(pr + 1) * 128], identb[:],
                     r=[f'QR{t % 2}', 'identb'], w=[PB[7]])

        def prepB2(t, qs, bsel):
            qab = QABs[bsel]
            src3 = bankb(7)[:, 0:512].rearrange("p (a n) -> p a n", a=4)
            p.cp('dve', qab[0:64, :, 0, qs * 128:(qs + 1) * 128], src3[0:64], r=[PB[7]], w=[f'QAB{bsel}'])
            p.cp('dve', qab[64:128, :, 1, qs * 128:(qs + 1) * 128], src3[64:128], r=[PB[7]], w=[f'QAB{bsel}'])

        def prepB(t, qs, bsel):
            prepB1(t)
            prepB2(t, qs, bsel)

        KB = [(0, 4), (4, 4), (8, 4), (12, 4), (16, 2)]

        def kvA(bi_):
            t0, nt = KB[bi_]
            pst = PSt[2]
            pk = [PB[4], PB[5]]
            kr = KRs[bi_ % 2]
            kk = f'KR{bi_ % 2}'
            for j in range(nt):
                t = t0 + j
                for k in range(KC):
                    p.mm(pst[:, j * 256:(j + 1) * 256], HT[:, k, t * 128:(t + 1) * 128], W0[:, k, 512:768],
                         start=(k == 0), stop=(k == KC - 1), r=['W0', f'HT{t}a', f'HT{t}b'], w=pk)
            v3 = pst[:, 0:nt * 256].rearrange("p (t c) -> p t c", c=256)
            kd = kr[:, 0:nt, :, 0:64].rearrange("p t h (d two) -> p t h d two", two=2)
            normrope(t0, nt, 2, v3[:, :, 0:128], pk[0], (kd[:, :, :, :, 0], kd[:, :, :, :, 1]), kk, 1, 64, 96)
            for kv in range(2):
                vsrc = v3[:, :, 128 + kv * 64:128 + (kv + 1) * 64]
                p.cp('act', VA3[:, t0:t0 + nt, kv, 0:64], vsrc, r=pk, w=['VA'])
                p.cp('act', VA3[:, t0:t0 + nt, kv, 128:192], vsrc, r=pk, w=['VA'])
            p.cp('act', kr[:, 0:nt, :, 64:128], kr[:, 0:nt, :, 0:64], r=[kk], w=[kk])

        def kvB(bi_):
            t0, nt = KB[bi_]
            kr = KRs[bi_ % 2]
            kk = f'KR{bi_ % 2}'
            tb_ = 7
            ptb = bankb(tb_)
            for h in range(2):
                for j in range(nt):
                    sl = h * nt + j
                    p.tr(ptb[:, sl * 128:(sl + 1) * 128], kr[:, j, h, :], identb[:], r=[kk, 'identb'], w=[PB[tb_]])
            p.cp('act', KT2[:, :, t0 * 128:(t0 + nt) * 128], ptb[:, 0:2 * nt * 128].rearrange("p (h n) -> p h n", h=2),
                 r=[PB[tb_]], w=['KT'])

        SZs = [AR.alloc([4, 512], BF16), AR.alloc([4, 512], BF16)]
        M2 = AR.alloc([512], F32)
        VEs = [AR.alloc([512], F32), AR.alloc([512], F32)]
        MNs = [AR.alloc([512], F32), AR.alloc([512], F32)]
        T1s = [AR.alloc([512], F32), AR.alloc([512], F32)]
        SQs = [AR.alloc([4, 512], BF16), AR.alloc([4, 512], BF16)]
        ytk = [f'YT{cc}' for cc in range(4)]

        def lnS(bi_):
            b0, bn = BLOCKS[bi_]
            SQ, VE, MN = SQs[bi_ % 2], VEs[bi_ % 2], MNs[bi_ % 2]
            sqk, vek, mnk = f'SQ{bi_ % 2}', f'VE{bi_ % 2}', f'MN{bi_ % 2}'
            for cc in range(4):
                p.act(SQ[:, cc, 0:bn], YT[:, cc, b0:b0 + bn], AF.Square, r=ytk, w=[sqk])
            for cc in range(4):
                p.mm(bank(0)[:, 0:bn], ONESM[:], YT[:, cc, b0:b0 + bn], start=(cc == 0), stop=(cc == 3),
                     r=ytk + ['ONESM'], w=[PB[0]])
            for cc in range(4):
                p.mm(bank(1)[:, 0:bn], ONESM[:], SQ[:, cc, 0:bn], start=(cc == 0), stop=(cc == 3), r=[sqk, 'ONESM'], w=[PB[1]])
            p.act(M2[:, 0:bn], bank(0)[:, 0:bn], AF.Square, r=[PB[0]], w=['M2'])
            p.cp('act', MN[:, 0:bn], bank(0)[:, 0:bn], r=[PB[0]], w=[mnk])
            p.stt(VE[:, 0:bn], bank(1)[:, 0:bn], EPS, M2[:, 0:bn], ALU.add, ALU.subtract, r=[PB[1], 'M2'], w=[vek])
            p.act(VE[:, 0:bn], VE[:, 0:bn], AF.Sqrt, r=[vek], w=[vek])
            p.recip(VE[:, 0:bn], VE[:, 0:bn], r=[vek], w=[vek])

        def lnZ(bi_):
            b0, bn = BLOCKS[bi_]
            hk = ht_keys(b0, bn)
            SZ = SZs[bi_ % 2]
            for cc in range(4):
                pz = 2 + (bi_ * 4 + cc) % 2
                for k in range(KC):
                    p.mm(bank(pz)[:, 0:bn], W1[:, k, cc * 128:(cc + 1) * 128], HT[:, k, b0:b0 + bn],
                         start=(k == 0), stop=(k == KC - 1), r=['W1'] + hk, w=[PB[pz]])
                p.act(SZ[:, cc, 0:bn], bank(pz)[:, 0:bn], AF.Silu, r=[PB[pz]], w=[f'SZ{bi_ % 2}_{cc}'])

        def lnP(bi_):
            b0, bn = BLOCKS[bi_]
            VE, MN, SZ = VEs[bi_ % 2], MNs[bi_ % 2], SZs[bi_ % 2]
            vek, mnk = f'VE{bi_ % 2}', f'MN{bi_ % 2}'
            for cc in range(4):
                T1 = T1s[cc % 2]
                tk = f'T1{cc % 2}'
                p.tt('dve', T1[:, 0:bn], YT[:, cc, b0:b0 + bn], MN[:, 0:bn], ALU.subtract, r=ytk + [mnk], w=[tk])
                p.tt('dve', T1[:, 0:bn], T1[:, 0:bn], VE[:, 0:bn], ALU.mult, r=[tk, vek], w=[tk])
                p.act(T1[:, 0:bn], T1[:, 0:bn], AF.Silu, r=[tk, 'VT'], w=[tk],
                      scale=VT[:, V_LNG + cc:V_LNG + cc + 1], bias=VT[:, V_LNB + cc:V_LNB + cc + 1])
                p.tt('dve', YT[:, cc, b0:b0 + bn], T1[:, 0:bn], SZ[:, cc, 0:bn], ALU.mult,
                     r=[tk, f'SZ{bi_ % 2}_{cc}'], w=[f'YTL{cc}_{bi_}'])

        lnS(0)
        lnZ(0)
        kvA(0)
        for bi_ in range(len(KB)):
            chains = []
            if bi_ >= 1:
                chains.append(p.capture(lambda: (kvB(bi_ - 1), prepB(bi_ - 2, bi_ - 2, 0) if bi_ >= 2 else None)))
            if bi_ + 1 < len(KB):
                chains.append(p.capture(lambda: (lnS(bi_ + 1), lnZ(bi_ + 1))))
                chains.append(p.capture(lambda: kvA(bi_ + 1)))
            if bi_ < 4:
                chains.append(p.capture(lambda: prepA(bi_)))
            chains.append(p.capture(lambda: lnP(bi_)))
            p.emit_rr(chains)
        kvB(4)
        prepB(2, 2, 0)
        prepB(3, 3, 0)
        p.barrier()
        AR.off = arena_mark
        PTs = [AR.alloc([2, 512], BF16) for _ in range(4)]
        SZ = AR.alloc([512], F32)
        RR = AR.alloc([512], F32)
        OA = AR.alloc([512], F32)
        OB = AR.alloc([512], F32)
        QABs[1] = AR.alloc([4, 2, 512], BF16)
        p.ms('pool', QABs[1], 0.0, w=['QAB1'])
        wload(W1, 'W1', w_in_d[0], 1792 + 512, 512)
        for bidx, (q0, qn) in enumerate(BLOCKS):
            nsub = qn // 128
            keytiles = list(range(NT)) if q0 < S else [16, 17]
            nk = len(keytiles)
            QAB = QABs[bidx % 2]
            qabk = f'QAB{bidx % 2}'
            sched = {}
            if bidx + 1 < len(BLOCKS):
                nq0, nqn = BLOCKS[bidx + 1]
                for qs in range(nqn // 128):
                    sched.setdefault(2 + 16 * qs, []).append(('A1', nq0 // 128 + qs, qs))
                    sched.setdefault(4 + 16 * qs, []).append(('A2', nq0 // 128 + qs, qs))
                    sched.setdefault(12 + 16 * qs, []).append(('B1', nq0 // 128 + qs, qs))
                    sched.setdefault(14 + 16 * qs, []).append(('B2', nq0 // 128 + qs, qs))
            steps = [(pr, idx, kt) for pr in range(4) for idx, kt in enumerate(keytiles)]

            def qk(sn):
                pr, idx, kt = steps[sn]
                st = PSt[sn % 2]
                sk = [PB[2 * (sn % 2)], PB[2 * (sn % 2) + 1]]
                for ab in range(2):
                    p.mm(st[:, ab * 512:ab * 512 + qn], KT2[:, pr // 2, kt * 128:(kt + 1) * 128], QAB[:, pr, ab, 0:qn],
                         r=['KT', qabk], w=sk)

            hk = ht_keys(q0, qn)
            qk(0)
            for sn, (pr, idx, kt) in enumerate(steps):
                kv = pr // 2
                if sn + 1 < len(steps):
                    qk(sn + 1)
                for (kind, pt_, pqs) in sched.get(sn, []):
                    if kind == 'A1':
                        prepA1(pt_)
                    elif kind == 'A2':
                        prepA2(pt_)
                    elif kind == 'B1':
                        prepB1(pt_)
                    else:
                        prepB2(pt_, pqs, (bidx + 1) % 2)
                st = PSt[sn % 2]
                sk = [PB[2 * (sn % 2)], PB[2 * (sn % 2) + 1]]
                pt = PTs[sn % 4]
                ptk = f'PT{sn % 4}'
                p.act(pt[:, :, 0:qn], st[:, :].rearrange("p (a n) -> p a n", a=2)[:, :, 0:qn], AF.Exp, r=sk, w=[ptk])
                p.mm(bank(4)[:, 0:qn], VA3[:, kt, kv, 0:128], pt[:, 0, 0:qn], start=(idx == 0), stop=(idx == nk - 1),
                     r=[ptk, 'VA'], w=[PB[4]])
                p.mm(bank(5)[:, 0:qn], VA3[:, kt, kv, 64:192], pt[:, 1, 0:qn], start=(idx == 0), stop=(idx == nk - 1),
                     r=[ptk, 'VA'], w=[PB[5]])
                if idx == nk - 1:
                    p.cp('act', OA[:, 0:qn], bank(4)[:, 0:qn], r=[PB[4]], w=['OA'])
                    p.cp('act', OB[:, 0:qn], bank(5)[:, 0:qn], r=[PB[5]], w=['OB'])
                    p.recip(RR[0:64, 0:qn], OA[64:128, 0:qn], r=['OA'], w=['RR'])
                    p.recip(RR[64:128, 0:qn], OB[0:64, 0:qn], r=['OB'], w=['RR'])
                    p.tt('dve', YT[0:64, 4 + pr, q0:q0 + qn], OA[0:64, 0:qn], RR[0:64, 0:qn], ALU.mult,
                         r=['OA', 'RR'], w=[f'YT{4 + pr}'])
                    p.tt('dve', YT[64:128, 4 + pr, q0:q0 + qn], OB[64:128, 0:qn], RR[64:128, 0:qn], ALU.mult,
                         r=['OB', 'RR'], w=[f'YT{4 + pr}'])

        SZ2 = [SZ, RR]
        n_ = 0
        for j in range(4):
            for (b0, bn) in BLOCKS:
                hk = ht_keys(b0, bn)
                bi = 6 + n_ % 2
                sz = SZ2[n_ % 2]
                szk = ['SZ', 'RR'][n_ % 2]
                n_ += 1
                for k in range(KC):
                    p.mm(bank(bi)[:, 0:bn], W1[:, k, j * 128:(j + 1) * 128], HT[:, k, b0:b0 + bn],
                         start=(k == 0), stop=(k == KC - 1), r=['W1'] + hk, w=[PB[bi]])
                p.act(sz[:, 0:bn], bank(bi)[:, 0:bn], AF.Silu, r=[PB[bi]], w=[szk])
                p.tt('dve', YT[:, 4 + j, b0:b0 + bn], YT[:, 4 + j, b0:b0 + bn], sz[:, 0:bn], ALU.mult,
                     r=[f'YT{4 + j}', szk], w=[f'YT{4 + j}'])

    def phase_out(l):
        p.barrier()
        AR.reset()
        XN = [AR.alloc([D], F32) for _ in range(4)]
        TMP = AR.alloc([D], F32)
        last = (l == 1)
        if last:
            FG = AR.alloc([D], F32)
            JUNK = AR.alloc([D], BF16)
            p.dma('sp', FG, fg_d.partition_broadcast(128), writes=['FG'], sem='FG')
        wload(W0, 'W0', w_out_d[l], 0, 1024)
        ntile = NTL if last else NT
        ytk = [f'YT{c}' for c in range(KC)]
        def tail(t):
            xn = XN[t % 4]
            nk = f"XN{t % 4}"
            p.ts('dve', MSQ[:, t:t + 1], SS[:, t:t + 1], 1.0 / D, EPS, ALU.mult, ALU.add, r=[f'SS{t}'], w=[f'MSQ{t}'])
            p.tt('pool', RS[:, t:t + 1], MSQ[:, t:t + 1], NHB[:, 0:1], ALU.pow, r=[f'MSQ{t}', 'NHB'], w=[f'RS{t}'])
            p.stt(xn, xn, RS[:, t:t + 1], FG, ALU.mult, ALU.mult, r=[nk, f'RS{t}', 'FG'], w=[nk])
            p.dma('sp', out_d[t * 128:(t + 1) * 128, :], xn, reads=[nk], writes=['out'], sem=f'st{t % 4}')

        XT3 = [XT[0], XT[1], AR.alloc([D], F32)]

        def xload(t):
            p.dma('sp', XT3[t % 3][:] if t % 3 < 2 else XT3[2], xsrc(l, t), reads=['xs1'] if l else [],
                  writes=[f"XT{t % 3}"], sem=f"XT{t % 3}")

        for t in range(ntile):
            xt = XT3[t % 3]
            xk = f"XT{t % 3}"
            xn = XN[t % 4]
            nk = f"XN{t % 4}"
            wsel = 0 if t < NTL else 1
            if t == 0:
                xload(0)
            if t + 1 < ntile:
                xload(t + 1)
            for half in range(2):
                bi = 2 * (t % 2) + half
                for k in range(KC):
                    p.mm(bank(bi), YT[:, k, t * 128:(t + 1) * 128], W0[:, k, half * 512:(half + 1) * 512],
                         start=(k == 0), stop=(k == KC - 1), r=['W0'] + ytk, w=[PB[bi]])
                sl = slice(half * 512, (half + 1) * 512)
                p.tt('dve', TMP[:, sl], bank(bi), GATE[:, wsel, sl], ALU.mult, r=[PB[bi], 'GATE'], w=['TMP'])
                p.tt('dve', xn[:, sl], TMP[:, sl], xt[:, sl], ALU.add, r=['TMP', xk], w=[nk])
            if not last:
                p.dma('sp', xs1[t * 128:(t + 1) * 128, :], xn, reads=[nk], writes=['xs1'], sem=f'st{t % 4}')
            else:
                p.act(JUNK, xn, AF.Square, r=[nk], w=['JUNK', f'SS{t}'], accum_out=SS[:, t:t + 1])
                if t > 1:
                    tail(t - 2)
        if last:
            tail(ntile - 2)
            tail(ntile - 1)

    def phase_fourier():
        p.barrier()
        AR.reset()
        Fm = AR.alloc([NTL, 512], BF16)
        DB = [AR.alloc([8, 512], BF16), AR.alloc([8, 512], BF16)]
        PQ = AR.alloc([2, 4, 512], BF16)
        UB = AR.alloc([512], F32)
        FRa = AR.alloc([512], BF16)
        FRb = AR.alloc([512], BF16)
        SZa = AR.alloc([512], F32)
        SZb = AR.alloc([512], F32)
        NYQ = AR.alloc([2], BF16)
        PN = AR.alloc([4, 2], BF16)
        FRN = AR.alloc([2], BF16)
        SZN = AR.alloc([2], F32)
        wload(W0, 'W0', w_in_d[1], 0, 512)
        wload(W1, 'W1', w_in_d[1], 2048, 512)
        p.dma('pool', NYQ, nyq_d, writes=['NYQ'], sem='NYQ')
        for t in range(NTL):
            bi = t % 2
            for k in range(KC):
                p.mm(bank(bi), HT[:, k, t * 128:(t + 1) * 128], W0[:, k, 0:512],
                     start=(k == 0), stop=(k == KC - 1), r=['W0', f'HT{t}a', f'HT{t}b'], w=[PB[bi]])
            p.cp('act', Fm[:, t, :], bank(bi), r=[PB[bi]], w=['Fm'])

        def zproj(g, t0, n, bi, sz, szk):
            hk = [f'HT{t}{s_}' for t in range(t0 // 128, (t0 + n - 1) // 128 + 1) for s_ in 'ab']
            for k in range(KC):
                p.mm(bank(bi)[:, 0:n], W1[:, k, g * 128:(g + 1) * 128], HT[:, k, t0:t0 + n],
                     start=(k == 0), stop=(k == KC - 1), r=['W1'] + hk, w=[PB[bi]])
            p.act(sz[:, 0:n], bank(bi)[:, 0:n], AF.Silu, r=[PB[bi]], w=[szk])

        nload = 0
        for kb in range(2):
            for trig in range(2):
                for half in range(2):
                    db = DB[nload % 2]
                    dk = f'DB{nload % 2}'
                    nload += 1
                    p.dma('sp', db, dft_d[trig, kb][:, half * 4096:(half + 1) * 4096].rearrange("p (l n) -> p l n", l=8),
                          writes=[dk], sem=dk)
                    for l8 in range(8):
                        lc = half * 8 + l8
                        for g in range(4):
                            p.mm(bank(g), Fm[:, lc, g * 128:(g + 1) * 128], db[:, l8, :],
                                 start=(lc == 0), stop=(lc == 15), r=['Fm', dk], w=[PB[g]])
                for g in range(4):
                    p.cp('act' if g % 2 == 0 else 'dve', PQ[:, trig, g, :], bank(g), r=[PB[g]], w=[f'PQ{g}'])
            b0 = kb * 512
            if kb == 0:
                mt0, mn = 1537, 511
            else:
                mt0, mn = 1025, 512
            for g in range(4):
                p.mm(bank(4), CCSC[:, 0:128], PQ[:, 0, g, :], r=['CCSC', f'PQ{g}'], w=[PB[4]])
                p.mm(bank(5), CCSC[:, 128:256], PQ[:, 1, g, :], r=['CCSC', f'PQ{g}'], w=[PB[5]])
                p.cp('act', UB, bank(4), r=[PB[4]], w=['UB'])
                p.tt('dve', FRa, UB, bank(5), ALU.add, r=['UB', PB[5]], w=['FRa'])
                p.tt('dve', FRb, UB, bank(5), ALU.subtract, r=['UB', PB[5]], w=['FRb'])
                p.mm(bank(6), WC[:, g, :], FRa, r=['WC', 'FRa'], w=[PB[6]])
                p.mm(bank(7), WC[:, g, :], FRb, r=['WC', 'FRb'], w=[PB[7]])
                zproj(g, b0, 512, 4, SZa, 'SZa')
                zproj(g, mt0, mn, 5, SZb, 'SZb')
                p.tt('dve', YT[:, g, b0:b0 + 512], bank(6), SZa, ALU.mult, r=[PB[6], 'SZa'], w=[f'YT{g}'])
                if kb == 0:
                    p.tt('dve', YT[:, g, 2047:1536:-1], bank(7)[:, 1:512], SZb[:, 510::-1], ALU.mult,
                         r=[PB[7], 'SZb'], w=[f'YT{g}'])
                else:
                    p.tt('dve', YT[:, g, 1536:1024:-1], bank(7)[:, 0:512], SZb[:, 511::-1], ALU.mult,
                         r=[PB[7], 'SZb'], w=[f'YT{g}'])
        for g in range(4):
            for lc in range(NTL):
                p.mm(bank(0)[:, g * 2:g * 2 + 2], Fm[:, lc, g * 128:(g + 1) * 128], NYQ, start=(lc == 0), stop=(lc == NTL - 1),
                     r=['Fm', 'NYQ'], w=[PB[0]])
        p.cp('act', PN, bank(0)[:, 0:8].rearrange("p (g n) -> p g n", g=4), r=[PB[0]], w=['PN'])
        for g in range(4):
            p.mm(bank(1)[:, 0:2], CCSC[:, 0:128], PN[:, g, :], r=['CCSC', 'PN'], w=[PB[1]])
            p.cp('act', FRN, bank(1)[:, 0:2], r=[PB[1]], w=['FRN'])
            p.mm(bank(2)[:, 0:2], WC[:, g, :], FRN, r=['WC', 'FRN'], w=[PB[2]])
            zproj(g, 1024, 2, 3, SZN, 'SZN')
            p.tt('dve', YT[:, g, 1024:1025], bank(2)[:, 0:1], SZN[:, 0:1], ALU.mult, r=[PB[2], 'SZN'], w=[f'YT{g}'])

    def phase_na():
        p.barrier()
        AR.reset()
        VA = AR.alloc([NT, 8, 65], BF16)
        KT2 = AR.alloc([T], BF16)
        QAB = AR.alloc([NTL, 2, 128], BF16)
        EBG = AR.alloc([2, NP, 128], BF16)
        PT = [AR.alloc([7, 2, 128], BF16), AR.alloc([7, 2, 128], BF16)]
        YB = AR.alloc([128], BF16)
        RINV = AR.alloc([2], F32)
        SZ = AR.alloc([512], F32)
        wload(W0, 'W0', w_in_d[1], 1536, 512, 0)
        p.ms('dve', VA, 1.0, w=['VA'])
        p.ms('pool', QAB, 0.0, w=['QAB'])
        for t in range(NT):
            bi = 6 + t % 2
            for k in range(KC):
                p.mm(bank(bi), HT[:, k, t * 128:(t + 1) * 128], W0[:, k, 0:512],
                     start=(k == 0), stop=(k == KC - 1), r=['W0', f'HT{t}a', f'HT{t}b'], w=[PB[bi]])
            p.cp('act' if t % 2 == 0 else 'dve', VA[:, t, :, 0:64], bank(bi).rearrange("p (h d) -> p h d", h=8),
                 r=[PB[bi]], w=['VA'])
        EBGs = [EBG, AR.alloc([2, NP, 128], BF16)]
        for hg in range(4):
            Wn, wnk = (W1, 'W1') if hg % 2 == 0 else (W0, 'W0')
            EBG, ebk = EBGs[hg % 2], f'EBG{hg % 2}'
            wload(Wn, wnk, w_in_d[1], 512 + hg * 128, 128, 0)
            wload(Wn, wnk, w_in_d[1], 1024 + hg * 128, 128, 128)
            wload(Wn, wnk, w_in_d[1], 2048 + 512 + hg * 128, 128, 256)
            p.dma('pool', EBG, bm_d[:, hg, :].rearrange("p (h a q) -> p h a q", h=2, q=128), writes=[ebk], sem=ebk)
            p.act(EBG, EBG, AF.Exp, r=[ebk], w=[ebk])
            for bi_, (b0, bn) in enumerate(BLOCKS):
                hk = ht_keys(b0, bn)
                bi = 6 + bi_ % 2
                for k in range(KC):
                    p.mm(bank(bi)[:, 0:bn], Wn[:, k, 128:256], HT[:, k, b0:b0 + bn],
                         start=(k == 0), stop=(k == KC - 1), r=[wnk] + hk, w=[PB[bi]])
                p.cp('act', KT2[:, b0:b0 + bn], bank(bi)[:, 0:bn], r=[PB[bi]], w=['KT2'])
            for bi_, (b0, bn) in enumerate(BLOCKS[:4]):
                hk = ht_keys(b0, bn)
                bi = 6 + bi_ % 2
                for k in range(KC):
                    p.mm(bank(bi), Wn[:, k, 0:128], HT[:, k, b0:b0 + bn],
                         start=(k == 0), stop=(k == KC - 1), r=[wnk] + hk, w=[PB[bi]])
                src3 = bank(bi).rearrange("p (t n) -> p t n", t=4)
                p.act(QAB[0:64, b0 // 128:b0 // 128 + 4, 0, :], src3[0:64], AF.Copy, r=[PB[bi]], w=['QAB'], scale=0.125)
                p.act(QAB[64:128, b0 // 128:b0 // 128 + 4, 1, :], src3[64:128], AF.Copy, r=[PB[bi]], w=['QAB'], scale=0.125)

            def Jof(i):
                return na_tiles[i] + [(16, None), (17, None)]

            def qkg(i):
                for idx, (j, pat) in enumerate(Jof(i)):
                    bi = idx // 2
                    p.mm(bank(bi)[:, (idx % 2) * 256:(idx % 2 + 1) * 256], KT2[:, j * 128:(j + 1) * 128],
                         QAB[:, i, :, :], r=['KT2', 'QAB'], w=[PB[bi]])

            EBv = EBG.rearrange("p h a q -> p a h q")

            def postA(i):
                pob = 4 + i % 2
                po3 = bank(pob)[:, 0:130].rearrange("p (h d) -> p h d", h=2)
                p.recip(RINV, po3[:, :, 64], r=[PB[pob]], w=['RINV'])
                p.tt('dve', YB.rearrange("p (h d) -> p h d", h=2), po3[:, :, 0:64],
                     RINV.unsqueeze(2).broadcast_to([128, 2, 64]), ALU.mult, r=[PB[pob], 'RINV'], w=['YB'])
                p.tr(bankb(6)[:, 0:128], YB, identb[:], r=['YB', 'identb'], w=[PB[6]])

            def postC(i):
                p.cp('dve', YT[:, 4 + hg, i * 128:(i + 1) * 128], bankb(6)[:, 0:128], r=[PB[6]], w=[f'YT{4 + hg}'])

            qkg(0)
            for i in range(NTL):
                J = Jof(i)
                nj = len(J)
                nl = nj - 2
                pt = PT[i % 2]
                ptk = f'PTn{i % 2}'
                pob = 4 + i % 2
                n0 = min(nj, 4)
                p.act(pt[:, 0:n0], PSt[0][:, 0:n0 * 256].rearrange("p (j h q) -> p j h q", h=2, q=128), AF.Exp,
                      r=[PB[0], PB[1]], w=[ptk + 'a'])
                if nj > 4:
                    p.act(pt[:, 4:nj], PSt[1][:, 0:(nj - 4) * 256].rearrange("p (j h q) -> p j h q", h=2, q=128), AF.Exp,
                          r=[PB[2], PB[3]], w=[ptk + 'b'])
                if i + 1 < NTL:
                    qkg(i + 1)
                if i > 0:
                    postA(i - 1)
                s0 = J[0][1]
                p.tt('dve', pt[:, 0:nl], pt[:, 0:nl], EBv[:, s0:s0 + nl], ALU.mult,
                     r=[ebk, ptk + 'a', ptk + 'b'], w=[ptk + 'a', ptk + 'b'])
                if i > 0:
                    postC(i - 1)
                for h in range(2):
                    for idx, (j, pat) in enumerate(J):
                        p.mm(bank(pob)[:, h * 65:(h + 1) * 65], pt[:, idx, h, :], VA[:, j, 2 * hg + h, :],
                             start=(idx == 0), stop=(idx == nj - 1), r=[ptk + 'a', ptk + 'b', 'VA'], w=[PB[pob]])
            postA(NTL - 1)
            postC(NTL - 1)
            for kb in range(4):
                b0 = kb * 512
                hk = ht_keys(b0, 512)
                bi = 6 + kb % 2
                for k in range(KC):
                    p.mm(bank(bi), Wn[:, k, 256:384], HT[:, k, b0:b0 + 512],
                         start=(k == 0), stop=(k == KC - 1), r=[wnk] + hk, w=[PB[bi]])
                p.act(SZ, bank(bi), AF.Silu, r=[PB[bi]], w=['SZ'])
                p.tt('dve', YT[:, 4 + hg, b0:b0 + 512], YT[:, 4 + hg, b0:b0 + 512], SZ, ALU.mult,
                     r=[f'YT{4 + hg}', 'SZ'], w=[f'YT{4 + hg}'])

    phases = [lambda: adaln(0), lambda: None, phase_conv, phase_attn, lambda: phase_out(0),
              lambda: adaln(1), lambda: None, phase_fourier, phase_na, lambda: phase_out(1)]
    for ph in phases[:nphase]:
        ph()
    p.finalize()
    return nc, plist


def _consts():
    ident = np.eye(128, dtype=np.float32)
    tok = np.arange(S)
    row = (tok // 64).astype(np.float32)
    col = (tok % 64).astype(np.float32)
    nf = 16
    inv = (np.float32(10000.0) ** (-np.arange(nf, dtype=np.float32) / nf)).astype(np.float32)
    ang = np.concatenate([row[:, None] * inv, col[:, None] * inv], axis=-1).astype(np.float32)
    cos = np.cos(ang).astype(np.float32)
    sin = np.sin(ang).astype(np.float32)
    cos = np.concatenate([cos, np.ones((LC, 32), np.float32)], 0)
    sin = np.concatenate([sin, np.zeros((LC, 32), np.float32)], 0)
    tab = np.concatenate([cos * 0.125, sin * 0.125, cos, sin], axis=-1).astype(np.float32)
    rope = np.ascontiguousarray(tab.reshape(NT, 128, 128).transpose(1, 0, 2)).reshape(128, NT * 128)
    l = np.arange(S)[:, None]
    k = np.arange(S)[None, :]
    ph = ((l * k) % S).astype(np.float64) * (2 * np.pi / S)
    dft = np.empty((2, 4, 128, 16 * 512), dtype=ml_dtypes.bfloat16)
    for trig, m in enumerate((np.cos(ph), np.sin(ph))):
        m4 = m.reshape(16, 128, 4, 512).transpose(2, 1, 0, 3)
        dft[trig] = m4.reshape(4, 128, 16 * 512).astype(ml_dtypes.bfloat16)
    c = np.arange(128)[:, None]
    mm_ = np.arange(128)[None, :]
    phc = ((c * mm_) % 128).astype(np.float64) * (2 * np.pi / 128)
    ccsc = np.concatenate([np.cos(phc) / 512.0, -np.sin(phc) / 512.0], axis=1).astype(np.float32)
    nyq = np.stack([(-1.0) ** np.arange(128), (-1.0) ** np.arange(128)], axis=1).astype(np.float32)
    return ident, rope, dft, ccsc, nyq


def _bm_tables(rpb, plist):
    NP = len(plist)
    wq = np.arange(64)
    wk = np.arange(64)
    cs = np.clip(wq - 8, 0, 48)
    col_ok = (wk[None, :] >= cs[:, None]) & (wk[None, :] < cs[:, None] + 16)
    dc = np.clip(wk[None, :] - wq[:, None] + 15, 0, 30)
    bm = np.full((128, 8, NP, 128), NEG, dtype=np.float32)
    for pi, key in enumerate(plist):
        n = 0
        for qr in range(2):
            for kr in range(2):
                ok, dr = key[n]
                n += 1
                if not ok:
                    continue
                for h in range(8):
                    blk = rpb[h, dr][dc]
                    blk = np.where(col_ok, blk, np.float32(NEG))
                    bm[kr * 64:(kr + 1) * 64, h, pi, qr * 64:(qr + 1) * 64] = blk.T
    return np.ascontiguousarray(bm.reshape(128, 4, 2 * NP * 128))


_CACHE = {}


def kernel(x, c, ctx, c_ctx, ab_w_ada, ab_b_ada, ab_norm_g, ab_w_in, ab_conv_w, ab_conv_b,
           ab_ln_g, ab_ln_b, ab_q_norm_g, ab_k_norm_g, ab_w_out, cd_w_ada, cd_b_ada, cd_norm_g,
           cd_w_in, cd_w_fourier, cd_rpb, cd_w_out, final_norm_g):
    f = lambda a: np.ascontiguousarray(np.asarray(a, dtype=np.float32))
    if 'nc' not in _CACHE:
        _CACHE['nc'] = build()
        _CACHE['consts'] = _consts()
    nc, plist = _CACHE['nc']
    ident, rope, dft, ccsc, nyq = _CACHE['consts']
    bm = _bm_tables(f(cd_rpb)[0], plist)
    shared = {
        "ident": ident, "rope": rope, "dft": dft, "ccsc": ccsc, "bm": bm, "nyq": nyq,
        "gqk": np.stack([f(ab_q_norm_g)[0], f(ab_k_norm_g)[0]], 0),
        "conv_w": f(ab_conv_w)[0],
        "w_ada0": f(ab_w_ada)[0], "w_ada1": f(cd_w_ada)[0],
        "b_gate0": f(ab_b_ada)[0, 2048:3072], "b_gate1": f(cd_b_ada)[0, 2048:3072],
        "w_in0": f(ab_w_in)[0], "w_in1": f(cd_w_in)[0],
        "w_out0": f(ab_w_out)[0], "w_out1": f(cd_w_out)[0],
        "w_four": f(cd_w_fourier)[0],
        "final_g": f(final_norm_g),
    }
    in_maps = []
    for b in range(8):
        vecs = np.concatenate([
            f(c)[b].reshape(8, 128), f(c_ctx).reshape(8, 128),
            f(ab_b_ada)[0].reshape(24, 128), f(ab_norm_g)[0].reshape(8, 128),
            f(ab_conv_b)[0].reshape(4, 128), f(ab_ln_g)[0].reshape(4, 128), f(ab_ln_b)[0].reshape(4, 128),
            f(cd_b_ada)[0].reshape(24, 128), f(cd_norm_g)[0].reshape(8, 128)], axis=0)
        m = dict(shared)
        m["x"] = f(x)[b]
        m["ctx"] = f(ctx)[b]
        m["vecs"] = np.ascontiguousarray(vecs)
        in_maps.append(m)
    if _CACHE.get('debug_hook') is not None:
        return _CACHE['debug_hook'](nc, in_maps)
    res = run_bass_kernel_spmd(nc, in_maps, core_ids=list(range(8)))
    return np.stack([np.asarray(r["out"], dtype=np.float32) for r in res.results], axis=0)
```
